# Optimizing a Trainium2 kernel written in Bass

```python
import math
import jax, jax.numpy as jnp
from jax import lax
import numpy as np

D_MODEL = 1024
BATCH = 4
SEQ = 4096
DEPTH = 1

PLE_DIM = 256
DN_HEADS = 8
DN_DK = 128
DN_DV = 128
DN_CONV = 4
DN_CHUNK = 64
DN_QK_W = DN_HEADS * DN_DK
DN_V_W = DN_HEADS * DN_DV
DN_CONV_CH = 2 * DN_QK_W + DN_V_W
MLA_HEADS = 8
MLA_Q_LORA = 384
MLA_KV_LORA = 256
MLA_NOPE = 128
MLA_ROPE = 64
MLA_V = 128
MLA_V_W = MLA_HEADS * MLA_V
ROPE_BASE = 10000.0
Q_BLOCK = 128
FFN_HIDDEN = -(-8 * D_MODEL // (3 * 256)) * 256
DEEPNORM_ALPHA = (2.0 * DEPTH) ** 0.25
DEEPNORM_BETA = (8.0 * DEPTH) ** -0.25
IN_SIZES = (DN_CONV_CH, DN_V_W, DN_HEADS, DN_HEADS, MLA_Q_LORA, MLA_KV_LORA, MLA_ROPE, D_MODEL, D_MODEL)
D_IN = sum(IN_SIZES)
SPLIT_IDX = tuple(int(v) for v in np.cumsum(IN_SIZES)[:-1])
NEG_BIG = -1e30

kernel_name = 'hybrid_deltanet_mla_deepnorm_block'


def layer_norm(t, g, b, eps=1e-5):
    tf = t.astype(jnp.float32)
    mu = jnp.mean(tf, axis=-1, keepdims=True)
    var = jnp.mean(jnp.square(tf - mu), axis=-1, keepdims=True)
    return ((tf - mu) * lax.rsqrt(var + eps) * g.astype(jnp.float32) + b.astype(jnp.float32)).astype(t.dtype)


def rms_norm(t, w, eps=1e-6):
    tf = t.astype(jnp.float32)
    return tf * lax.rsqrt(jnp.mean(jnp.square(tf), axis=-1, keepdims=True) + eps) * w.astype(jnp.float32)


def l2_normalize(t, eps=1e-6):
    tf = t.astype(jnp.float32)
    return tf * lax.rsqrt(jnp.sum(jnp.square(tf), axis=-1, keepdims=True) + eps)


def rope_tables(positions):
    inv_freq = ROPE_BASE ** (-jnp.arange(0, MLA_ROPE, 2, dtype=jnp.float32) / MLA_ROPE)
    ang = positions.astype(jnp.float32)[..., None] * inv_freq
    return jnp.cos(ang), jnp.sin(ang)


def apply_rope(t, cos, sin):
    t1, t2 = jnp.split(t.astype(jnp.float32), 2, axis=-1)
    return jnp.concatenate([t1 * cos - t2 * sin, t2 * cos + t1 * sin], axis=-1)


def causal_depthwise_conv(t, w):
    width, ch = w.shape
    return lax.conv_general_dilated(t, w[:, None, :].astype(t.dtype), window_strides=(1,), padding=[(width - 1, 0)], dimension_numbers=('NWC', 'WIO', 'NWC'), feature_group_count=ch)


def gated_delta_rule(q, k, v, beta, g):
    b, s, h, dk = q.shape
    dv = v.shape[-1]
    c = DN_CHUNK
    n = s // c

    def chunk(t):
        return jnp.swapaxes(t.reshape((b, n, c) + t.shape[2:]), 2, 3)

    q, k, v, beta, g = (chunk(t) for t in (q, k, v, beta, g))
    g = jnp.cumsum(g, axis=-1)
    tril = jnp.tril(jnp.ones((c, c), dtype=bool))
    strict = jnp.tril(jnp.ones((c, c), dtype=bool), k=-1)
    diff = g[..., :, None] - g[..., None, :]
    decay = jnp.where(tril, jnp.exp(jnp.where(tril, diff, 0.0)), 0.0)
    k_beta = k * beta[..., None]
    l_mat = jnp.where(strict, jnp.einsum('bnhid,bnhjd->bnhij', k_beta, k) * decay, 0.0)
    eye = jnp.eye(c, dtype=q.dtype)
    t_inv = lax.linalg.triangular_solve(eye + l_mat, jnp.broadcast_to(eye, l_mat.shape), left_side=True, lower=True, unit_diagonal=True)
    u = jnp.einsum('bnhij,bnhje->bnhie', t_inv, v * beta[..., None])
    w = jnp.einsum('bnhij,bnhjd->bnhid', t_inv, k_beta * jnp.exp(g)[..., None])
    intra = jnp.where(tril, jnp.einsum('bnhid,bnhjd->bnhij', q, k) * decay, 0.0)
    q_dec = q * jnp.exp(g)[..., None]
    g_last = g[..., -1]
    k_tail = k * jnp.exp(g_last[..., None] - g)[..., None]

    def step(state, xs):
        w_c, u_c, q_c, a_c, kt_c, gl_c = xs
        v_new = u_c - jnp.einsum('bhcd,bhde->bhce', w_c, state)
        o_c = jnp.einsum('bhcd,bhde->bhce', q_c, state) + jnp.einsum('bhij,bhje->bhie', a_c, v_new)
        state = state * jnp.exp(gl_c)[..., None, None] + jnp.einsum('bhcd,bhce->bhde', kt_c, v_new)
        return state, o_c

    xs = tuple(jnp.moveaxis(t, 1, 0) for t in (w, u, q_dec, intra, k_tail, g_last))
    state0 = jnp.zeros((b, h, dk, dv), q.dtype)
    _, o = lax.scan(step, state0, xs)
    return jnp.transpose(o, (1, 0, 3, 2, 4)).reshape(b, s, h, dv)


def mla_attention(q_lat, q_rope, c_kv, k_rope):
    b, s, h, c = q_lat.shape
    nblk = s // Q_BLOCK
    scale = (MLA_NOPE + MLA_ROPE) ** -0.5
    ckv = c_kv.astype(jnp.float32)
    kr = k_rope.astype(jnp.float32)
    key_idx = jnp.arange(s)

    def blocks(t):
        return jnp.swapaxes(t.reshape((b, nblk, Q_BLOCK) + t.shape[2:]), 0, 1)

    def one_block(args):
        ql, qr, blk = args
        sc = (jnp.einsum('bqhc,bkc->bhqk', ql, ckv) + jnp.einsum('bqhr,bkr->bhqk', qr, kr)) * scale
        q_idx = blk * Q_BLOCK + jnp.arange(Q_BLOCK)
        sc = jnp.where(key_idx[None, :] <= q_idx[:, None], sc, NEG_BIG)
        pr = jax.nn.softmax(sc, axis=-1)
        return jnp.einsum('bhqk,bkc->bqhc', pr, ckv)

    out = lax.map(one_block, (blocks(q_lat.astype(jnp.float32)), blocks(q_rope.astype(jnp.float32)), jnp.arange(nblk)))
    return jnp.swapaxes(out, 0, 1).reshape(b, s, h, c)


def setup_inputs(seed: int = 0) -> dict:
    key = jax.random.key(seed)
    ks = jax.random.split(key, 32)

    def nrm(k, shape, scale):
        return jax.random.normal(k, shape, jnp.float32) * scale

    x = nrm(ks[0], (BATCH, SEQ, D_MODEL), 1.0)
    p = nrm(ks[1], (DEPTH, BATCH, SEQ, PLE_DIM), 1.0)
    positions = jax.random.randint(ks[2], (BATCH, 1), 0, 1024, dtype=jnp.int32) + jnp.arange(SEQ, dtype=jnp.int32)[None, :]
    w_in = nrm(ks[3], (DEPTH, D_MODEL, D_IN), D_MODEL ** -0.5)
    conv_w = nrm(ks[4], (DEPTH, DN_CONV, DN_CONV_CH), DN_CONV ** -0.5)
    dn_a_log = jnp.log(jax.random.uniform(ks[5], (DEPTH, DN_HEADS), jnp.float32, 1.0, 16.0))
    dt = jnp.exp(jax.random.uniform(ks[6], (DEPTH, DN_HEADS), jnp.float32, math.log(1e-3), math.log(1e-1)))
    dn_dt_bias = dt + jnp.log(-jnp.expm1(-dt))
    dn_norm_w = 1.0 + nrm(ks[7], (DEPTH, DN_DV), 0.01)
    q_norm_w = 1.0 + nrm(ks[8], (DEPTH, MLA_Q_LORA), 0.01)
    w_uq = nrm(ks[9], (DEPTH, MLA_Q_LORA, MLA_HEADS, MLA_NOPE + MLA_ROPE), MLA_Q_LORA ** -0.5)
    kv_norm_w = 1.0 + nrm(ks[10], (DEPTH, MLA_KV_LORA), 0.01)
    w_uk = nrm(ks[11], (DEPTH, MLA_KV_LORA, MLA_HEADS, MLA_NOPE), MLA_KV_LORA ** -0.5)
    w_uv = nrm(ks[12], (DEPTH, MLA_KV_LORA, MLA_HEADS, MLA_V), MLA_KV_LORA ** -0.5)
    w_br_dn = nrm(ks[13], (DEPTH, DN_V_W, D_MODEL), DN_V_W ** -0.5)
    w_br_mla = nrm(ks[14], (DEPTH, MLA_V_W, D_MODEL), MLA_V_W ** -0.5)
    w_o = nrm(ks[15], (DEPTH, D_MODEL, D_MODEL), D_MODEL ** -0.5 * DEEPNORM_BETA)
    ln1_g = 1.0 + nrm(ks[16], (DEPTH, D_MODEL), 0.01)
    ln1_b = nrm(ks[17], (DEPTH, D_MODEL), 0.01)
    w_ffn_in = nrm(ks[18], (DEPTH, D_MODEL, 2 * FFN_HIDDEN), D_MODEL ** -0.5)
    w_ffn_out = nrm(ks[19], (DEPTH, FFN_HIDDEN, D_MODEL), FFN_HIDDEN ** -0.5 * DEEPNORM_BETA)
    w_ple = nrm(ks[20], (DEPTH, PLE_DIM, D_MODEL), PLE_DIM ** -0.5 * DEEPNORM_BETA)
    w_ple_gate = nrm(ks[21], (DEPTH, D_MODEL, D_MODEL), D_MODEL ** -0.5)
    ln2_g = 1.0 + nrm(ks[22], (DEPTH, D_MODEL), 0.01)
    ln2_b = nrm(ks[23], (DEPTH, D_MODEL), 0.01)
    return {'x': x, 'p': p, 'positions': positions, 'w_in': w_in, 'conv_w': conv_w, 'dn_a_log': dn_a_log, 'dn_dt_bias': dn_dt_bias, 'dn_norm_w': dn_norm_w, 'q_norm_w': q_norm_w, 'w_uq': w_uq, 'kv_norm_w': kv_norm_w, 'w_uk': w_uk, 'w_uv': w_uv, 'w_br_dn': w_br_dn, 'w_br_mla': w_br_mla, 'w_o': w_o, 'ln1_g': ln1_g, 'ln1_b': ln1_b, 'w_ffn_in': w_ffn_in, 'w_ffn_out': w_ffn_out, 'w_ple': w_ple, 'w_ple_gate': w_ple_gate, 'ln2_g': ln2_g, 'ln2_b': ln2_b}


def reference(x, p, positions, w_in, conv_w, dn_a_log, dn_dt_bias, dn_norm_w, q_norm_w, w_uq, kv_norm_w, w_uk, w_uv, w_br_dn, w_br_mla, w_o, ln1_g, ln1_b, w_ffn_in, w_ffn_out, w_ple, w_ple_gate, ln2_g, ln2_b):
    b, s, _ = x.shape
    cos, sin = rope_tables(positions)
    h = x
    for i in range(DEPTH):
        proj = h @ w_in[i]
        qkv, z, b_raw, a_raw, cq, ckv, kr, gate_dn, gate_mla = jnp.split(proj, SPLIT_IDX, axis=-1)

        qkv = jax.nn.silu(causal_depthwise_conv(qkv, conv_w[i]))
        dq, dk, dv = jnp.split(qkv, [DN_QK_W, 2 * DN_QK_W], axis=-1)
        dq = l2_normalize(dq.reshape(b, s, DN_HEADS, DN_DK)) * (DN_DK ** -0.5)
        dk = l2_normalize(dk.reshape(b, s, DN_HEADS, DN_DK))
        dv = dv.reshape(b, s, DN_HEADS, DN_DV).astype(jnp.float32)
        beta = jax.nn.sigmoid(b_raw.astype(jnp.float32))
        g = -jnp.exp(dn_a_log[i].astype(jnp.float32)) * jax.nn.softplus(a_raw.astype(jnp.float32) + dn_dt_bias[i].astype(jnp.float32))
        o_dn = gated_delta_rule(dq, dk, dv, beta, g)
        o_dn = rms_norm(o_dn, dn_norm_w[i]) * jax.nn.silu(z.reshape(b, s, DN_HEADS, DN_DV).astype(jnp.float32))
        y_dn = o_dn.reshape(b, s, DN_V_W).astype(h.dtype) @ w_br_dn[i]

        c_q = rms_norm(cq, q_norm_w[i]).astype(h.dtype)
        q_full = jnp.einsum('bsc,chd->bshd', c_q, w_uq[i])
        q_nope, q_rope = jnp.split(q_full, [MLA_NOPE], axis=-1)
        q_rope = apply_rope(q_rope, cos[:, :, None, :], sin[:, :, None, :])
        c_kv = rms_norm(ckv, kv_norm_w[i])
        k_rope = apply_rope(kr, cos, sin)
        q_lat = jnp.einsum('bshd,chd->bshc', q_nope.astype(jnp.float32), w_uk[i].astype(jnp.float32))
        out_lat = mla_attention(q_lat, q_rope, c_kv, k_rope)
        o_mla = jnp.einsum('bshc,chd->bshd', out_lat, w_uv[i].astype(jnp.float32))
        y_mla = o_mla.reshape(b, s, MLA_V_W).astype(h.dtype) @ w_br_mla[i]

        mixed = jax.nn.sigmoid(gate_dn) * y_dn + jax.nn.sigmoid(gate_mla) * y_mla
        h = layer_norm(DEEPNORM_ALPHA * h + mixed @ w_o[i], ln1_g[i], ln1_b[i])

        gt, up = jnp.split(h @ w_ffn_in[i], 2, axis=-1)
        ffn = (jax.nn.silu(gt) * up) @ w_ffn_out[i]
        ple = jax.nn.sigmoid(h @ w_ple_gate[i]) * (p[i] @ w_ple[i])
        h = layer_norm(DEEPNORM_ALPHA * h + ffn + ple, ln2_g[i], ln2_b[i])
    return h
```

```python
import numpy as np
from contextlib import ExitStack
import concourse.bass as bass
import concourse.mybir as mybir
from concourse.bass_utils import run_bass_kernel_spmd

F32 = mybir.dt.float32
BF16 = mybir.dt.bfloat16
I32 = mybir.dt.int32
AF = mybir.ActivationFunctionType
ALU = mybir.AluOpType
NEG = -30000.0
ALPHA = 2.0 ** 0.25


class Prog:
    ENG = ('pe', 'act', 'dve', 'pool', 'sp')

    def __init__(self, nc):
        self.nc = nc
        self.ops = []
        self.last_w = {}
        self.readers = {}
        self.dma_cnt = {}

    def op(self, eng, fn, reads=(), writes=(), dma_slot=None):
        if getattr(self, 'defer', None) is not None:
            self.defer.append((eng, fn, list(reads), list(writes), dma_slot))
            return None
        o = dict(eng=eng, fn=fn, deps=set(), idx=len(self.ops), need_inc=False, dma_slot=dma_slot)
        isps = lambda k: isinstance(k, tuple) and k[0] in ('pf', 'pb')
        writes = [k[:2] if isps(k) else k for k in writes] + [k[:2] for k in reads if isps(k)]
        reads = [k for k in reads if not isps(k)]
        writes = list(dict.fromkeys(writes))
        for r in reads:
            w = self.last_w.get(r)
            if w is not None:
                o['deps'].add(w)
        for w_ in writes:
            w = self.last_w.get(w_)
            if w is not None:
                o['deps'].add(w)
            for rd in self.readers.get(w_, ()):
                o['deps'].add(rd)
        for r in reads:
            self.readers.setdefault(r, []).append(o['idx'])
        for w_ in writes:
            self.last_w[w_] = o['idx']
            self.readers[w_] = []
        o['deps'].discard(o['idx'])
        if dma_slot is not None:
            self.dma_cnt[dma_slot] = self.dma_cnt.get(dma_slot, 0) + 1
            o['dma_val'] = 16 * self.dma_cnt[dma_slot]
        self.ops.append(o)
        return o

    def deferred(self, thunk):
        self.defer = []
        thunk()
        rec, self.defer = self.defer, None
        hops, cur = [], None
        for r in rec:
            if r[0] != cur:
                hops.append([])
                cur = r[0]
            hops[-1].append(r)

        def mk(hop):
            def run():
                for (eng, fn, rd, wr, slot) in hop:
                    self.op(eng, fn, rd, wr, slot)
            return run
        return [mk(h) for h in hops]

    def emit(self, es):
        nc = self.nc
        ops = self.ops
        for o in ops:
            for d in o['deps']:
                a = ops[d]
                if a['dma_slot'] is None:
                    if a['eng'] == 'pe' and o['eng'] == 'pe':
                        continue
                    a['need_inc'] = True
        SEG = 30000
        cnt = {e: 0 for e in self.ENG}
        for o in ops:
            if o['dma_slot'] is None and o['need_inc']:
                c = cnt[o['eng']]
                o['sem'] = (o['eng'], c // SEG)
                o['val'] = c % SEG + 1
                cnt[o['eng']] = c + 1
        sems = {}

        def getsem(key):
            if key not in sems:
                sems[key] = es.enter_context(nc.semaphore("s_" + "_".join(str(k) for k in key)))
            return sems[key]
        waited = {e: {} for e in self.ENG}
        for o in ops:
            need = {}
            for d in o['deps']:
                a = ops[d]
                if a['dma_slot'] is not None:
                    key = ('dma', a['dma_slot'])
                    v = a['dma_val']
                else:
                    if a['eng'] == 'pe' and o['eng'] == 'pe':
                        continue
                    key = a['sem']
                    v = a['val']
                if need.get(key, 0) < v:
                    need[key] = v
            w = []
            for key, v in need.items():
                if waited[o['eng']].get(key, 0) >= v:
                    continue
                waited[o['eng']][key] = v
                w.append((getsem(key), v))
            o['waits'] = w
            if o['dma_slot'] is not None:
                o['incsem'] = getsem(('dma', o['dma_slot']))
            elif o['need_inc']:
                o['incsem'] = getsem(o['sem'])
        final = [(getsem(('dma', k)), 16 * v) for k, v in self.dma_cnt.items()]
        per = {e: [o for o in ops if o['eng'] == e] for e in self.ENG}

        def run(eng_obj, lst, fin=False):
            for o in lst:
                for (s, v) in o['waits']:
                    eng_obj.wait_ge(s, v)
                ins = o['fn'](eng_obj)
                if o['dma_slot'] is not None:
                    ins.then_inc(o['incsem'], 16)
                elif o['need_inc']:
                    ins.then_inc(o['incsem'], 1)
            if fin:
                for (s, v) in final:
                    eng_obj.wait_ge(s, v)
        with nc.Block() as block:
            @block.tensor
            def _(e):
                run(e, per['pe'])

            @block.scalar
            def _(e):
                run(e, per['act'])

            @block.vector
            def _(e):
                run(e, per['dve'])

            @block.gpsimd
            def _(e):
                run(e, per['pool'])

            @block.sync
            def _(e):
                run(e, per['sp'], fin=True)


def build(TH, dbg=False):
    NTH = TH // 128
    NT = 2 * NTH
    nc = bass.Bass("TRN2", target_bir_lowering=False)

    def di(n, s, dt=F32):
        return nc.dram_tensor(n, s, dt, kind="ExternalInput").ap()
    xs = di("xs", [2 * TH, 1024])
    pp = di("pp", [TH, 256])
    pos = di("pos", [128, NT], I32)
    kbias_d = di("kbias", [128, NT])
    invf_d = di("invf", [128, 64])
    w_in = di("w_in", [1024, 6864])
    conv_w = di("conv_w", [4, 3072])
    a_log = di("dn_a_log", [8])
    dt_bias = di("dn_dt_bias", [8])
    dn_norm_w = di("dn_norm_w", [128])
    q_norm_w = di("q_norm_w", [3, 128])
    w_uq = di("w_uq", [384, 1536])
    kv_norm_w = di("kv_norm_w", [256])
    w_uk = di("w_uk", [256, 1024])
    w_uv = di("w_uv", [256, 1024])
    w_br_dn = di("w_br_dn", [1024, 1024])
    w_br_mla = di("w_br_mla", [1024, 1024])
    w_o = di("w_o", [1024, 1024])
    ln1_g = di("ln1_g", [1024])
    ln1_b = di("ln1_b", [1024])
    w_ffn_in = di("w_ffn_in", [1024, 5632])
    w_ffn_out = di("w_ffn_out", [2816, 1024])
    w_ple = di("w_ple", [256, 1024])
    w_ple_gate = di("w_ple_gate", [1024, 1024])
    ln2_g = di("ln2_g", [1024])
    ln2_b = di("ln2_b", [1024])
    out = nc.dram_tensor("out", [TH, 1024], F32, kind="ExternalOutput").ap()
    odn_d = nc.dram_tensor("odn_s", [TH, 1024], F32).ap()
    ymla_d = nc.dram_tensor("ymla_s", [TH, 1024], F32).ap()
    h1_d = nc.dram_tensor("h1_s", [TH, 1024], F32).ap()
    cq_d = nc.dram_tensor("cq_s", [TH // 128, 128, 384], BF16).ap()
    qn_d = nc.dram_tensor("qn_s", [8, 128, TH], BF16).ap()
    kn_d = nc.dram_tensor("kn_s", [8, 128, 2 * TH], BF16).ap()
    vn_d = nc.dram_tensor("vn_s", [8, 128, 2 * TH], BF16).ap()

    es = ExitStack()
    with es:
        P = Prog(nc)
        uid = [0]

        def sbt(es_, name, shape, dt=F32):
            return es_.enter_context(nc.sbuf_tensor(name, shape, dt))

        pf = [es.enter_context(nc.psum_tensor("pf%d" % i, [128, 512], F32)) for i in range(6)]
        pb = [es.enter_context(nc.psum_tensor("pb%d" % i, [128, 1024], BF16)) for i in range(2)]

        def MM(o, lhsT, rhs, start, stop, r, w):
            P.op('pe', lambda e: e.matmul(o, lhsT=lhsT, rhs=rhs, start=start, stop=stop), reads=r, writes=w)

        def TR(o, i, ident, r, w):
            P.op('pe', lambda e: e.transpose(out=o, in_=i, identity=ident), reads=r, writes=w)

        def ACT(o, i, func, r, w, bias=None, scale=None, accum=None):
            kw = {}
            if bias is not None:
                kw['bias'] = bias
            if scale is not None:
                kw['scale'] = scale
            if accum is not None:
                kw['accum_out'] = accum
            P.op('act', lambda e: e.activation(out=o, in_=i, func=func, **kw), reads=r, writes=w)

        def TT(eng, o, a, b, op, r, w):
            P.op(eng, lambda e: e.tensor_tensor(out=o, in0=a, in1=b, op=op), reads=r, writes=w)

        def TS(eng, o, a, s1, op0, r, w, s2=None, op1=None):
            if op1 is None:
                P.op(eng, lambda e: e.tensor_scalar(out=o, in0=a, scalar1=s1, scalar2=None, op0=op0), reads=r, writes=w)
            else:
                P.op(eng, lambda e: e.tensor_scalar(out=o, in0=a, scalar1=s1, scalar2=s2, op0=op0, op1=op1), reads=r, writes=w)

        def STT(eng, o, a, s, b, op0, op1, r, w):
            P.op(eng, lambda e: e.scalar_tensor_tensor(out=o, in0=a, scalar=s, in1=b, op0=op0, op1=op1), reads=r, writes=w)

        def CP(eng, o, i, r, w):
            if eng == 'act':
                P.op('act', lambda e: e.copy(out=o, in_=i), reads=r, writes=w)
            else:
                P.op(eng, lambda e: e.tensor_copy(out=o, in_=i), reads=r, writes=w)

        def DMA(eng, o, i, r, w, slot):
            P.op(eng, lambda e: e.dma_start(out=o, in_=i), reads=r, writes=w, dma_slot=slot)

        def MEMSET(eng, o, v, w):
            P.op(eng, lambda e: e.memset(o, v), writes=w)

        def AFSEL(o, pattern, cmp, fill, cm, key):
            P.op('pool', lambda e: e.affine_select(out=o, in_=o, pattern=pattern, compare_op=cmp, fill=fill, base=0, channel_multiplier=cm), reads=[key], writes=[key])

        ident_f = sbt(es, "ident_f", [128, 128])
        ident_b = sbt(es, "ident_b", [128, 128], BF16)
        ones_f = sbt(es, "ones_f", [128, 128])
        ones_b = sbt(es, "ones_b", [128, 128], BF16)
        triU_f = sbt(es, "triU_f", [128, 128])
        caus_b = sbt(es, "caus_b", [128, 128], BF16)
        maskLs = sbt(es, "maskLs", [128, 128])
        maskU = sbt(es, "maskU", [128, 128])
        onecol = sbt(es, "onecol", [128, 1])
        epscol = sbt(es, "epscol", [128, 4])
        MEMSET('pool', ident_f[:], 0.0, ['ident_f'])
        AFSEL(ident_f[:], [[-1, 128]], ALU.not_equal, 1.0, 1, 'ident_f')
        CP('pool', ident_b[:], ident_f[:], ['ident_f'], ['ident_b'])
        MEMSET('pool', ones_f[:], 1.0, ['ones_f'])
        MEMSET('pool', ones_b[:], 1.0, ['ones_b'])
        MEMSET('pool', onecol[:], 1.0, ['onecol'])
        MEMSET('pool', epscol[:, 0:1], 1e-6, ['epscol'])
        MEMSET('pool', epscol[:, 1:2], 1e-5, ['epscol'])
        MEMSET('pool', triU_f[:], 1.0, ['triU_f'])
        AFSEL(triU_f[:], [[1, 128]], ALU.is_ge, 0.0, -1, 'triU_f')
        CP('pool', caus_b[:], triU_f[:], ['triU_f'], ['caus_b'])
        MEMSET('pool', maskLs[:], 0.0, ['maskLs'])
        AFSEL(maskLs[:], [[-1, 128]], ALU.is_gt, NEG, 1, 'maskLs')
        MEMSET('pool', maskU[:], 0.0, ['maskU'])
        AFSEL(maskU[:], [[1, 128]], ALU.is_ge, NEG, -1, 'maskU')

        negA = sbt(es, "negA", [128, 8])
        dtb = sbt(es, "dtb", [128, 8])
        dnw = sbt(es, "dnw", [128, 128])
        kvw = sbt(es, "kvw", [128, 256])
        invf = sbt(es, "invf_s", [128, 64])
        kbias = sbt(es, "kbias_s", [128, NT])
        DMA('sp', negA[:], a_log.partition_broadcast(128), [], ['negA'], 'c_negA')
        DMA('sp', dtb[:], dt_bias.partition_broadcast(128), [], ['dtb'], 'c_dtb')
        DMA('sp', dnw[:], dn_norm_w.partition_broadcast(128), [], ['dnw'], 'c_dnw')
        DMA('sp', kvw[:], kv_norm_w.partition_broadcast(128), [], ['kvw'], 'c_kvw')
        DMA('sp', invf[:], invf_d, [], ['invf'], 'c_invf')
        DMA('sp', kbias[:], kbias_d, [], ['kbias'], 'c_kbias')
        ACT(negA[:], negA[:], AF.Exp, ['negA'], ['negA'])
        TS('dve', negA[:], negA[:], -1.0, ALU.mult, ['negA'], ['negA'])

        cw = sbt(es, "cw", [128, 24, 4])
        qnw = sbt(es, "qnw", [128, 3])
        bar_t = sbt(es, "bar_t", [128, 8])
        eAB = ExitStack()
        ckvT = sbt(eAB, "ckvT", [128, 2, NT * 128], BF16)
        krT = sbt(eAB, "krT", [64, NT * 128], BF16)
        ckvk = sbt(eAB, "ckvk", [128, NT, 256], BF16)
        cs_all = sbt(eAB, "cs_all", [128, NT, 64])
        with ExitStack() as e0:
            cwr = sbt(e0, "cwr", [4, 3072])
            qnr = sbt(e0, "qnr", [3, 128])
            DMA('sp', cwr[:], conv_w, [], ['cwr'], 'c_cwr')
            DMA('sp', qnr[:], q_norm_w, [], ['qnr'], 'c_qnr')
            for blk in range(24):
                TR(pf[0][:, blk * 4:(blk + 1) * 4], cwr[0:4, blk * 128:(blk + 1) * 128], ident_f[0:4, 0:4], ['cwr', 'ident_f'], [('pf', 0)])
            CP('dve', cw[:].rearrange("p b i -> p (b i)"), pf[0][:, 0:96], [('pf', 0)], ['cw'])
            TR(pf[1][:, 0:3], qnr[0:3, :], ident_f[0:3, 0:3], ['qnr', 'ident_f'], [('pf', 1)])
            CP('dve', qnw[:], pf[1][:, 0:3], [('pf', 1)], ['qnw'])

        keysA = ['cwr', 'qnr', 'wuq_s', 'wuk_s', 'wuv_s', 'wbm_s', 'AT', 'BT', 'VT']

        def barrier(extra_reads):
            allk = list(set(list(P.last_w.keys()) + list(P.readers.keys())))
            P.op('pool', lambda e: e.memset(bar_t[:, 0:1], 1.0), reads=allk, writes=allk + ['BAR'])
            P.op('dve', lambda e: e.memset(bar_t[:, 1:2], 1.0), reads=['BAR'], writes=['BAR_dve'])
            P.op('act', lambda e: e.copy(out=bar_t[:, 2:3], in_=bar_t[:, 0:1]), reads=['BAR'], writes=['BAR_act'])
            P.op('pe', lambda e: e.transpose(out=pf[0][0:4, 0:4], in_=ident_f[0:4, 0:4], identity=ident_f[0:4, 0:4]), reads=['BAR'], writes=[('pf', 0)])
            P.op('sp', lambda e: e.dma_start(out=bar_t[:, 4:8], in_=kbias_d[:, 0:4]), reads=['BAR'], writes=['BAR_sp'], dma_slot='bar_sp')

        barrier(keysA)

        NGRP = NT // 4
        GP = NTH // 4
        beta_a = sbt(eAB, "beta_a", [128, NT, 8])
        gcs_a = sbt(eAB, "gcs_a", [128, NT, 8])
        eg_a = sbt(eAB, "eg_a", [128, NT, 8])
        ekt_a = sbt(eAB, "ekt_a", [128, NT, 8])
        egl_a = sbt(eAB, "egl_a", [128, NT, 8])
        bkg_a = sbt(eAB, "bkg_a", [128, NT, 8])
        g8_a = sbt(eAB, "g8_a", [128, NT, 8])
        with ExitStack() as er:
            posi = sbt(er, "posi", [128, NT], I32)
            posf = sbt(er, "posf", [128, NT])
            ang = sbt(er, "ang", [128, NT, 64])
            angi = sbt(er, "angi", [128, NT, 64], I32)
            angf = sbt(er, "angf", [128, NT, 64])
            DMA('sp', posi[:], pos, [], ['posi'], 'posi')
            CP('dve', posf[:], posi[:], ['posi'], ['posf'])
            TT('dve', ang[:], invf[:, None, :].broadcast_to([128, NT, 64]), posf[:, :, None].broadcast_to([128, NT, 64]), ALU.mult, ['invf', 'posf'], ['ang'])
            TS('dve', ang[:, :, 32:64], ang[:, :, 32:64], 0.25, ALU.add, ['ang'], ['ang'])
            CP('dve', angi[:], ang[:], ['ang'], ['angi'])
            CP('dve', angf[:], angi[:], ['angi'], ['angf'])
            TT('dve', ang[:], ang[:], angf[:], ALU.subtract, ['ang', 'angf'], ['ang'])
            TS('dve', angf[:], ang[:], 0.5, ALU.is_gt, ['ang'], ['angf'])
            TT('dve', ang[:], ang[:], angf[:], ALU.subtract, ['ang', 'angf'], ['ang'])
            TS('dve', angf[:], ang[:], -0.5, ALU.is_lt, ['ang'], ['angf'])
            TT('dve', ang[:], ang[:], angf[:], ALU.add, ['ang', 'angf'], ['ang'])
            ACT(cs_all[:], ang[:], AF.Sin, ['ang'], ['cs_all'], scale=2 * np.pi)
        barrier([])
        with ExitStack() as ea:
            wA = sbt(ea, "wA", [128, 8, 3792], BF16)
            wA_grp = {c2: (2 if c2 < 4 else (0 if c2 < 8 else 1)) for c2 in range(12)}
            for kc in range(8):
                DMA('pool', wA[:, kc, 3072:3792], w_in[kc * 128:(kc + 1) * 128, 4096:4816], [], [('wA', 'small')], 'wA_small')
            for gi, (c0_, c1_) in enumerate(((1024, 2048), (2048, 3072), (0, 1024))):
                for kc in range(8):
                    DMA('pool', wA[:, kc, c0_:c1_], w_in[kc * 128:(kc + 1) * 128, c0_:c1_], [], [('wA', gi)], 'wA_%d' % gi)
            wAk = [('wA', 'small')]
            xb = [sbt(ea, "xb%d" % i, [128, 1024], BF16) for i in range(2)]
            xTg = [sbt(ea, "xTg%d" % i, [128, 8, 512], BF16) for i in range(2)]
            tmp8 = sbt(ea, "tmp8", [128, 4, 8])
            smallf = sbt(ea, "smallf", [128, 4, 8])
            ssq8 = [sbt(ea, "ssq8_%d" % i, [128, 8]) for i in range(2)]
            junk = sbt(ea, "junk", [128, 384])
            junk2 = [sbt(ea, "junk2_%d" % i, [128, 64]) for i in range(2)]
            cqb = [sbt(ea, "cqb%d" % i, [128, 384], BF16) for i in range(2)]
            krr = [sbt(ea, "krr%d" % i, [128, 64]) for i in range(2)]
            krb = [sbt(ea, "krb%d" % i, [128, 64], BF16) for i in range(2)]
            cqs = [sbt(ea, "cqs%d" % i, [128, 384], BF16) for i in range(2)]
            hist = sbt(ea, "hist", [128, 24, 3])
            D_CB, D_CT, D_SG, D_SL, D_SQ, D_LN, D_OB = 3, 4, 3, 6, 3, 3, 3
            cb = [sbt(ea, "cb%d" % i, [128, 515]) for i in range(D_CB)]
            ct = [sbt(ea, "ct%d" % i, [128, 512]) for i in range(D_CT)]
            sg = [sbt(ea, "sg%d" % i, [128, 512]) for i in range(D_SG)]
            sl = [sbt(ea, "sl%d" % i, [128, 512]) for i in range(D_SL)]
            sq = [sbt(ea, "sq%d" % i, [128, 512], BF16) for i in range(D_SQ)]
            lnr = [sbt(ea, "lnr%d" % i, [128, 512]) for i in range(D_LN)]
            ob = [sbt(ea, "ob%d" % i, [128, 512], BF16) for i in range(D_OB)]
            MEMSET('pool', hist[:], 0.0, ['hist'])
            bankc = [0]

            def nb():
                b = 2 + bankc[0] % 4
                bankc[0] += 1
                return b

            def x_prologue(g):
                Xg = xTg[g % 2]
                for j in range(4):
                    kt = g * 4 + j
                    s2 = kt % 2
                    DMA('pool', xb[s2][:], xs[kt * 128:(kt + 1) * 128, :], [], [('xb', s2)], 'xb%d' % s2)
                    for kc in range(8):
                        TR(pb[s2][:, kc * 128:(kc + 1) * 128], xb[s2][:, kc * 128:(kc + 1) * 128], ident_b[:], [('xb', s2), 'ident_b'], [('pb', s2)])
                    CP('act', Xg[:, :, j * 128:(j + 1) * 128], pb[s2][:, 0:1024].rearrange("p (a b) -> p a b", a=8), [('pb', s2)], [('xTg', g % 2)])

            def gates(g):
                Xg = xTg[g % 2]
                kXg = ('xTg', g % 2)
                t4 = slice(g * 4, g * 4 + 4)
                ka = ('gates', g)
                for j in range(4):
                    for kc in range(8):
                        MM(pf[0][:, j * 16:(j + 1) * 16], Xg[:, kc, j * 128:(j + 1) * 128], wA[:, kc, 3072:3088], kc == 0, kc == 7, [kXg] + wAk, [('pf', 0)])
                pba = pf[0][:, 0:64].rearrange("p (j c) -> p j c", j=4)
                ACT(beta_a[:, t4, :], pba[:, :, 0:8], AF.Sigmoid, [('pf', 0)], [ka])
                TT('dve', tmp8[:], pba[:, :, 8:16], dtb[:, None, :].broadcast_to([128, 4, 8]), ALU.add, [('pf', 0), 'dtb'], ['tmp8'])
                ACT(tmp8[:], tmp8[:], AF.Exp, ['tmp8'], ['tmp8'])
                ACT(tmp8[:], tmp8[:], AF.Ln, ['tmp8', 'onecol'], ['tmp8'], bias=onecol[:, 0:1])
                TT('dve', g8_a[:, t4, :], tmp8[:], negA[:, None, :].broadcast_to([128, 4, 8]), ALU.mult, ['tmp8', 'negA'], [ka])
                g32 = g8_a[:, t4, :].rearrange("p j c -> p (j c)")
                MM(pf[0][:, 64:96], triU_f[:], g32, True, True, ['triU_f', ka], [('pf', 0)])
                MM(pf[0][:, 96:128], ones_f[:], g32, True, True, ['ones_f', ka], [('pf', 0)])
                pgc = pf[0][:, 64:96].rearrange("p (j c) -> p j c", j=4)
                pgl = pf[0][:, 96:128].rearrange("p (j c) -> p j c", j=4)
                CP('dve', gcs_a[:, t4, :], pgc, [('pf', 0)], [ka])
                ACT(eg_a[:, t4, :], pgc, AF.Exp, [('pf', 0)], [ka])
                ACT(egl_a[:, t4, :], pgl, AF.Exp, [('pf', 0)], [ka])
                TT('dve', smallf[:], pgl, gcs_a[:, t4, :], ALU.subtract, [('pf', 0), ka], ['smallf'])
                ACT(ekt_a[:, t4, :], smallf[:], AF.Exp, ['smallf'], [ka])
                TT('dve', bkg_a[:, t4, :], beta_a[:, t4, :], eg_a[:, t4, :], ALU.mult, [ka], [ka])

            def keys_tile(kt):
                g, j = kt // 4, kt % 4
                Xg = xTg[g % 2]
                kXg = ('xTg', g % 2)
                own = kt >= NTH
                ot = kt - NTH
                s2 = kt % 2
                tsl_ = slice(j * 128, (j + 1) * 128)
                cs = cs_all[:, kt, :]
                for kc in range(8):
                    MM(pf[1][:, 0:320], Xg[:, kc, tsl_], wA[:, kc, 3472:3792], kc == 0, kc == 7, [kXg] + wAk, [('pf', 1)])
                sq_ = ssq8[s2]
                ks = ('ssq8', s2)
                ACT(junk[:, 0:256], pf[1][:, 0:256], AF.Square, [('pf', 1)], ['junk', ks], accum=sq_[:, 0:1])
                ACT(sq_[:, 1:2], sq_[:, 0:1], AF.Ln, [ks, 'epscol'], [ks], bias=epscol[:, 0:1], scale=1.0 / 256)
                ACT(sq_[:, 2:3], sq_[:, 1:2], AF.Exp, [ks], [ks], scale=-0.5)
                STT('dve', ckvk[:, kt, :], pf[1][:, 0:256], sq_[:, 2:3], kvw[:], ALU.mult, ALU.mult, [('pf', 1), ks, 'kvw'], [('ckvk', kt)])
                kr_, j2_ = krr[s2], junk2[s2]
                TT('dve', kr_[:, 0:32], pf[1][:, 256:288], cs[:, 32:64], ALU.mult, [('pf', 1), 'cs_all'], [('krr', s2)])
                TT('dve', kr_[:, 32:64], pf[1][:, 288:320], cs[:, 32:64], ALU.mult, [('pf', 1), 'cs_all'], [('krr', s2)])
                TT('dve', j2_[:, 0:32], pf[1][:, 288:320], cs[:, 0:32], ALU.mult, [('pf', 1), 'cs_all'], [('junk2', s2)])
                TT('dve', j2_[:, 32:64], pf[1][:, 256:288], cs[:, 0:32], ALU.mult, [('pf', 1), 'cs_all'], [('junk2', s2)])
                TT('pool', kr_[:, 0:32], kr_[:, 0:32], j2_[:, 0:32], ALU.subtract, [('krr', s2), ('junk2', s2)], [('krr', s2)])
                TT('pool', kr_[:, 32:64], kr_[:, 32:64], j2_[:, 32:64], ALU.add, [('krr', s2), ('junk2', s2)], [('krr', s2)])
                CP('pool', krb[s2][:], kr_[:], [('krr', s2)], [('krb', s2)])
                for c in range(2):
                    TR(pb[s2][:, c * 128:(c + 1) * 128], ckvk[:, kt, c * 128:(c + 1) * 128], ident_b[:], [('ckvk', kt), 'ident_b'], [('pb', s2)])
                TR(pb[s2][0:64, 256:384], krb[s2][:], ident_b[:], [('krb', s2), 'ident_b'], [('pb', s2)])
                if own:
                    for kc in range(8):
                        MM(pf[0][:, 128:512], Xg[:, kc, tsl_], wA[:, kc, 3088:3472], kc == 0, kc == 7, [kXg] + wAk, [('pf', 0)])
                    ACT(junk[:, 0:384], pf[0][:, 128:512], AF.Square, [('pf', 0)], ['junk', ks], accum=sq_[:, 3:4])
                    ACT(sq_[:, 4:5], sq_[:, 3:4], AF.Ln, [ks, 'epscol'], [ks], bias=epscol[:, 0:1], scale=1.0 / 384)
                    ACT(sq_[:, 5:6], sq_[:, 4:5], AF.Exp, [ks], [ks], scale=-0.5)
                    TS('dve', cqb[s2][:], pf[0][:, 128:512], sq_[:, 5:6], ALU.mult, [('pf', 0), ks], [('cqb', s2)])
                    for c in range(3):
                        TR(pb[s2][:, 384 + c * 128:384 + (c + 1) * 128], cqb[s2][:, c * 128:(c + 1) * 128], ident_b[:], [('cqb', s2), 'ident_b'], [('pb', s2)])
                CP('act', ckvT[:, :, kt * 128:(kt + 1) * 128], pb[s2][:, 0:256].rearrange("p (c t) -> p c t", c=2), [('pb', s2)], [('ckvT', kt)])
                CP('act', krT[:, kt * 128:(kt + 1) * 128], pb[s2][0:64, 256:384], [('pb', s2)], [('krT', kt)])
                if own:
                    CP('act', cqs[s2][:], pb[s2][:, 384:768], [('pb', s2)], [('cqs', s2)])
                    DMA('sp', cq_d[ot], cqs[s2][:], [('cqs', s2)], [('cq_d', ot)], 'cqs%d' % s2)

            blocks = []
            for g in range(NGRP):
                own_g = g >= GP
                k_in_g = 0
                for h in range(8):
                    for kind in ('q', 'k', 'v'):
                        hist_only = False
                        if kind == 'q' and not own_g:
                            if g != GP - 1:
                                continue
                            hist_only = True
                        blocks.append(dict(g=g, h=h, kind=kind, hist_only=hist_only, blk={'q': 0, 'k': 8, 'v': 16}[kind] + h, idx=len(blocks), k_in_g=k_in_g))
                        k_in_g += 1

            def stage(B, s):
                g, h, kind, blk = B['g'], B['h'], B['kind'], B['blk']
                i_ = B['idx']
                rcb, rct, rsg, rsl, rsq, rln, rob = i_ % D_CB, i_ % D_CT, i_ % D_SG, i_ % D_SL, i_ % D_SQ, i_ % D_LN, i_ % D_OB
                Xg = xTg[g % 2]
                kXg = ('xTg', g % 2)
                gsl = slice(g * 512, (g + 1) * 512)
                bP = 2 + i_ % 2
                bO = 4 + i_ % 2
                if s == 0:
                    if B['k_in_g'] == 0:
                        hs_ = P.deferred(lambda: gates(g))
                        pending.extend(hs_)
                        pending_grp.extend([g] * len(hs_))
                    if B['k_in_g'] in (2, 5, 8, 11):
                        kt_ = g * 4 + (B['k_in_g'] - 2) // 3
                        hs_ = P.deferred(lambda: keys_tile(kt_))
                        pending.extend(hs_)
                        pending_grp.extend([g] * len(hs_))
                    if B['k_in_g'] == 13 and g + 1 < NGRP:
                        while pending:
                            pending.pop(0)()
                            pending_grp.pop(0)
                        x_prologue(g + 1)
                    for _ in range(3):
                        if pending:
                            pending.pop(0)()
                            pending_grp.pop(0)
                    for kc in range(8):
                        MM(pf[bP][:], wA[:, kc, blk * 128:(blk + 1) * 128], Xg[:, kc, :], kc == 0, kc == 7, [kXg, ('wA', wA_grp[blk // 2])], [('pf', bP)])
                    return
                if s == 1:
                    CP('act', cb[rcb][:, 3:515], pf[bP][:], [('pf', bP)], [('cb', rcb)])
                    CP('pool', cb[rcb][:, 0:3], hist[:, blk, :], [('hist', blk)], [('cb', rcb)])
                    CP('pool', hist[:, blk, :], cb[rcb][:, 512:515], [('cb', rcb)], [('hist', blk)])
                    return
                if B['hist_only']:
                    return
                if s == 2:
                    TS('dve', ct[rct][:], cb[rcb][:, 0:512], cw[:, blk, 0:1], ALU.mult, [('cb', rcb), 'cw'], [('ct', rct)])
                    for i in range(1, 4):
                        STT('dve', ct[rct][:], cb[rcb][:, i:i + 512], cw[:, blk, i:i + 1], ct[rct][:], ALU.mult, ALU.add, [('cb', rcb), 'cw', ('ct', rct)], [('ct', rct)])
                elif s == 3:
                    ACT(sg[rsg][:], ct[rct][:], AF.Exp, [('ct', rct)], [('sg', rsg)], scale=-1.0)
                    ACT(sg[rsg][:], sg[rsg][:], AF.Ln, [('sg', rsg), 'onecol'], [('sg', rsg)], bias=onecol[:, 0:1])
                    ACT(sg[rsg][:], sg[rsg][:], AF.Exp, [('sg', rsg)], [('sg', rsg)], scale=-1.0)
                elif s == 4:
                    if kind == 'v':
                        TT('dve', ob[rob][:], ct[rct][:], sg[rsg][:], ALU.mult, [('ct', rct), ('sg', rsg)], [('ob', rob)])
                        DMA('sp', vn_d[h, :, gsl], ob[rob][:], [('ob', rob)], [('vn_d', g)], 'ob%d' % rob)
                        return
                    TT('dve', sl[rsl][:], ct[rct][:], sg[rsg][:], ALU.mult, [('ct', rct), ('sg', rsg)], [('sl', rsl)])
                elif kind == 'v':
                    return
                elif s == 5:
                    TT('pool', sq[rsq][:], sl[rsl][:], sl[rsl][:], ALU.mult, [('sl', rsl)], [('sq', rsq)])
                elif s == 6:
                    MM(pf[bO][:], ones_b[:], sq[rsq][:], True, True, ['ones_b', ('sq', rsq)], [('pf', bO)])
                elif s == 7:
                    ACT(lnr[rln][:], pf[bO][:], AF.Ln, [('pf', bO), 'epscol'], [('lnr', rln)], bias=epscol[:, 0:1])
                    ACT(lnr[rln][:], lnr[rln][:], AF.Exp, [('lnr', rln)], [('lnr', rln)], scale=-0.5)
                elif s == 8:
                    if kind == 'q':
                        STT('dve', ob[rob][:], sl[rsl][:], 128.0 ** -0.5, lnr[rln][:], ALU.mult, ALU.mult, [('sl', rsl), ('lnr', rln)], [('ob', rob)])
                        DMA('sp', qn_d[h, :, (g - GP) * 512:(g - GP + 1) * 512], ob[rob][:], [('ob', rob)], [('qn_d', g)], 'ob%d' % rob)
                    else:
                        TT('dve', ob[rob][:], sl[rsl][:], lnr[rln][:], ALU.mult, [('sl', rsl), ('lnr', rln)], [('ob', rob)])
                        DMA('sp', kn_d[h, :, gsl], ob[rob][:], [('ob', rob)], [('kn_d', g)], 'ob%d' % rob)

            pending = []
            pending_grp = []
            x_prologue(0)
            NS = 9
            for step in range(len(blocks) + NS - 1):
                for s in reversed(range(NS)):
                    i = step - s
                    if 0 <= i < len(blocks):
                        stage(blocks[i], s)
            while pending:
                pending.pop(0)()
            barrier([])

        with ExitStack() as ea:
            NSL = 2

            def per_slot(name, shape, dt=F32):
                return [sbt(ea, "%s_s%d" % (name, i), shape, dt) for i in range(NSL)]
            NIN = 4
            kin = [sbt(ea, "kin_s%d" % i, [128, 8, 128], BF16) for i in range(NIN)]
            vin = [sbt(ea, "vin_s%d" % i, [128, 8, 128], BF16) for i in range(NIN)]
            qin = [sbt(ea, "qin_s%d" % i, [128, 8, 128], BF16) for i in range(NIN)]
            rhsg = sbt(ea, "rhsg", [128, 8, 128])
            negones = sbt(ea, "negones", [128, 128])
            MEMSET('pool', negones[:], -1.0, ['negones'])
            Kbg = per_slot("Kbg", [128, 8, 128], BF16)
            Ktt = per_slot("Ktt", [128, 8, 128], BF16)
            Vb = per_slot("Vb", [128, 8, 128], BF16)
            Lf = [sbt(ea, "Lf%d" % i, [128, 4, 128]) for i in range(2)]
            LS = [[[sbt(ea, "LS%d_%d_%d" % (s_, hb_, c_), [128, 2, 4, 128], BF16) for c_ in range(2)] for hb_ in range(2)] for s_ in range(NSL)]
            MS = [[[sbt(ea, "MS%d_%d_%d" % (s_, hb_, c_), [128, 2, 4, 128], BF16) for c_ in range(2)] for hb_ in range(2)] for s_ in range(NSL)]
            YS = [[[sbt(ea, "YS%d_%d_%d" % (s_, hb_, c_), [128, 2, 4, 128], BF16) for c_ in range(2)] for hb_ in range(2)] for s_ in range(NSL)]
            YF = [[[sbt(ea, "YF%d_%d_%d" % (s_, hb_, c_), [128, 4, 128]) for c_ in range(2)] for hb_ in range(2)] for s_ in range(NSL)]
            decL = [sbt(ea, "decL%d" % i, [128, 4, 128]) for i in range(2)]
            decT = [sbt(ea, "decT%d" % i, [128, 4, 128]) for i in range(2)]
            ATb = per_slot("ATb", [128, 8, 128], BF16)
            Ub = per_slot("Ub", [128, 8, 128])
            WTb = per_slot("WTb", [128, 8, 128], BF16)
            Vn = sbt(ea, "Vn", [128, 8, 128], BF16)
            Sf = sbt(ea, "Sf", [128, 8, 128])
            Sb = sbt(ea, "Sb", [128, 8, 128], BF16)
            o1 = sbt(ea, "o1", [128, 4, 128])
            otile = [sbt(ea, "otile", [128, 8, 128])] * NSL
            MEMSET('pool', Sf[:], 0.0, ['Sf0', 'Sf1'])
            MEMSET('pool', Sb[:], 0.0, ['Sb0', 'Sb1'])
            bankc = [0]

            def nb6():
                b = bankc[0] % 6
                bankc[0] += 1
                return b, ('pf', b)

            def f4(t, hb):
                return t[:, hb * 4:(hb + 1) * 4, :]

            def p4(b):
                return pf[b][:].rearrange("p (h t) -> p h t", h=4)

            def bc4(ap2):
                return ap2[:, None, :].broadcast_to([128, 4, 128])

            def col4(arr, kt, hb):
                return arr[:, kt, hb * 4:(hb + 1) * 4, None].broadcast_to([128, 4, 128])

            def a2_loads(kt):
                own = kt >= NTH
                ot = kt - NTH
                sl = kt % NIN
                tok = slice(kt * 128, (kt + 1) * 128)
                DMA('sp', kin[sl][:], kn_d[:, :, tok].rearrange("h d t -> d h t"), [('kn_d', kt // 4)], [('kin', sl)], 'kin%d' % sl)
                DMA('sp', vin[sl][:], vn_d[:, :, tok].rearrange("h d t -> d h t"), [('vn_d', kt // 4)], [('vin', sl)], 'vin%d' % sl)
                if own:
                    DMA('sp', qin[sl][:], qn_d[:, :, ot * 128:(ot + 1) * 128].rearrange("h d t -> d h t"), [('qn_d', kt // 4)], [('qin', sl)], 'qin%d' % sl)

            def a2_prep(kt):
                own = kt >= NTH
                sl = kt % NSL
                ss = 's%d' % sl
                ka = ('gates', kt // 4)
                TT('pool', rhsg[:], triU_f[:, None, :].broadcast_to([128, 8, 128]), g8_a[:, kt, :, None].broadcast_to([128, 8, 128]), ALU.mult, ['triU_f', ka], ['rhsg'])
                hops = []
                for hb in range(2):
                    hops.append(a2_prep_hops(kt, hb))
                for hi in range(len(hops[0])):
                    for hp in hops:
                        hp[hi]()

            def a2_prep_hops(kt, hb):
                own = kt >= NTH
                sl = kt % NSL
                si = kt % NIN
                ss = 's%d' % sl
                ka = ('gates', kt // 4)
                kh = 'h%d' % hb + ss
                dL, dT, Lf_ = decL[hb], decT[hb], Lf[hb]
                kdL, kdT, kLf = ('decL', hb), ('decT', hb), ('Lf', hb)
                L0, M0, Y0, YF0 = LS[sl][hb][0], MS[sl][hb][0], YS[sl][hb][0], YF[sl][hb][0]
                st = {}
                pkb = ('pb', hb)

                def p1():
                    for i in range(4):
                        TR(pb[hb][:, i * 128:(i + 1) * 128], kin[si][:, hb * 4 + i, :], ident_b[:], [('kin', si), 'ident_b'], [pkb])
                    for i in range(4):
                        TR(pb[hb][:, 512 + i * 128:512 + (i + 1) * 128], vin[si][:, hb * 4 + i, :], ident_b[:], [('vin', si), 'ident_b'], [pkb])
                    st['bkk'], st['kkk'] = nb6()
                    for i in range(4):
                        h = hb * 4 + i
                        MM(pf[st['bkk']][:, i * 128:(i + 1) * 128], kin[si][:, h, :], kin[si][:, h, :], True, True, [('kin', si)], [st['kkk']])
                    st['bdf'], st['kdf'] = nb6()
                    for i in range(4):
                        h = hb * 4 + i
                        MM(pf[st['bdf']][:, i * 128:(i + 1) * 128], negones[:], rhsg[:, h, :], True, False, ['negones', 'rhsg'], [st['kdf']])
                        MM(pf[st['bdf']][:, i * 128:(i + 1) * 128], rhsg[:, h, :], ones_f[:], False, True, ['ones_f', 'rhsg'], [st['kdf']])

                def p2():
                    pk = pb[hb][:, 0:512].rearrange("p (h t) -> p h t", h=4)
                    pv_ = pb[hb][:, 512:1024].rearrange("p (h t) -> p h t", h=4)
                    TT('dve', dL[:], p4(st['bdf']), bc4(maskLs), ALU.add, [st['kdf'], 'maskLs'], [kdL])
                    if own:
                        TT('dve', dT[:], bc4(maskU), p4(st['bdf']), ALU.subtract, [st['kdf'], 'maskU'], [kdT])
                    TT('dve', f4(Kbg[sl], hb), pk, col4(bkg_a, kt, hb), ALU.mult, [pkb, ka], ['Kbg' + kh])
                    TT('pool' if False else 'dve', f4(Ktt[sl], hb), pk, col4(ekt_a, kt, hb), ALU.mult, [pkb, ka], ['Ktt' + kh])
                    TT('dve', f4(Vb[sl], hb), pv_, col4(beta_a, kt, hb), ALU.mult, [pkb, ka], ['Vb' + kh])

                def p3():
                    ACT(dL[:], dL[:], AF.Exp, [kdL], [kdL])
                    if own:
                        ACT(dT[:], dT[:], AF.Exp, [kdT], [kdT])

                def p4_():
                    TT('dve', dL[:], dL[:], col4(beta_a, kt, hb), ALU.mult, [kdL, ka], [kdL])
                    TT('dve', Lf_[:], p4(st['bkk']), dL[:], ALU.mult, [st['kkk'], kdL], [kLf])

                def p5():
                    CP('act', L0[:, 0], Lf_[:], [kLf], ['L0' + kh])

                def p6():
                    TT('dve', L0[:, 1], Lf_[:], L0[:, 0], ALU.subtract, [kLf, 'L0' + kh], ['L0' + kh])

                def p7():
                    for a_ in range(2):
                        for i in range(4):
                            TR(pb[hb][:, a_ * 512 + i * 128:a_ * 512 + (i + 1) * 128], L0[:, a_, i, :], ident_b[:], ['L0' + kh, 'ident_b'], [pkb])
                    if own:
                        st['bkq'], st['kkq'] = nb6()
                        for i in range(4):
                            h = hb * 4 + i
                            MM(pf[st['bkq']][:, i * 128:(i + 1) * 128], kin[si][:, h, :], qin[si][:, h, :], True, True, [('kin', si), ('qin', si)], [st['kkq']])

                def p8():
                    CP('act', M0[:].rearrange("p a h t -> p (a h t)"), pb[hb][:, 0:1024], [pkb], ['M0' + kh])
                    TT('dve', YF0[:], bc4(ident_f), pb[hb][:, 0:512].rearrange("p (h t) -> p h t", h=4), ALU.subtract, ['ident_f', pkb], ['YF0' + kh])
                    TT('dve', YF0[:], YF0[:], pb[hb][:, 512:1024].rearrange("p (h t) -> p h t", h=4), ALU.subtract, ['YF0' + kh, pkb], ['YF0' + kh])
                    if own:
                        TT('dve', f4(ATb[sl], hb), p4(st['bkq']), dT[:], ALU.mult, [st['kkq'], kdT], ['ATb' + kh])

                def p9():
                    CP('act', Y0[:, 0], YF0[:], ['YF0' + kh], ['Y0' + kh])

                def p10():
                    TT('pool', Y0[:, 1], YF0[:], Y0[:, 0], ALU.subtract, ['YF0' + kh, 'Y0' + kh], ['Y0' + kh])
                return [p1, p2, p3, p4_, p5, p6, p7, p8, p9, p10]

            def a2_level_hops(kt, lev, hb):
                sl = kt % NSL
                kh = 'h%d' % hb + 's%d' % sl
                cur, nxt = lev % 2, (lev + 1) % 2
                Lc, Mc, Yc, YFc = 'L%d' % cur + kh, 'M%d' % cur + kh, 'Y%d' % cur + kh, 'YF%d' % cur + kh
                Ln_, Mn_, Yn_, YFn = 'L%d' % nxt + kh, 'M%d' % nxt + kh, 'Y%d' % nxt + kh, 'YF%d' % nxt + kh
                Lc_t, Mc_t, Yc_t = LS[sl][hb][cur], MS[sl][hb][cur], YS[sl][hb][cur]
                Ln_t, Mn_t, Yn_t = LS[sl][hb][nxt], MS[sl][hb][nxt], YS[sl][hb][nxt]
                st = {}

                def h1():
                    st['bl'], st['kl'] = nb6()
                    for i in range(4):
                        o_ = pf[st['bl']][:, i * 128:(i + 1) * 128]
                        MM(o_, Mc_t[:, 0, i, :], Lc_t[:, 0, i, :], True, False, [Mc, Lc], [st['kl']])
                        MM(o_, Mc_t[:, 0, i, :], Lc_t[:, 1, i, :], False, False, [Mc, Lc], [st['kl']])
                        MM(o_, Mc_t[:, 1, i, :], Lc_t[:, 0, i, :], False, True, [Mc, Lc], [st['kl']])

                def h2():
                    CP('act', Ln_t[:, 0], p4(st['bl']), [st['kl']], [Ln_])

                def h3():
                    TT('dve', Ln_t[:, 1], p4(st['bl']), Ln_t[:, 0], ALU.subtract, [st['kl'], Ln_], [Ln_])

                def h4():
                    if lev < 5:
                        for a_ in range(2):
                            for i in range(4):
                                TR(pb[hb][:, a_ * 512 + i * 128:a_ * 512 + (i + 1) * 128], Ln_t[:, a_, i, :], ident_b[:], [Ln_, 'ident_b'], [('pb', hb)])
                    st['by'], st['ky'] = nb6()
                    for i in range(4):
                        o_ = pf[st['by']][:, i * 128:(i + 1) * 128]
                        MM(o_, Ln_t[:, 0, i, :], Yc_t[:, 0, i, :], True, False, [Ln_, Yc], [st['ky']])
                        MM(o_, Ln_t[:, 0, i, :], Yc_t[:, 1, i, :], False, False, [Ln_, Yc], [st['ky']])
                        MM(o_, Ln_t[:, 1, i, :], Yc_t[:, 0, i, :], False, True, [Ln_, Yc], [st['ky']])

                def h5():
                    if lev < 5:
                        CP('act', Mn_t[:, 0].rearrange("p h t -> p (h t)"), pb[hb][:, 0:512], [('pb', hb)], [Mn_])
                        CP('dve', Mn_t[:, 1].rearrange("p h t -> p (h t)"), pb[hb][:, 512:1024], [('pb', hb)], [Mn_])
                    TT('dve', YF[sl][hb][nxt][:], p4(st['by']), YF[sl][hb][cur][:], ALU.add, [st['ky'], YFc], [YFn])

                def h6():
                    CP('act', Yn_t[:, 0], YF[sl][hb][nxt][:], [YFn], [Yn_])

                def h7():
                    if lev < 5:
                        TT('pool', Yn_t[:, 1], YF[sl][hb][nxt][:], Yn_t[:, 0], ALU.subtract, [YFn, Yn_], [Yn_])
                return [h1, h2, h3, h4, h5, h6, h7]

            def a2_finish(kt):
                sl = kt % NSL
                for hb in range(2):
                    kh = 'h%d' % hb + 's%d' % sl
                    TTv = YS[sl][hb][0]
                    kT_ = 'Y0' + kh
                    bu, ku = nb6()
                    for i in range(4):
                        h = hb * 4 + i
                        MM(pf[bu][:, i * 128:(i + 1) * 128], TTv[:, 0, i, :], Vb[sl][:, h, :], True, True, [kT_, 'Vb' + kh], [ku])
                    CP('act', f4(Ub[sl], hb), p4(bu), [ku], ['Ub' + kh])
                    bw, kw_ = nb6()
                    for i in range(4):
                        h = hb * 4 + i
                        MM(pf[bw][:, i * 128:(i + 1) * 128], Kbg[sl][:, h, :], TTv[:, 0, i, :], True, True, [kT_, 'Kbg' + kh], [kw_])
                    CP('dve', f4(WTb[sl], hb), p4(bw), [kw_], ['WTb' + kh])

            def a2_recur(kt):
                own = kt >= NTH
                ot = kt - NTH
                sl = kt % NSL
                si = kt % NIN
                ka = ('gates', kt // 4)
                for hb in range(2):
                    kh = 'h%d' % hb + 's%d' % sl
                    bws, kws = nb6()
                    for i in range(4):
                        h = hb * 4 + i
                        MM(pf[bws][:, i * 128:(i + 1) * 128], WTb[sl][:, h, :], Sb[:, h, :], True, True, ['WTb' + kh, 'Sb%d' % hb], [kws])
                    TT('dve', f4(Vn, hb), f4(Ub[sl], hb), p4(bws), ALU.subtract, ['Ub' + kh, kws], ['Vn%d' % hb])
                    if own:
                        bq1, kq1 = nb6()
                        for i in range(4):
                            h = hb * 4 + i
                            MM(pf[bq1][:, i * 128:(i + 1) * 128], qin[si][:, h, :], Sb[:, h, :], True, True, [('qin', si), 'Sb%d' % hb], [kq1])
                        bq2, kq2 = nb6()
                        for i in range(4):
                            h = hb * 4 + i
                            MM(pf[bq2][:, i * 128:(i + 1) * 128], ATb[sl][:, h, :], Vn[:, h, :], True, True, ['ATb' + kh, 'Vn%d' % hb], [kq2])
                        TT('dve', o1[:], p4(bq1), col4(eg_a, kt, hb), ALU.mult, [kq1, ka], ['o1'])
                        TT('dve', f4(otile[sl], hb), o1[:], p4(bq2), ALU.add, ['o1', kq2], ['otile'])
                    bkv, kkv = nb6()
                    for i in range(4):
                        h = hb * 4 + i
                        MM(pf[bkv][:, i * 128:(i + 1) * 128], Ktt[sl][:, h, :], Vn[:, h, :], True, True, ['Ktt' + kh, 'Vn%d' % hb], [kkv])
                    TT('pool', f4(Sf, hb), f4(Sf, hb), col4(egl_a, kt, hb), ALU.mult, ['Sf%d' % hb, ka], ['Sf%d' % hb])
                    TT('dve', f4(Sf, hb), f4(Sf, hb), p4(bkv), ALU.add, ['Sf%d' % hb, kkv], ['Sf%d' % hb])
                    CP('act', f4(Sb, hb), f4(Sf, hb), ['Sf%d' % hb], ['Sb%d' % hb])
                if own:
                    DMA('sp', odn_d[ot * 128:(ot + 1) * 128, :], otile[sl][:].rearrange("p h t -> p (h t)"), ['otile'], [('odn_d', ot)], 'odn_out')

            for kt in range(min(4, NT)):
                a2_loads(kt)
            for p_ in range(NT // 2):
                t0, t1 = 2 * p_, 2 * p_ + 1
                a2_prep(t0)
                a2_prep(t1)
                for lev in range(6):
                    hops = [a2_level_hops(kt, lev, hb) for kt in (t0, t1) for hb in range(2)]
                    for hi in (0, 1, 2):
                        for hp in hops:
                            hp[hi]()
                    for pair in ((0, 1), (2, 3)):
                        for si_ in pair:
                            hops[si_][3]()
                        for si_ in pair:
                            hops[si_][4]()
                    for hi in (5, 6):
                        for hp in hops:
                            hp[hi]()
                a2_finish(t0)
                a2_finish(t1)
                a2_recur(t0)
                a2_recur(t1)
                for kt in (t0 + 4, t1 + 4):
                    if kt < NT:
                        a2_loads(kt)
            barrier([])

        SCALE = 192.0 ** -0.5
        with ExitStack() as eb:
            Wabs = sbt(eb, "Wabs", [128, 3, 8, 256], BF16)
            wuqr = sbt(eb, "wuqr", [128, 3, 512], BF16)
            Wov = sbt(eb, "Wov", [128, 2, 8, 1024], BF16)
            with ExitStack() as e1:
                wuv_s = sbt(e1, "wuv_s", [128, 2, 1024])
                wbm_s = sbt(e1, "wbm_s", [128, 8, 1024])
                VT = sbt(e1, "VTt", [128, 256])
                DMA('sp', wuv_s[:], w_uv.rearrange("(c p) n -> p c n", p=128), [], ['wuv_s'], 'c_wuv')
                DMA('sp', wbm_s[:], w_br_mla.rearrange("(c p) n -> p c n", p=128), [], ['wbm_s'], 'c_wbm')
                for h in range(8):
                    for c in range(2):
                        TR(pf[1][:, 256 + c * 128:256 + (c + 1) * 128], wuv_s[:, c, h * 128:(h + 1) * 128], ident_f[:], ['wuv_s', 'ident_f'], [('pf', 1)])
                    CP('dve', VT[:], pf[1][:, 256:512], [('pf', 1)], ['VT'])
                    for c in range(2):
                        for n in range(2):
                            MM(pf[3][:], VT[:, c * 128:(c + 1) * 128], wbm_s[:, h, n * 512:(n + 1) * 512], True, True, ['VT', 'wbm_s'], [('pf', 3)])
                            CP('dve', Wov[:, c, h, n * 512:(n + 1) * 512], pf[3][:], [('pf', 3)], ['Wov'])
                wuq_s = sbt(e1, "wuq_s", [128, 3, 1536])
                wuk_s = sbt(e1, "wuk_s", [128, 2, 1024])
                AT = sbt(e1, "ATt", [128, 384])
                BT = sbt(e1, "BTt", [128, 256])
                DMA('sp', wuq_s[:], w_uq.rearrange("(c p) n -> p c n", p=128), [], ['wuq_s'], 'c_wuq')
                DMA('sp', wuk_s[:], w_uk.rearrange("(c p) n -> p c n", p=128), [], ['wuk_s'], 'c_wuk')
                for c in range(3):
                    TS('dve', wuq_s[:, c, :], wuq_s[:, c, :], qnw[:, c:c + 1], ALU.mult, ['wuq_s', 'qnw'], ['wuq_s'])
                for c in range(3):
                    CP('dve', wuqr[:, c, :].rearrange("p (h r) -> p h r", h=8), wuq_s[:, c, :].rearrange("p (h d) -> p h d", h=8)[:, :, 128:192], ['wuq_s'], ['wuqr'])
                for h in range(8):
                    for c in range(3):
                        TR(pf[0][:, c * 128:(c + 1) * 128], wuq_s[:, c, h * 192:h * 192 + 128], ident_f[:], ['wuq_s', 'ident_f'], [('pf', 0)])
                    CP('act', AT[:], pf[0][:, 0:384], [('pf', 0)], ['AT'])
                    for c in range(2):
                        TR(pf[1][:, c * 128:(c + 1) * 128], wuk_s[:, c, h * 128:(h + 1) * 128], ident_f[:], ['wuk_s', 'ident_f'], [('pf', 1)])
                    CP('dve', BT[:], pf[1][:, 0:256], [('pf', 1)], ['BT'])
                    for c in range(3):
                        MM(pf[2][:, 0:256], AT[:, c * 128:(c + 1) * 128], BT[:], True, True, ['AT', 'BT'], [('pf', 2)])
                        CP('act', Wabs[:, c, h, :], pf[2][:, 0:256], [('pf', 2)], ['Wabs'])

            barrier([])
            cqt = [sbt(eb, "cqt%d" % i, [128, 3, 128], BF16) for i in range(2)]
            qlT2 = [sbt(eb, "qlT%d" % i, [128, 2, 8, 128], BF16) for i in range(2)]
            qrp = sbt(eb, "qrp", [128, 8, 64])
            qrt = sbt(eb, "qrt", [128, 8, 64])
            qrb = sbt(eb, "qrb", [128, 8, 64], BF16)
            qrT2 = [sbt(eb, "qrT%d" % i, [64, 8, 128], BF16) for i in range(2)]
            PT = [sbt(eb, "PT%d" % i, [128, 512], BF16) for i in range(2)]
            rec = sbt(eb, "rec", [128, 512])
            olT = sbt(eb, "olT", [128, 2, 8, 128], BF16)
            ym = sbt(eb, "ym", [128, 1024])

            def cq_load(qt):
                if qt < NTH:
                    DMA('sp', cqt[qt % 2][:].rearrange("p c t -> p (c t)"), cq_d[qt], [('cq_d', qt)], [('cqt', qt % 2)], 'cqt%d' % (qt % 2))

            def prologue_rounds(qt):
                u = qt % 2
                cq_ = cqt[u]
                kq = ('cqt', u)
                rounds = []
                for cc in range(2):
                    for hq in range(2):
                        def rnd(cc=cc, hq=hq):
                            for i in range(4):
                                h = hq * 4 + i
                                for c in range(3):
                                    MM(pf[5][:, i * 128:(i + 1) * 128], Wabs[:, c, h, cc * 128:(cc + 1) * 128], cq_[:, c, :], c == 0, c == 2, ['Wabs', kq], [('pf', 5)])
                            CP('dve', qlT2[u][:, cc, hq * 4:(hq + 1) * 4, :].rearrange("p h t -> p (h t)"), pf[5][:], [('pf', 5)], [('qlT', u)])
                        rounds.append(rnd)

                def rope():
                    for c in range(3):
                        MM(pf[5][:], cq_[:, c, :], wuqr[:, c, :], c == 0, c == 2, ['wuqr', kq], [('pf', 5)])
                    pv = pf[5][:].rearrange("p (h r) -> p h r", h=8)
                    sinb = cs_all[:, NTH + qt, None, 0:32].broadcast_to([128, 8, 32])
                    cosb = cs_all[:, NTH + qt, None, 32:64].broadcast_to([128, 8, 32])
                    TT('dve', qrp[:, :, 0:32], pv[:, :, 0:32], cosb, ALU.mult, [('pf', 5), 'cs_all'], ['qrp'])
                    TT('dve', qrp[:, :, 32:64], pv[:, :, 32:64], cosb, ALU.mult, [('pf', 5), 'cs_all'], ['qrp'])
                    TT('dve', qrt[:, :, 0:32], pv[:, :, 32:64], sinb, ALU.mult, [('pf', 5), 'cs_all'], ['qrt'])
                    TT('dve', qrt[:, :, 32:64], pv[:, :, 0:32], sinb, ALU.mult, [('pf', 5), 'cs_all'], ['qrt'])
                    TT('dve', qrb[:, :, 0:32], qrp[:, :, 0:32], qrt[:, :, 0:32], ALU.subtract, ['qrp', 'qrt'], ['qrb'])
                    TT('dve', qrb[:, :, 32:64], qrp[:, :, 32:64], qrt[:, :, 32:64], ALU.add, ['qrp', 'qrt'], ['qrb'])
                    for h in range(8):
                        TR(pb[0][0:64, h * 128:(h + 1) * 128], qrb[:, h, :], ident_b[:], ['qrb', 'ident_b'], [('pb', 0)])
                    CP('dve', qrT2[u][:].rearrange("p h t -> p (h t)"), pb[0][0:64, 0:1024], [('pb', 0)], [('qrT', u)])
                rounds.append(rope)
                return rounds

            cq_load(0)
            cq_load(1)
            for r_ in prologue_rounds(0):
                r_()
            for qt in range(NTH):
                qlT = qlT2[qt % 2]
                qrT = qrT2[qt % 2]
                kql, kqr = ('qlT', qt % 2), ('qrT', qt % 2)
                nxt_rounds = prologue_rounds(qt + 1) if qt + 1 < NTH else []
                nkb = NTH + qt + 1
                for hg in range(2):
                    hs = slice(hg * 4, (hg + 1) * 4)
                    def scores(kb):
                        sp_ = kb % 2
                        ksl = slice(kb * 128, (kb + 1) * 128)
                        for cc in range(2):
                            MM(pf[sp_][:], ckvT[:, cc, ksl], qlT[:, cc, hs, :].rearrange("p h t -> p (h t)"), cc == 0, False, [('ckvT', kb), kql], [('pf', sp_)])
                        MM(pf[sp_][:], krT[:, ksl], qrT[:, hs, :].rearrange("p h t -> p (h t)"), False, True, [('krT', kb), kqr], [('pf', sp_)])

                    scores(0)
                    for kb in range(nkb):
                        sp_ = kb % 2
                        if kb + 1 < nkb:
                            scores(kb + 1)
                        ACT(PT[sp_][:], pf[sp_][:], AF.Exp, [('pf', sp_), 'kbias'], [('PT', sp_)], bias=kbias[:, kb:kb + 1], scale=SCALE)
                        if kb == nkb - 1:
                            TT('pool', PT[sp_][:].rearrange("p (h t) -> p h t", h=4), PT[sp_][:].rearrange("p (h t) -> p h t", h=4), caus_b[:, None, :].broadcast_to([128, 4, 128]), ALU.mult, [('PT', sp_), 'caus_b'], [('PT', sp_)])
                        for cc in range(2):
                            MM(pf[2 + cc][:], ckvk[:, kb, cc * 128:(cc + 1) * 128], PT[sp_][:], kb == 0, kb == nkb - 1, [('ckvk', kb), ('PT', sp_)], [('pf', 2 + cc)])
                        MM(pf[4][:], ones_b[:], PT[sp_][:], kb == 0, kb == nkb - 1, ['ones_b', ('PT', sp_)], [('pf', 4)])
                        if hg == 1 and kb % 3 == 1 and nxt_rounds:
                            nxt_rounds.pop(0)()
                    ACT(rec[:], pf[4][:], AF.Ln, [('pf', 4)], ['rec'])
                    ACT(rec[:], rec[:], AF.Exp, ['rec'], ['rec'], scale=-1.0)
                    for cc in range(2):
                        TT('dve', olT[:, cc, hs, :].rearrange("p h t -> p (h t)"), pf[2 + cc][:], rec[:], ALU.mult, [('pf', 2 + cc), 'rec'], ['olT'])
                while nxt_rounds:
                    nxt_rounds.pop(0)()
                for n in range(2):
                    i = 0
                    for h in range(8):
                        for cc in range(2):
                            MM(pf[n][:], olT[:, cc, h, :], Wov[:, cc, h, n * 512:(n + 1) * 512], i == 0, i == 15, ['olT', 'Wov'], [('pf', n)])
                            i += 1
                    CP('act' if n else 'dve', ym[:, n * 512:(n + 1) * 512], pf[n][:], [('pf', n)], ['ym'])
                DMA('sp', ymla_d[qt * 128:(qt + 1) * 128, :], ym[:], ['ym'], [('ymla_d', qt)], 'ym_out')
                cq_load(qt + 2)
            barrier([])
        eAB.close()

        def load_w(es_, name, src, kchunks, ncols, c0=0, colblk=512, groups=None):
            t = sbt(es_, name, [128, kchunks, ncols], BF16)
            nblk = ncols // colblk
            ngrp = len(groups) if groups is not None else nblk
            if name == 'wf1':
                parts = [([(0, 1536), (2816, 4352)], [(name, j) for j in range(6)], 'w_wf1a'),
                         ([(1536, 2816), (4352, 5632)], [(name, j) for j in range(6, 11)], 'w_wf1b')]
                for ranges, keys, slot in parts:
                    for (a_, b_) in ranges:
                        for kc in range(kchunks):
                            DMA('pool', t[:, kc, a_:b_], src[kc * 128:(kc + 1) * 128, c0 + a_:c0 + b_], [], keys, slot)
                return t
            keys = [(name, gi) for gi in range(ngrp)]
            for kc in range(kchunks):
                DMA('pool', t[:, kc, :], src[kc * 128:(kc + 1) * 128, c0:c0 + ncols], [], keys, 'w_' + name)
            return t

        def bcast_vec(es_, name, src, n):
            t = sbt(es_, name, [128, n])
            DMA('sp', t[:], src.partition_broadcast(128), [], [name], 'v_' + name)
            return t

        def layer_norm(x_ap, g_t, b_t, out_ap, stats, mv, rstd, rkey, wkey, gk, bk, sfx):
            xr = x_ap.rearrange("p (c f) -> p c f", c=2)
            for c in range(2):
                P.op('dve', (lambda o_, i_: (lambda e: e.bn_stats(out=o_, in_=i_)))(stats[:, c, :], xr[:, c, :]), reads=[rkey], writes=['ln_stats' + sfx])
            P.op('dve', lambda e: e.bn_aggr(out=mv[:], in_=stats[:]), reads=['ln_stats' + sfx], writes=['ln_mv' + sfx])
            ACT(rstd[:, 0:1], mv[:, 1:2], AF.Ln, ['ln_mv' + sfx, 'epscol'], ['ln_rstd' + sfx], bias=epscol[:, 1:2])
            ACT(rstd[:, 0:1], rstd[:, 0:1], AF.Exp, ['ln_rstd' + sfx], ['ln_rstd' + sfx], scale=-0.5)
            TS('dve', x_ap, x_ap, mv[:, 0:1], ALU.subtract, [rkey, 'ln_mv' + sfx, 'ln_rstd' + sfx], [rkey], s2=rstd[:, 0:1], op1=ALU.mult)
            TT('dve', x_ap, x_ap, g_t[:], ALU.mult, [rkey, gk], [rkey])
            TT('dve', out_ap, x_ap, b_t[:], ALU.add, [rkey, bk], [wkey])

        with ExitStack() as ec:
            wg = load_w(ec, "wg", w_in, 8, 2048, 4816)
            wz = load_w(ec, "wz", w_in, 8, 1024, 3072)
            wbd = load_w(ec, "wbd", w_br_dn, 8, 1024)
            wo = load_w(ec, "wo", w_o, 8, 1024)
            g1 = bcast_vec(ec, "g1", ln1_g, 1024)
            b1 = bcast_vec(ec, "b1", ln1_b, 1024)

            def two(name, shape, dt=F32):
                return [sbt(ec, "%s_%d" % (name, i), shape, dt) for i in range(2)]
            zs = two("zs", [128, 1024])
            orms = two("orms", [128, 8])
            odf = two("odf", [128, 1024])
            junkc = sbt(ec, "junkc", [128, 128])
            xf = two("xf", [128, 1024])
            xbb = two("xbb", [128, 1024], BF16)
            xTc = two("xTc", [128, 8, 128], BF16)
            odt = two("odt", [128, 1024], BF16)
            odT = two("odT", [128, 8, 128], BF16)
            ymt = two("ymt", [128, 1024])
            sg = two("sg", [128, 2048])
            mix = two("mix", [128, 1024])
            mixb = two("mixb", [128, 1024], BF16)
            mixT = two("mixT", [128, 8, 128], BF16)
            r1 = two("r1", [128, 1024])
            hout = two("hout", [128, 1024])
            stats = two("stats", [128, 2, 6])
            mv = two("mv", [128, 2])
            rstd = two("rstd", [128, 1])

            def c1_loads(t):
                u = t % 2
                tsl = slice(t * 128, (t + 1) * 128)
                DMA('sp', xf[u][:], xs[TH + t * 128:TH + (t + 1) * 128, :], [], [('xf', u)], 'c_xf%d' % u)
                DMA('pool', xbb[u][:], xs[TH + t * 128:TH + (t + 1) * 128, :], [], [('xbb', u)], 'c_xbb%d' % u)
                DMA('sp', odf[u][:], odn_d[tsl, :], [('odn_d', t)], [('odf', u)], 'c_odt%d' % u)
                DMA('sp', ymt[u][:], ymla_d[tsl, :], [('ymla_d', t)], [('ymt', u)], 'c_ymt%d' % u)

            def c1_A(t):
                u = t % 2
                for kc in range(8):
                    TR(pb[0][:, kc * 128:(kc + 1) * 128], xbb[u][:, kc * 128:(kc + 1) * 128], ident_b[:], [('xbb', u), 'ident_b'], [('pb', 0)])
                CP('act', xTc[u][:].rearrange("p a b -> p (a b)"), pb[0][:, 0:1024], [('pb', 0)], [('xTc', u)])
                for h in range(8):
                    ACT(junkc[:], odf[u][:, h * 128:(h + 1) * 128], AF.Square, [('odf', u)], ['junkc', ('orms', u)], accum=orms[u][:, h:h + 1])
                ACT(orms[u][:], orms[u][:], AF.Ln, [('orms', u), 'epscol'], [('orms', u)], bias=epscol[:, 0:1], scale=1.0 / 128)
                ACT(orms[u][:], orms[u][:], AF.Exp, [('orms', u)], [('orms', u)], scale=-0.5)
                for h in range(8):
                    STT('dve', odf[u][:, h * 128:(h + 1) * 128], odf[u][:, h * 128:(h + 1) * 128], orms[u][:, h:h + 1], dnw[:], ALU.mult, ALU.mult, [('odf', u), ('orms', u), 'dnw'], [('odf', u)])
                for n in range(4):
                    for kc in range(8):
                        MM(pf[n][:], xTc[u][:, kc, :], wg[:, kc, n * 512:(n + 1) * 512], kc == 0, kc == 7, [('xTc', u), ('wg', n)], [('pf', n)])
                    sgn = sg[u][:, n * 512:(n + 1) * 512]
                    ACT(sgn, pf[n][:], AF.Exp, [('pf', n)], [('sg', u, n)], scale=-1.0)
                    ACT(sgn, sgn, AF.Ln, [('sg', u, n), 'onecol'], [('sg', u, n)], bias=onecol[:, 0:1])
                    ACT(sgn, sgn, AF.Exp, [('sg', u, n)], [('sg', u, n)], scale=-1.0)
                for n in range(2):
                    for kc in range(8):
                        MM(pf[4 + n][:], xTc[u][:, kc, :], wz[:, kc, n * 512:(n + 1) * 512], kc == 0, kc == 7, [('xTc', u), ('wz', n)], [('pf', 4 + n)])
                    zn = zs[u][:, n * 512:(n + 1) * 512]
                    ACT(zn, pf[4 + n][:], AF.Exp, [('pf', 4 + n)], [('zs', u, n)], scale=-1.0)
                    ACT(zn, zn, AF.Ln, [('zs', u, n), 'onecol'], [('zs', u, n)], bias=onecol[:, 0:1])
                    ACT(zn, zn, AF.Exp, [('zs', u, n)], [('zs', u, n)], scale=-1.0)
                    TT('dve', zn, zn, pf[4 + n][:], ALU.mult, [('zs', u, n), ('pf', 4 + n)], [('zs', u, n)])
                    TT('dve', odt[u][:, n * 512:(n + 1) * 512], odf[u][:, n * 512:(n + 1) * 512], zn, ALU.mult, [('odf', u), ('zs', u, n)], [('odt', u)])
                for kc in range(8):
                    TR(pb[1][:, kc * 128:(kc + 1) * 128], odt[u][:, kc * 128:(kc + 1) * 128], ident_b[:], [('odt', u), 'ident_b'], [('pb', 1)])
                CP('act', odT[u][:].rearrange("p a b -> p (a b)"), pb[1][:, 0:1024], [('pb', 1)], [('odT', u)])
                TT('dve', ymt[u][:], ymt[u][:], sg[u][:, 1024:2048], ALU.mult, [('ymt', u), ('sg', u, 2), ('sg', u, 3)], [('ymt', u)])
                for n in range(2):
                    for kc in range(8):
                        MM(pf[4 + n][:], odT[u][:, kc, :], wbd[:, kc, n * 512:(n + 1) * 512], kc == 0, kc == 7, [('odT', u), ('wbd', n)], [('pf', 4 + n)])
                    TT('dve', mix[u][:, n * 512:(n + 1) * 512], pf[4 + n][:], sg[u][:, n * 512:(n + 1) * 512], ALU.mult, [('pf', 4 + n), ('sg', u, n)], [('mix', u)])
                TT('dve', mixb[u][:], mix[u][:], ymt[u][:], ALU.add, [('mix', u), ('ymt', u)], [('mixb', u)])

            def c1_B(t):
                u = t % 2
                tsl = slice(t * 128, (t + 1) * 128)
                for kc in range(8):
                    TR(pb[0][:, kc * 128:(kc + 1) * 128], mixb[u][:, kc * 128:(kc + 1) * 128], ident_b[:], [('mixb', u), 'ident_b'], [('pb', 0)])
                CP('act', mixT[u][:].rearrange("p a b -> p (a b)"), pb[0][:, 0:1024], [('pb', 0)], [('mixT', u)])
                for n in range(2):
                    for kc in range(8):
                        MM(pf[n][:], mixT[u][:, kc, :], wo[:, kc, n * 512:(n + 1) * 512], kc == 0, kc == 7, [('mixT', u), ('wo', n)], [('pf', n)])
                    STT('dve', r1[u][:, n * 512:(n + 1) * 512], xf[u][:, n * 512:(n + 1) * 512], ALPHA, pf[n][:], ALU.mult, ALU.add, [('xf', u), ('pf', n)], [('r1', u)])
                layer_norm(r1[u][:], g1, b1, hout[u][:], stats[u], mv[u], rstd[u], ('r1', u), ('hout', u), 'g1', 'b1', 'c1%d' % u)
                DMA('sp', h1_d[tsl, :], hout[u][:], [('hout', u)], [('h1_d', t)], 'c_hout%d' % u)

            c1_loads(0)
            if NTH > 1:
                c1_loads(1)
            c1_A(0)
            for t in range(NTH):
                if t + 1 < NTH:
                    c1_A(t + 1)
                c1_B(t)
                if t + 2 < NTH:
                    c1_loads(t + 2)
            barrier([])

        with ExitStack() as ed:
            wf1 = load_w(ed, "wf1", w_ffn_in, 8, 5632, colblk=256, groups=[[j, 11 + j] for j in range(11)])
            wpg = load_w(ed, "wpg", w_ple_gate, 8, 1024)
            wpl = load_w(ed, "wpl", w_ple, 2, 1024)
            wf2 = load_w(ed, "wf2", w_ffn_out, 22, 1024)
            g2 = bcast_vec(ed, "g2", ln2_g, 1024)
            b2 = bcast_vec(ed, "b2", ln2_b, 1024)

            def two(name, shape, dt=F32):
                return [sbt(ed, "%s_%d" % (name, i), shape, dt) for i in range(2)]
            hf = two("hf", [128, 1024])
            hb2 = two("hb2", [128, 1024], BF16)
            hT = two("hT", [128, 8, 128], BF16)
            pt_ = [sbt(ed, "pt_", [128, 256], BF16)] * 2
            pT = [sbt(ed, "pT", [128, 2, 128], BF16)] * 2
            sgt = two("sgt", [128, 256])
            act = sbt(ed, "act", [128, 2816], BF16)
            actT = sbt(ed, "actT", [128, 22, 128], BF16)
            plg = sbt(ed, "plg", [128, 1024])
            yout = two("yout", [128, 1024])
            stats2 = sbt(ed, "stats2", [128, 2, 6])
            mv2 = sbt(ed, "mv2", [128, 2])
            rstd2 = sbt(ed, "rstd2", [128, 1])

            def c2_loads(t):
                u = t % 2
                tsl = slice(t * 128, (t + 1) * 128)
                DMA('sp', hf[u][:], h1_d[tsl, :], [('h1_d', t)], [('hf', u)], 'd_hf%d' % u)
                DMA('pool', hb2[u][:], h1_d[tsl, :], [('h1_d', t)], [('hb', u)], 'd_hb%d' % u)

            def c2_A(t):
                u = t % 2
                DMA('pool', pt_[u][:], pp[t * 128:(t + 1) * 128, :], [], ['pt_'], 'd_pt')
                for kc in range(8):
                    TR(pb[0][:, kc * 128:(kc + 1) * 128], hb2[u][:, kc * 128:(kc + 1) * 128], ident_b[:], [('hb', u), 'ident_b'], [('pb', 0)])
                CP('act', hT[u][:].rearrange("p a b -> p (a b)"), pb[0][:, 0:1024], [('pb', 0)], [('hT', u)])
                for kc in range(2):
                    TR(pb[1][:, kc * 128:(kc + 1) * 128], pt_[u][:, kc * 128:(kc + 1) * 128], ident_b[:], ['pt_', 'ident_b'], [('pb', 1)])
                CP('dve', pT[u][:].rearrange("p a b -> p (a b)"), pb[1][:, 0:256], [('pb', 1)], ['pT'])
                for j in range(11):
                    bk = j % 2
                    for kc in range(8):
                        MM(pf[bk][:, 0:256], hT[u][:, kc, :], wf1[:, kc, j * 256:(j + 1) * 256], kc == 0, kc == 7, [('hT', u), ('wf1', j)], [('pf', bk)])
                    for kc in range(8):
                        MM(pf[bk][:, 256:512], hT[u][:, kc, :], wf1[:, kc, 2816 + j * 256:2816 + (j + 1) * 256], kc == 0, kc == 7, [('hT', u), ('wf1', j)], [('pf', bk)])
                    s_ = sgt[bk]
                    ACT(s_[:], pf[bk][:, 0:256], AF.Exp, [('pf', bk)], [('sgt', bk)], scale=-1.0)
                    ACT(s_[:], s_[:], AF.Ln, [('sgt', bk), 'onecol'], [('sgt', bk)], bias=onecol[:, 0:1])
                    ACT(s_[:], s_[:], AF.Exp, [('sgt', bk)], [('sgt', bk)], scale=-1.0)
                    TT('dve', s_[:], s_[:], pf[bk][:, 0:256], ALU.mult, [('sgt', bk), ('pf', bk)], [('sgt', bk)])
                    TT('dve', act[:, j * 256:(j + 1) * 256], s_[:], pf[bk][:, 256:512], ALU.mult, [('sgt', bk), ('pf', bk)], [('act', j)])

            def c2_B1(t):
                u = t % 2
                for n in range(2):
                    for kc in range(8):
                        MM(pf[4][:], hT[u][:, kc, :], wpg[:, kc, n * 512:(n + 1) * 512], kc == 0, kc == 7, [('hT', u), ('wpg', n)], [('pf', 4)])
                    pn = plg[:, n * 512:(n + 1) * 512]
                    ACT(pn, pf[4][:], AF.Exp, [('pf', 4)], [('plg', n)], scale=-1.0)
                    ACT(pn, pn, AF.Ln, [('plg', n), 'onecol'], [('plg', n)], bias=onecol[:, 0:1])
                    ACT(pn, pn, AF.Exp, [('plg', n)], [('plg', n)], scale=-1.0)
                    for kc in range(2):
                        MM(pf[5][:], pT[u][:, kc, :], wpl[:, kc, n * 512:(n + 1) * 512], kc == 0, kc == 1, ['pT', ('wpl', n)], [('pf', 5)])
                    TT('dve', pn, pn, pf[5][:], ALU.mult, [('plg', n), ('pf', 5)], [('plg', n)])
                for j in range(22):
                    TR(pb[(j // 8) % 2][:, (j % 8) * 128:(j % 8 + 1) * 128], act[:, j * 128:(j + 1) * 128], ident_b[:], [('act', j // 2), 'ident_b'], [('pb', (j // 8) % 2)])
                    if j % 8 == 7 or j == 21:
                        j0 = (j // 8) * 8
                        n_ = j - j0 + 1
                        CP('act' if (j // 8) % 2 else 'dve', actT[:, j0:j0 + n_, :].rearrange("p a b -> p (a b)"), pb[(j // 8) % 2][:, 0:n_ * 128], [('pb', (j // 8) % 2)], [('actT', j // 8)])
                for n in range(2):
                    for j in range(22):
                        MM(pf[2 + n][:], actT[:, j, :], wf2[:, j, n * 512:(n + 1) * 512], j == 0, j == 21, [('actT', j // 8), ('wf2', n)], [('pf', 2 + n)])

            def c2_B2(t):
                u = t % 2
                tsl = slice(t * 128, (t + 1) * 128)
                yk = ('yout', u)
                for n in range(2):
                    STT('dve', yout[u][:, n * 512:(n + 1) * 512], hf[u][:, n * 512:(n + 1) * 512], ALPHA, pf[2 + n][:], ALU.mult, ALU.add, [('hf', u), ('pf', 2 + n)], [yk])
                TT('dve', yout[u][:], yout[u][:], plg[:], ALU.add, [yk, ('plg', 0), ('plg', 1)], [yk])
                layer_norm(yout[u][:], g2, b2, yout[u][:], stats2, mv2, rstd2, yk, yk, 'g2', 'b2', 'c2')
                DMA('sp', out[tsl, :], yout[u][:], [yk], [('out', t)], 'd_out%d' % u)

            c2_loads(0)
            if NTH > 1:
                c2_loads(1)
            c2_A(0)
            for t in range(NTH):
                c2_B1(t)
                if t + 1 < NTH:
                    c2_A(t + 1)
                c2_B2(t)
                if t + 2 < NTH:
                    c2_loads(t + 2)
        P.emit(es)
    return nc


_CACHE = {}


def _host_inputs(inputs, TH, b, half):
    x = np.asarray(inputs['x'], np.float32)
    S = 2 * TH
    own = x[b, half * TH:(half + 1) * TH]
    pre = x[b, 0:TH] if half == 1 else np.zeros((TH, 1024), np.float32)
    posa = np.asarray(inputs['positions'], np.int32)[b]
    pos_own = posa[half * TH:(half + 1) * TH]
    pos_pre = posa[0:TH] if half == 1 else np.zeros((TH,), np.int32)
    kb = np.zeros((128, 2 * (TH // 128)), np.float32)
    if half == 0:
        kb[:, :TH // 128] = NEG
    invf = (10000.0 ** (-np.arange(0, 64, 2, dtype=np.float32) / 64.0)).astype(np.float32) / np.float32(2 * np.pi)
    invf2 = np.broadcast_to(np.concatenate([invf, invf])[None, :], (128, 64)).astype(np.float32)
    m = {
        'xs': np.ascontiguousarray(np.concatenate([pre, own], 0)),
        'pp': np.ascontiguousarray(np.asarray(inputs['p'], np.float32)[0, b, half * TH:(half + 1) * TH]),
        'pos': np.ascontiguousarray(np.concatenate([pos_pre, pos_own]).reshape(-1, 128).T),
        'kbias': kb, 'invf': np.ascontiguousarray(invf2),
    }
    sq = lambda k: np.ascontiguousarray(np.asarray(inputs[k], np.float32)[0])
    m['w_in'] = sq('w_in')
    m['conv_w'] = sq('conv_w')
    m['dn_a_log'] = sq('dn_a_log')
    m['dn_dt_bias'] = sq('dn_dt_bias')
    m['dn_norm_w'] = sq('dn_norm_w')
    m['q_norm_w'] = sq('q_norm_w').reshape(3, 128)
    m['w_uq'] = sq('w_uq').reshape(384, 1536)
    m['kv_norm_w'] = sq('kv_norm_w')
    m['w_uk'] = sq('w_uk').reshape(256, 1024)
    m['w_uv'] = sq('w_uv').reshape(256, 1024)
    for k in ('w_br_dn', 'w_br_mla', 'w_o', 'ln1_g', 'ln1_b', 'w_ffn_in', 'w_ffn_out', 'w_ple', 'w_ple_gate', 'ln2_g', 'ln2_b'):
        m[k] = sq(k)
    return m


def kernel(**inputs):
    x = np.asarray(inputs['x'])
    B, S, _ = x.shape
    TH = S // 2
    if TH not in _CACHE:
        _CACHE[TH] = build(TH)
    nc = _CACHE[TH]
    in_maps = []
    cores = []
    for b in range(B):
        for half in range(2):
            in_maps.append(_host_inputs(inputs, TH, b, half))
            cores.append((b, half))
    res = run_bass_kernel_spmd(nc, in_maps, core_ids=list(range(len(in_maps))))
    out = np.zeros((B, S, 1024), np.float32)
    for i, (b, half) in enumerate(cores):
        out[b, half * TH:(half + 1) * TH] = res.results[i]['out']
    return out
```

```python
import numpy as np
from contextlib import ExitStack
import concourse.bass as bass
import concourse.mybir as mybir
from concourse.bass_utils import run_bass_kernel_spmd

F32 = mybir.dt.float32
BF16 = mybir.dt.bfloat16
I32 = mybir.dt.int32
AF = mybir.ActivationFunctionType
ALU = mybir.AluOpType
NEG = -30000.0
ALPHA = 2.0 ** 0.25


class Prog:
    ENG = ('pe', 'act', 'dve', 'pool', 'sp')

    def __init__(self, nc):
        self.nc = nc
        self.ops = []
        self.last_w = {}
        self.readers = {}
        self.dma_cnt = {}

    def op(self, eng, fn, reads=(), writes=(), dma_slot=None):
        if getattr(self, 'defer', None) is not None:
            self.defer.append((eng, fn, list(reads), list(writes), dma_slot))
            return None
        o = dict(eng=eng, fn=fn, deps=set(), idx=len(self.ops), need_inc=False, dma_slot=dma_slot)
        al = getattr(self, 'alias', {})
        if al:
            reads = [x for k in reads for x in al.get(k, [k])]
        isps = lambda k: isinstance(k, tuple) and k[0] in ('pf', 'pb')
        writes = [k[:2] if isps(k) else k for k in writes] + [k[:2] for k in reads if isps(k)]
        reads = [k for k in reads if not isps(k)]
        writes = list(dict.fromkeys(writes))
        for r in reads:
            w = self.last_w.get(r)
            if w is not None:
                o['deps'].add(w)
        for w_ in writes:
            w = self.last_w.get(w_)
            if w is not None:
                o['deps'].add(w)
            for rd in self.readers.get(w_, ()):
                o['deps'].add(rd)
        for r in reads:
            self.readers.setdefault(r, []).append(o['idx'])
        for w_ in writes:
            self.last_w[w_] = o['idx']
            self.readers[w_] = []
        o['deps'].discard(o['idx'])
        if dma_slot is not None:
            self.dma_cnt[dma_slot] = self.dma_cnt.get(dma_slot, 0) + 1
            o['dma_val'] = 16 * self.dma_cnt[dma_slot]
        self.ops.append(o)
        return o

    def deferred(self, thunk):
        self.defer = []
        thunk()
        rec, self.defer = self.defer, None
        hops, cur = [], None
        for r in rec:
            if r[0] != cur:
                hops.append([])
                cur = r[0]
            hops[-1].append(r)

        def mk(hop):
            def run():
                for (eng, fn, rd, wr, slot) in hop:
                    self.op(eng, fn, rd, wr, slot)
            return run
        return [mk(h) for h in hops]

    def emit(self, es):
        nc = self.nc
        ops = self.ops
        for o in ops:
            for d in o['deps']:
                a = ops[d]
                if a['dma_slot'] is None:
                    if a['eng'] == 'pe' and o['eng'] == 'pe':
                        continue
                    a['need_inc'] = True
        SEG = 30000
        cnt = {e: 0 for e in self.ENG}
        for o in ops:
            if o['dma_slot'] is None and o['need_inc']:
                c = cnt[o['eng']]
                o['sem'] = (o['eng'], c // SEG)
                o['val'] = c % SEG + 1
                cnt[o['eng']] = c + 1
        sems = {}

        def getsem(key):
            if key not in sems:
                sems[key] = es.enter_context(nc.semaphore("s_" + "_".join(str(k) for k in key)))
            return sems[key]
        waited = {e: {} for e in self.ENG}
        for o in ops:
            need = {}
            for d in o['deps']:
                a = ops[d]
                if a['dma_slot'] is not None:
                    key = ('dma', a['dma_slot'])
                    v = a['dma_val']
                else:
                    if a['eng'] == 'pe' and o['eng'] == 'pe':
                        continue
                    key = a['sem']
                    v = a['val']
                if need.get(key, 0) < v:
                    need[key] = v
            w = []
            for key, v in need.items():
                if waited[o['eng']].get(key, 0) >= v:
                    continue
                waited[o['eng']][key] = v
                w.append((getsem(key), v))
            o['waits'] = w
            if o['dma_slot'] is not None:
                o['incsem'] = getsem(('dma', o['dma_slot']))
            elif o['need_inc']:
                o['incsem'] = getsem(o['sem'])
        final = [(getsem(('dma', k)), 16 * v) for k, v in self.dma_cnt.items()]
        per = {e: [o for o in ops if o['eng'] == e] for e in self.ENG}

        def run(eng_obj, lst, fin=False):
            for o in lst:
                for (s, v) in o['waits']:
                    eng_obj.wait_ge(s, v)
                ins = o['fn'](eng_obj)
                if o['dma_slot'] is not None:
                    ins.then_inc(o['incsem'], 16)
                elif o['need_inc']:
                    ins.then_inc(o['incsem'], 1)
            if fin:
                for (s, v) in final:
                    eng_obj.wait_ge(s, v)
        with nc.Block() as block:
            @block.tensor
            def _(e):
                run(e, per['pe'])

            @block.scalar
            def _(e):
                run(e, per['act'])

            @block.vector
            def _(e):
                run(e, per['dve'])

            @block.gpsimd
            def _(e):
                run(e, per['pool'])

            @block.sync
            def _(e):
                run(e, per['sp'], fin=True)


def build(TH, dbg=False):
    NTH = TH // 128
    NT = 2 * NTH
    nc = bass.Bass("TRN2", target_bir_lowering=False)

    def di(n, s, dt=F32):
        return nc.dram_tensor(n, s, dt, kind="ExternalInput").ap()
    xs = di("xs", [2 * TH, 1024])
    pp = di("pp", [TH, 256])
    pos = di("pos", [128, NT], I32)
    kbias_d = di("kbias", [128, NT])
    invf_d = di("invf", [128, 64])
    w_in = di("w_in", [1024, 6864])
    conv_w = di("conv_w", [4, 3072])
    a_log = di("dn_a_log", [8])
    dt_bias = di("dn_dt_bias", [8])
    dn_norm_w = di("dn_norm_w", [128])
    q_norm_w = di("q_norm_w", [3, 128])
    w_uq = di("w_uq", [384, 1536])
    kv_norm_w = di("kv_norm_w", [256])
    w_uk = di("w_uk", [256, 1024])
    w_uv = di("w_uv", [256, 1024])
    w_br_dn = di("w_br_dn", [1024, 1024])
    w_br_mla = di("w_br_mla", [1024, 1024])
    w_o = di("w_o", [1024, 1024])
    ln1_g = di("ln1_g", [1024])
    ln1_b = di("ln1_b", [1024])
    w_ffn_in = di("w_ffn_in", [1024, 5632])
    w_ffn_out = di("w_ffn_out", [2816, 1024])
    w_ple = di("w_ple", [256, 1024])
    w_ple_gate = di("w_ple_gate", [1024, 1024])
    ln2_g = di("ln2_g", [1024])
    ln2_b = di("ln2_b", [1024])
    out = nc.dram_tensor("out", [TH, 1024], F32, kind="ExternalOutput").ap()
    odn_d = nc.dram_tensor("odn_s", [TH, 1024], F32).ap()
    ymla_d = nc.dram_tensor("ymla_s", [TH, 1024], F32).ap()
    h1_d = nc.dram_tensor("h1_s", [TH, 1024], F32).ap()
    cq_d = nc.dram_tensor("cq_s", [TH // 128, 128, 384], BF16).ap()
    qn_d = nc.dram_tensor("qn_s", [8, 128, TH], BF16).ap()
    kn_d = nc.dram_tensor("kn_s", [8, 128, 2 * TH], BF16).ap()
    vn_d = nc.dram_tensor("vn_s", [8, 128, 2 * TH], BF16).ap()

    es = ExitStack()
    with es:
        P = Prog(nc)
        uid = [0]

        def sbt(es_, name, shape, dt=F32):
            return es_.enter_context(nc.sbuf_tensor(name, shape, dt))

        pf = [es.enter_context(nc.psum_tensor("pf%d" % i, [128, 512], F32)) for i in range(6)]
        pb = [es.enter_context(nc.psum_tensor("pb%d" % i, [128, 1024], BF16)) for i in range(2)]

        def MM(o, lhsT, rhs, start, stop, r, w):
            P.op('pe', lambda e: e.matmul(o, lhsT=lhsT, rhs=rhs, start=start, stop=stop), reads=r, writes=w)

        def TR(o, i, ident, r, w):
            P.op('pe', lambda e: e.transpose(out=o, in_=i, identity=ident), reads=r, writes=w)

        def ACT(o, i, func, r, w, bias=None, scale=None, accum=None):
            kw = {}
            if bias is not None:
                kw['bias'] = bias
            if scale is not None:
                kw['scale'] = scale
            if accum is not None:
                kw['accum_out'] = accum
            P.op('act', lambda e: e.activation(out=o, in_=i, func=func, **kw), reads=r, writes=w)

        def TT(eng, o, a, b, op, r, w):
            P.op(eng, lambda e: e.tensor_tensor(out=o, in0=a, in1=b, op=op), reads=r, writes=w)

        def TS(eng, o, a, s1, op0, r, w, s2=None, op1=None):
            if op1 is None:
                P.op(eng, lambda e: e.tensor_scalar(out=o, in0=a, scalar1=s1, scalar2=None, op0=op0), reads=r, writes=w)
            else:
                P.op(eng, lambda e: e.tensor_scalar(out=o, in0=a, scalar1=s1, scalar2=s2, op0=op0, op1=op1), reads=r, writes=w)

        def STT(eng, o, a, s, b, op0, op1, r, w):
            P.op(eng, lambda e: e.scalar_tensor_tensor(out=o, in0=a, scalar=s, in1=b, op0=op0, op1=op1), reads=r, writes=w)

        def CP(eng, o, i, r, w):
            if eng == 'act':
                P.op('act', lambda e: e.copy(out=o, in_=i), reads=r, writes=w)
            else:
                P.op(eng, lambda e: e.tensor_copy(out=o, in_=i), reads=r, writes=w)

        def DMA(eng, o, i, r, w, slot):
            P.op(eng, lambda e: e.dma_start(out=o, in_=i), reads=r, writes=w, dma_slot=slot)

        def MEMSET(eng, o, v, w):
            P.op(eng, lambda e: e.memset(o, v), writes=w)

        def AFSEL(o, pattern, cmp, fill, cm, key):
            P.op('pool', lambda e: e.affine_select(out=o, in_=o, pattern=pattern, compare_op=cmp, fill=fill, base=0, channel_multiplier=cm), reads=[key], writes=[key])

        ident_f = sbt(es, "ident_f", [128, 128])
        ident_b = sbt(es, "ident_b", [128, 128], BF16)
        ones_f = sbt(es, "ones_f", [128, 128])
        ones_b = sbt(es, "ones_b", [128, 128], BF16)
        triU_f = sbt(es, "triU_f", [128, 128])
        caus_b = sbt(es, "caus_b", [128, 128], BF16)
        maskLs = sbt(es, "maskLs", [128, 128])
        maskU = sbt(es, "maskU", [128, 128])
        onecol = sbt(es, "onecol", [128, 1])
        epscol = sbt(es, "epscol", [128, 4])
        MEMSET('pool', ident_f[:], 0.0, ['ident_f'])
        AFSEL(ident_f[:], [[-1, 128]], ALU.not_equal, 1.0, 1, 'ident_f')
        CP('pool', ident_b[:], ident_f[:], ['ident_f'], ['ident_b'])
        MEMSET('pool', ones_f[:], 1.0, ['ones_f'])
        MEMSET('pool', ones_b[:], 1.0, ['ones_b'])
        MEMSET('pool', onecol[:], 1.0, ['onecol'])
        MEMSET('pool', epscol[:, 0:1], 1e-6, ['epscol'])
        MEMSET('pool', epscol[:, 1:2], 1e-5, ['epscol'])
        MEMSET('pool', triU_f[:], 1.0, ['triU_f'])
        AFSEL(triU_f[:], [[1, 128]], ALU.is_ge, 0.0, -1, 'triU_f')
        CP('pool', caus_b[:], triU_f[:], ['triU_f'], ['caus_b'])
        MEMSET('pool', maskLs[:], 0.0, ['maskLs'])
        AFSEL(maskLs[:], [[-1, 128]], ALU.is_gt, NEG, 1, 'maskLs')
        MEMSET('pool', maskU[:], 0.0, ['maskU'])
        AFSEL(maskU[:], [[1, 128]], ALU.is_ge, NEG, -1, 'maskU')

        negA = sbt(es, "negA", [128, 8])
        dtb = sbt(es, "dtb", [128, 8])
        dnw = sbt(es, "dnw", [128, 128])
        kvw = sbt(es, "kvw", [128, 256])
        invf = sbt(es, "invf_s", [128, 64])
        kbias = sbt(es, "kbias_s", [128, NT])
        DMA('sp', negA[:], a_log.partition_broadcast(128), [], ['negA'], 'c_negA')
        DMA('sp', dtb[:], dt_bias.partition_broadcast(128), [], ['dtb'], 'c_dtb')
        DMA('sp', dnw[:], dn_norm_w.partition_broadcast(128), [], ['dnw'], 'c_dnw')
        DMA('sp', kvw[:], kv_norm_w.partition_broadcast(128), [], ['kvw'], 'c_kvw')
        DMA('sp', invf[:], invf_d, [], ['invf'], 'c_invf')
        DMA('sp', kbias[:], kbias_d, [], ['kbias'], 'c_kbias')
        ACT(negA[:], negA[:], AF.Exp, ['negA'], ['negA'])
        TS('dve', negA[:], negA[:], -1.0, ALU.mult, ['negA'], ['negA'])

        cw = sbt(es, "cw", [128, 24, 4])
        qnw = sbt(es, "qnw", [128, 3])
        bar_t = sbt(es, "bar_t", [128, 8])
        eAB = ExitStack()
        ckvT = sbt(eAB, "ckvT", [128, 2, NT * 128], BF16)
        krT = sbt(eAB, "krT", [64, NT * 128], BF16)
        ckvk = sbt(eAB, "ckvk", [128, NT, 256], BF16)
        cs_all = sbt(eAB, "cs_all", [128, NT, 64])
        with ExitStack() as e0:
            cwr = sbt(e0, "cwr", [4, 3072])
            qnr = sbt(e0, "qnr", [3, 128])
            DMA('sp', cwr[:], conv_w, [], ['cwr'], 'c_cwr')
            DMA('sp', qnr[:], q_norm_w, [], ['qnr'], 'c_qnr')
            for blk in range(24):
                TR(pf[0][:, blk * 4:(blk + 1) * 4], cwr[0:4, blk * 128:(blk + 1) * 128], ident_f[0:4, 0:4], ['cwr', 'ident_f'], [('pf', 0)])
            CP('dve', cw[:].rearrange("p b i -> p (b i)"), pf[0][:, 0:96], [('pf', 0)], ['cw'])
            TR(pf[1][:, 0:3], qnr[0:3, :], ident_f[0:3, 0:3], ['qnr', 'ident_f'], [('pf', 1)])
            CP('dve', qnw[:], pf[1][:, 0:3], [('pf', 1)], ['qnw'])

        keysA = ['cwr', 'qnr', 'wuq_s', 'wuk_s', 'wuv_s', 'wbm_s', 'AT', 'BT', 'VT']

        def barrier(extra_reads):
            allk = list(set(list(P.last_w.keys()) + list(P.readers.keys())))
            P.op('pool', lambda e: e.memset(bar_t[:, 0:1], 1.0), reads=allk, writes=allk + ['BAR'])
            P.op('dve', lambda e: e.memset(bar_t[:, 1:2], 1.0), reads=['BAR'], writes=['BAR_dve'])
            P.op('act', lambda e: e.copy(out=bar_t[:, 2:3], in_=bar_t[:, 0:1]), reads=['BAR'], writes=['BAR_act'])
            P.op('pe', lambda e: e.transpose(out=pf[0][0:4, 0:4], in_=ident_f[0:4, 0:4], identity=ident_f[0:4, 0:4]), reads=['BAR'], writes=[('pf', 0)])
            P.op('sp', lambda e: e.dma_start(out=bar_t[:, 4:8], in_=kbias_d[:, 0:4]), reads=['BAR'], writes=['BAR_sp'], dma_slot='bar_sp')

        barrier(keysA)

        NGRP = NT // 4
        GP = NTH // 4
        beta_a = sbt(eAB, "beta_a", [128, NT, 8])
        gcs_a = sbt(eAB, "gcs_a", [128, NT, 8])
        eg_a = sbt(eAB, "eg_a", [128, NT, 8])
        ekt_a = sbt(eAB, "ekt_a", [128, NT, 8])
        egl_a = sbt(eAB, "egl_a", [128, NT, 8])
        bkg_a = sbt(eAB, "bkg_a", [128, NT, 8])
        g8_a = sbt(eAB, "g8_a", [128, NT, 8])
        with ExitStack() as er:
            posi = sbt(er, "posi", [128, NT], I32)
            posf = sbt(er, "posf", [128, NT])
            ang = sbt(er, "ang", [128, NT, 64])
            angi = sbt(er, "angi", [128, NT, 64], I32)
            angf = sbt(er, "angf", [128, NT, 64])
            DMA('sp', posi[:], pos, [], ['posi'], 'posi')
            CP('dve', posf[:], posi[:], ['posi'], ['posf'])
            TT('dve', ang[:], invf[:, None, :].broadcast_to([128, NT, 64]), posf[:, :, None].broadcast_to([128, NT, 64]), ALU.mult, ['invf', 'posf'], ['ang'])
            TS('dve', ang[:, :, 32:64], ang[:, :, 32:64], 0.25, ALU.add, ['ang'], ['ang'])
            CP('dve', angi[:], ang[:], ['ang'], ['angi'])
            CP('dve', angf[:], angi[:], ['angi'], ['angf'])
            TT('dve', ang[:], ang[:], angf[:], ALU.subtract, ['ang', 'angf'], ['ang'])
            TS('dve', angf[:], ang[:], 0.5, ALU.is_gt, ['ang'], ['angf'])
            TT('dve', ang[:], ang[:], angf[:], ALU.subtract, ['ang', 'angf'], ['ang'])
            TS('dve', angf[:], ang[:], -0.5, ALU.is_lt, ['ang'], ['angf'])
            TT('dve', ang[:], ang[:], angf[:], ALU.add, ['ang', 'angf'], ['ang'])
            ACT(cs_all[:], ang[:], AF.Sin, ['ang'], ['cs_all'], scale=2 * np.pi)
        barrier([])
        with ExitStack() as ea:
            wA = sbt(ea, "wA", [128, 8, 3792], BF16)
            wA_grp = {c2: 0 for c2 in range(12)}
            for kc in range(8):
                DMA('pool', wA[:, kc, 3072:3792], w_in[kc * 128:(kc + 1) * 128, 4096:4816], [], [('wA', 'small')], 'wA_small')
            for kc in range(8):
                DMA('pool', wA[:, kc, 0:3072], w_in[kc * 128:(kc + 1) * 128, 0:3072], [], [('wA', 0)], 'wA_0')
            wAk = [('wA', 'small')]
            xb = [sbt(ea, "xb%d" % i, [128, 1024], BF16) for i in range(2)]
            xTg = [sbt(ea, "xTg%d" % i, [128, 8, 512], BF16) for i in range(2)]
            tmp8 = sbt(ea, "tmp8", [128, 4, 8])
            smallf = sbt(ea, "smallf", [128, 4, 8])
            ssq8 = [sbt(ea, "ssq8_%d" % i, [128, 8]) for i in range(2)]
            junk = sbt(ea, "junk", [128, 384])
            junk2 = [sbt(ea, "junk2_%d" % i, [128, 64]) for i in range(2)]
            cqb = [sbt(ea, "cqb%d" % i, [128, 384], BF16) for i in range(2)]
            krr = [sbt(ea, "krr%d" % i, [128, 64]) for i in range(2)]
            krb = [sbt(ea, "krb%d" % i, [128, 64], BF16) for i in range(2)]
            cqs = [sbt(ea, "cqs%d" % i, [128, 384], BF16) for i in range(2)]
            hist = sbt(ea, "hist", [128, 24, 3])
            D_CB, D_CT, D_SG, D_SL, D_SQ, D_LN, D_OB = 3, 4, 3, 6, 3, 3, 3
            cb = [sbt(ea, "cb%d" % i, [128, 515]) for i in range(D_CB)]
            ct = [sbt(ea, "ct%d" % i, [128, 512]) for i in range(D_CT)]
            sg = [sbt(ea, "sg%d" % i, [128, 512]) for i in range(D_SG)]
            sl = [sbt(ea, "sl%d" % i, [128, 512]) for i in range(D_SL)]
            sq = [sbt(ea, "sq%d" % i, [128, 512], BF16) for i in range(D_SQ)]
            lnr = [sbt(ea, "lnr%d" % i, [128, 512]) for i in range(D_LN)]
            ob = [sbt(ea, "ob%d" % i, [128, 512], BF16) for i in range(D_OB)]
            MEMSET('pool', hist[:], 0.0, ['hist'])
            bankc = [0]

            def nb():
                b = 2 + bankc[0] % 4
                bankc[0] += 1
                return b

            def x_prologue(g):
                Xg = xTg[g % 2]
                for j in range(4):
                    kt = g * 4 + j
                    s2 = kt % 2
                    DMA('pool', xb[s2][:], xs[kt * 128:(kt + 1) * 128, :], [], [('xb', s2)], 'xb%d' % s2)
                    for kc in range(8):
                        TR(pb[s2][:, kc * 128:(kc + 1) * 128], xb[s2][:, kc * 128:(kc + 1) * 128], ident_b[:], [('xb', s2), 'ident_b'], [('pb', s2)])
                    CP('act', Xg[:, :, j * 128:(j + 1) * 128], pb[s2][:, 0:1024].rearrange("p (a b) -> p a b", a=8), [('pb', s2)], [('xTg', g % 2)])

            def gates(g):
                Xg = xTg[g % 2]
                kXg = ('xTg', g % 2)
                t4 = slice(g * 4, g * 4 + 4)
                ka = ('gates', g)
                for j in range(4):
                    for kc in range(8):
                        MM(pf[0][:, j * 16:(j + 1) * 16], Xg[:, kc, j * 128:(j + 1) * 128], wA[:, kc, 3072:3088], kc == 0, kc == 7, [kXg] + wAk, [('pf', 0)])
                pba = pf[0][:, 0:64].rearrange("p (j c) -> p j c", j=4)
                ACT(beta_a[:, t4, :], pba[:, :, 0:8], AF.Sigmoid, [('pf', 0)], [ka])
                TT('dve', tmp8[:], pba[:, :, 8:16], dtb[:, None, :].broadcast_to([128, 4, 8]), ALU.add, [('pf', 0), 'dtb'], ['tmp8'])
                ACT(tmp8[:], tmp8[:], AF.Exp, ['tmp8'], ['tmp8'])
                ACT(tmp8[:], tmp8[:], AF.Ln, ['tmp8', 'onecol'], ['tmp8'], bias=onecol[:, 0:1])
                TT('dve', g8_a[:, t4, :], tmp8[:], negA[:, None, :].broadcast_to([128, 4, 8]), ALU.mult, ['tmp8', 'negA'], [ka])
                g32 = g8_a[:, t4, :].rearrange("p j c -> p (j c)")
                MM(pf[0][:, 64:96], triU_f[:], g32, True, True, ['triU_f', ka], [('pf', 0)])
                MM(pf[0][:, 96:128], ones_f[:], g32, True, True, ['ones_f', ka], [('pf', 0)])
                pgc = pf[0][:, 64:96].rearrange("p (j c) -> p j c", j=4)
                pgl = pf[0][:, 96:128].rearrange("p (j c) -> p j c", j=4)
                CP('dve', gcs_a[:, t4, :], pgc, [('pf', 0)], [ka])
                ACT(eg_a[:, t4, :], pgc, AF.Exp, [('pf', 0)], [ka])
                ACT(egl_a[:, t4, :], pgl, AF.Exp, [('pf', 0)], [ka])
                TT('dve', smallf[:], pgl, gcs_a[:, t4, :], ALU.subtract, [('pf', 0), ka], ['smallf'])
                ACT(ekt_a[:, t4, :], smallf[:], AF.Exp, ['smallf'], [ka])
                TT('dve', bkg_a[:, t4, :], beta_a[:, t4, :], eg_a[:, t4, :], ALU.mult, [ka], [ka])

            def keys_tile(kt):
                g, j = kt // 4, kt % 4
                Xg = xTg[g % 2]
                kXg = ('xTg', g % 2)
                own = kt >= NTH
                ot = kt - NTH
                s2 = kt % 2
                tsl_ = slice(j * 128, (j + 1) * 128)
                cs = cs_all[:, kt, :]
                for kc in range(8):
                    MM(pf[1][:, 0:320], Xg[:, kc, tsl_], wA[:, kc, 3472:3792], kc == 0, kc == 7, [kXg] + wAk, [('pf', 1)])
                sq_ = ssq8[s2]
                ks = ('ssq8', s2)
                ACT(junk[:, 0:256], pf[1][:, 0:256], AF.Square, [('pf', 1)], ['junk', ks], accum=sq_[:, 0:1])
                ACT(sq_[:, 1:2], sq_[:, 0:1], AF.Ln, [ks, 'epscol'], [ks], bias=epscol[:, 0:1], scale=1.0 / 256)
                ACT(sq_[:, 2:3], sq_[:, 1:2], AF.Exp, [ks], [ks], scale=-0.5)
                STT('dve', ckvk[:, kt, :], pf[1][:, 0:256], sq_[:, 2:3], kvw[:], ALU.mult, ALU.mult, [('pf', 1), ks, 'kvw'], [('ckvk', kt)])
                kr_, j2_ = krr[s2], junk2[s2]
                TT('dve', kr_[:, 0:32], pf[1][:, 256:288], cs[:, 32:64], ALU.mult, [('pf', 1), 'cs_all'], [('krr', s2)])
                TT('dve', kr_[:, 32:64], pf[1][:, 288:320], cs[:, 32:64], ALU.mult, [('pf', 1), 'cs_all'], [('krr', s2)])
                TT('dve', j2_[:, 0:32], pf[1][:, 288:320], cs[:, 0:32], ALU.mult, [('pf', 1), 'cs_all'], [('junk2', s2)])
                TT('dve', j2_[:, 32:64], pf[1][:, 256:288], cs[:, 0:32], ALU.mult, [('pf', 1), 'cs_all'], [('junk2', s2)])
                TT('pool', kr_[:, 0:32], kr_[:, 0:32], j2_[:, 0:32], ALU.subtract, [('krr', s2), ('junk2', s2)], [('krr', s2)])
                TT('pool', kr_[:, 32:64], kr_[:, 32:64], j2_[:, 32:64], ALU.add, [('krr', s2), ('junk2', s2)], [('krr', s2)])
                CP('pool', krb[s2][:], kr_[:], [('krr', s2)], [('krb', s2)])
                for c in range(2):
                    TR(pb[s2][:, c * 128:(c + 1) * 128], ckvk[:, kt, c * 128:(c + 1) * 128], ident_b[:], [('ckvk', kt), 'ident_b'], [('pb', s2)])
                TR(pb[s2][0:64, 256:384], krb[s2][:], ident_b[:], [('krb', s2), 'ident_b'], [('pb', s2)])
                if own:
                    for kc in range(8):
                        MM(pf[0][:, 128:512], Xg[:, kc, tsl_], wA[:, kc, 3088:3472], kc == 0, kc == 7, [kXg] + wAk, [('pf', 0)])
                    ACT(junk[:, 0:384], pf[0][:, 128:512], AF.Square, [('pf', 0)], ['junk', ks], accum=sq_[:, 3:4])
                    ACT(sq_[:, 4:5], sq_[:, 3:4], AF.Ln, [ks, 'epscol'], [ks], bias=epscol[:, 0:1], scale=1.0 / 384)
                    ACT(sq_[:, 5:6], sq_[:, 4:5], AF.Exp, [ks], [ks], scale=-0.5)
                    TS('dve', cqb[s2][:], pf[0][:, 128:512], sq_[:, 5:6], ALU.mult, [('pf', 0), ks], [('cqb', s2)])
                    for c in range(3):
                        TR(pb[s2][:, 384 + c * 128:384 + (c + 1) * 128], cqb[s2][:, c * 128:(c + 1) * 128], ident_b[:], [('cqb', s2), 'ident_b'], [('pb', s2)])
                CP('act', ckvT[:, :, kt * 128:(kt + 1) * 128], pb[s2][:, 0:256].rearrange("p (c t) -> p c t", c=2), [('pb', s2)], [('ckvT', kt)])
                CP('act', krT[:, kt * 128:(kt + 1) * 128], pb[s2][0:64, 256:384], [('pb', s2)], [('krT', kt)])
                if own:
                    CP('act', cqs[s2][:], pb[s2][:, 384:768], [('pb', s2)], [('cqs', s2)])
                    DMA('sp', cq_d[ot], cqs[s2][:], [('cqs', s2)], [('cq_d', ot)], 'cqs%d' % s2)

            blocks = []
            for g in range(NGRP):
                own_g = g >= GP
                k_in_g = 0
                for h in range(8):
                    for kind in ('q', 'k', 'v'):
                        hist_only = False
                        if kind == 'q' and not own_g:
                            if g != GP - 1:
                                continue
                            hist_only = True
                        blocks.append(dict(g=g, h=h, kind=kind, hist_only=hist_only, blk={'q': 0, 'k': 8, 'v': 16}[kind] + h, idx=len(blocks), k_in_g=k_in_g))
                        k_in_g += 1

            def stage(B, s):
                g, h, kind, blk = B['g'], B['h'], B['kind'], B['blk']
                i_ = B['idx']
                rcb, rct, rsg, rsl, rsq, rln, rob = i_ % D_CB, i_ % D_CT, i_ % D_SG, i_ % D_SL, i_ % D_SQ, i_ % D_LN, i_ % D_OB
                Xg = xTg[g % 2]
                kXg = ('xTg', g % 2)
                gsl = slice(g * 512, (g + 1) * 512)
                bP = 2 + i_ % 2
                bO = 4 + i_ % 2
                if s == 0:
                    if B['k_in_g'] == 0:
                        hs_ = P.deferred(lambda: gates(g))
                        pending.extend(hs_)
                        pending_grp.extend([g] * len(hs_))
                    if B['k_in_g'] in (2, 5, 8, 11):
                        kt_ = g * 4 + (B['k_in_g'] - 2) // 3
                        hs_ = P.deferred(lambda: keys_tile(kt_))
                        pending.extend(hs_)
                        pending_grp.extend([g] * len(hs_))
                    if B['k_in_g'] == 13 and g + 1 < NGRP:
                        while pending:
                            pending.pop(0)()
                            pending_grp.pop(0)
                        x_prologue(g + 1)
                    for _ in range(3):
                        if pending:
                            pending.pop(0)()
                            pending_grp.pop(0)
                    for kc in range(8):
                        MM(pf[bP][:], wA[:, kc, blk * 128:(blk + 1) * 128], Xg[:, kc, :], kc == 0, kc == 7, [kXg, ('wA', wA_grp[blk // 2])], [('pf', bP)])
                    return
                if s == 1:
                    CP('act', cb[rcb][:, 3:515], pf[bP][:], [('pf', bP)], [('cb', rcb)])
                    CP('pool', cb[rcb][:, 0:3], hist[:, blk, :], [('hist', blk)], [('cb', rcb)])
                    CP('pool', hist[:, blk, :], cb[rcb][:, 512:515], [('cb', rcb)], [('hist', blk)])
                    return
                if B['hist_only']:
                    return
                if s == 2:
                    TS('dve', ct[rct][:], cb[rcb][:, 0:512], cw[:, blk, 0:1], ALU.mult, [('cb', rcb), 'cw'], [('ct', rct)])
                    for i in range(1, 4):
                        STT('dve', ct[rct][:], cb[rcb][:, i:i + 512], cw[:, blk, i:i + 1], ct[rct][:], ALU.mult, ALU.add, [('cb', rcb), 'cw', ('ct', rct)], [('ct', rct)])
                elif s == 3:
                    ACT(sg[rsg][:], ct[rct][:], AF.Exp, [('ct', rct)], [('sg', rsg)], scale=-1.0)
                    ACT(sg[rsg][:], sg[rsg][:], AF.Ln, [('sg', rsg), 'onecol'], [('sg', rsg)], bias=onecol[:, 0:1])
                    ACT(sg[rsg][:], sg[rsg][:], AF.Exp, [('sg', rsg)], [('sg', rsg)], scale=-1.0)
                elif s == 4:
                    if kind == 'v':
                        TT('dve', ob[rob][:], ct[rct][:], sg[rsg][:], ALU.mult, [('ct', rct), ('sg', rsg)], [('ob', rob)])
                        DMA('sp', vn_d[h, :, gsl], ob[rob][:], [('ob', rob)], [('vn_d', g)], 'ob%d' % rob)
                        return
                    TT('dve', sl[rsl][:], ct[rct][:], sg[rsg][:], ALU.mult, [('ct', rct), ('sg', rsg)], [('sl', rsl)])
                elif kind == 'v':
                    return
                elif s == 5:
                    TT('pool', sq[rsq][:], sl[rsl][:], sl[rsl][:], ALU.mult, [('sl', rsl)], [('sq', rsq)])
                elif s == 6:
                    MM(pf[bO][:], ones_b[:], sq[rsq][:], True, True, ['ones_b', ('sq', rsq)], [('pf', bO)])
                elif s == 7:
                    ACT(lnr[rln][:], pf[bO][:], AF.Ln, [('pf', bO), 'epscol'], [('lnr', rln)], bias=epscol[:, 0:1])
                    ACT(lnr[rln][:], lnr[rln][:], AF.Exp, [('lnr', rln)], [('lnr', rln)], scale=-0.5)
                elif s == 8:
                    if kind == 'q':
                        STT('dve', ob[rob][:], sl[rsl][:], 128.0 ** -0.5, lnr[rln][:], ALU.mult, ALU.mult, [('sl', rsl), ('lnr', rln)], [('ob', rob)])
                        DMA('sp', qn_d[h, :, (g - GP) * 512:(g - GP + 1) * 512], ob[rob][:], [('ob', rob)], [('qn_d', g)], 'ob%d' % rob)
                    else:
                        TT('dve', ob[rob][:], sl[rsl][:], lnr[rln][:], ALU.mult, [('sl', rsl), ('lnr', rln)], [('ob', rob)])
                        DMA('sp', kn_d[h, :, gsl], ob[rob][:], [('ob', rob)], [('kn_d', g)], 'ob%d' % rob)

            pending = []
            pending_grp = []
            x_prologue(0)
            NS = 9
            for step in range(len(blocks) + NS - 1):
                for s in reversed(range(NS)):
                    i = step - s
                    if 0 <= i < len(blocks):
                        stage(blocks[i], s)
            while pending:
                pending.pop(0)()
            barrier([])

        with ExitStack() as ea:
            NSL = 2

            def per_slot(name, shape, dt=F32):
                return [sbt(ea, "%s_s%d" % (name, i), shape, dt) for i in range(NSL)]
            NIN = 4
            kin = [sbt(ea, "kin_s%d" % i, [128, 8, 128], BF16) for i in range(NIN)]
            vin = [sbt(ea, "vin_s%d" % i, [128, 8, 128], BF16) for i in range(NIN)]
            qin = [sbt(ea, "qin_s%d" % i, [128, 8, 128], BF16) for i in range(NIN)]
            rhsg = sbt(ea, "rhsg", [128, 8, 128])
            negones = sbt(ea, "negones", [128, 128])
            MEMSET('pool', negones[:], -1.0, ['negones'])
            Kbg = per_slot("Kbg", [128, 8, 128], BF16)
            Ktt = per_slot("Ktt", [128, 8, 128], BF16)
            Vb = per_slot("Vb", [128, 8, 128], BF16)
            Lf = [sbt(ea, "Lf%d" % i, [128, 4, 128]) for i in range(2)]
            LS = [[[sbt(ea, "LS%d_%d_%d" % (s_, hb_, c_), [128, 2, 4, 128], BF16) for c_ in range(2)] for hb_ in range(2)] for s_ in range(NSL)]
            MS = [[[sbt(ea, "MS%d_%d_%d" % (s_, hb_, c_), [128, 2, 4, 128], BF16) for c_ in range(2)] for hb_ in range(2)] for s_ in range(NSL)]
            YS = [[[sbt(ea, "YS%d_%d_%d" % (s_, hb_, c_), [128, 2, 4, 128], BF16) for c_ in range(2)] for hb_ in range(2)] for s_ in range(NSL)]
            YF = [[[sbt(ea, "YF%d_%d_%d" % (s_, hb_, c_), [128, 4, 128]) for c_ in range(2)] for hb_ in range(2)] for s_ in range(NSL)]
            decL = [sbt(ea, "decL%d" % i, [128, 4, 128]) for i in range(2)]
            decT = [sbt(ea, "decT%d" % i, [128, 4, 128]) for i in range(2)]
            ATb = per_slot("ATb", [128, 8, 128], BF16)
            Ub = per_slot("Ub", [128, 8, 128])
            WTb = per_slot("WTb", [128, 8, 128], BF16)
            Vn = sbt(ea, "Vn", [128, 8, 128], BF16)
            Sf = sbt(ea, "Sf", [128, 8, 128])
            Sb = sbt(ea, "Sb", [128, 8, 128], BF16)
            o1 = sbt(ea, "o1", [128, 4, 128])
            otile = [sbt(ea, "otile", [128, 8, 128])] * NSL
            MEMSET('pool', Sf[:], 0.0, ['Sf0', 'Sf1'])
            MEMSET('pool', Sb[:], 0.0, ['Sb0', 'Sb1'])
            bankc = [0]

            def nb6():
                b = bankc[0] % 6
                bankc[0] += 1
                return b, ('pf', b)

            def f4(t, hb):
                return t[:, hb * 4:(hb + 1) * 4, :]

            def p4(b):
                return pf[b][:].rearrange("p (h t) -> p h t", h=4)

            def bc4(ap2):
                return ap2[:, None, :].broadcast_to([128, 4, 128])

            def col4(arr, kt, hb):
                return arr[:, kt, hb * 4:(hb + 1) * 4, None].broadcast_to([128, 4, 128])

            def a2_loads(kt):
                own = kt >= NTH
                ot = kt - NTH
                sl = kt % NIN
                tok = slice(kt * 128, (kt + 1) * 128)
                DMA('sp', kin[sl][:], kn_d[:, :, tok].rearrange("h d t -> d h t"), [('kn_d', kt // 4)], [('kin', sl)], 'kin%d' % sl)
                DMA('sp', vin[sl][:], vn_d[:, :, tok].rearrange("h d t -> d h t"), [('vn_d', kt // 4)], [('vin', sl)], 'vin%d' % sl)
                if own:
                    DMA('sp', qin[sl][:], qn_d[:, :, ot * 128:(ot + 1) * 128].rearrange("h d t -> d h t"), [('qn_d', kt // 4)], [('qin', sl)], 'qin%d' % sl)

            def a2_prep(kt):
                own = kt >= NTH
                sl = kt % NSL
                ss = 's%d' % sl
                ka = ('gates', kt // 4)
                TT('pool', rhsg[:], triU_f[:, None, :].broadcast_to([128, 8, 128]), g8_a[:, kt, :, None].broadcast_to([128, 8, 128]), ALU.mult, ['triU_f', ka], ['rhsg'])
                hops = []
                for hb in range(2):
                    hops.append(a2_prep_hops(kt, hb))
                for hi in range(len(hops[0])):
                    for hp in hops:
                        hp[hi]()

            def a2_prep_hops(kt, hb):
                own = kt >= NTH
                sl = kt % NSL
                si = kt % NIN
                ss = 's%d' % sl
                ka = ('gates', kt // 4)
                kh = 'h%d' % hb + ss
                dL, dT, Lf_ = decL[hb], decT[hb], Lf[hb]
                kdL, kdT, kLf = ('decL', hb), ('decT', hb), ('Lf', hb)
                L0, M0, Y0, YF0 = LS[sl][hb][0], MS[sl][hb][0], YS[sl][hb][0], YF[sl][hb][0]
                st = {}
                pkb = ('pb', hb)

                def p1():
                    for i in range(4):
                        TR(pb[hb][:, i * 128:(i + 1) * 128], kin[si][:, hb * 4 + i, :], ident_b[:], [('kin', si), 'ident_b'], [pkb])
                    for i in range(4):
                        TR(pb[hb][:, 512 + i * 128:512 + (i + 1) * 128], vin[si][:, hb * 4 + i, :], ident_b[:], [('vin', si), 'ident_b'], [pkb])
                    st['bkk'], st['kkk'] = nb6()
                    for i in range(4):
                        h = hb * 4 + i
                        MM(pf[st['bkk']][:, i * 128:(i + 1) * 128], kin[si][:, h, :], kin[si][:, h, :], True, True, [('kin', si)], [st['kkk']])
                    st['bdf'], st['kdf'] = nb6()
                    for i in range(4):
                        h = hb * 4 + i
                        MM(pf[st['bdf']][:, i * 128:(i + 1) * 128], negones[:], rhsg[:, h, :], True, False, ['negones', 'rhsg'], [st['kdf']])
                        MM(pf[st['bdf']][:, i * 128:(i + 1) * 128], rhsg[:, h, :], ones_f[:], False, True, ['ones_f', 'rhsg'], [st['kdf']])

                def p2():
                    pk = pb[hb][:, 0:512].rearrange("p (h t) -> p h t", h=4)
                    pv_ = pb[hb][:, 512:1024].rearrange("p (h t) -> p h t", h=4)
                    TT('dve', dL[:], p4(st['bdf']), bc4(maskLs), ALU.add, [st['kdf'], 'maskLs'], [kdL])
                    if own:
                        TT('dve', dT[:], bc4(maskU), p4(st['bdf']), ALU.subtract, [st['kdf'], 'maskU'], [kdT])
                    TT('dve', f4(Kbg[sl], hb), pk, col4(bkg_a, kt, hb), ALU.mult, [pkb, ka], ['Kbg' + kh])
                    TT('pool' if False else 'dve', f4(Ktt[sl], hb), pk, col4(ekt_a, kt, hb), ALU.mult, [pkb, ka], ['Ktt' + kh])
                    TT('dve', f4(Vb[sl], hb), pv_, col4(beta_a, kt, hb), ALU.mult, [pkb, ka], ['Vb' + kh])

                def p3():
                    ACT(dL[:], dL[:], AF.Exp, [kdL], [kdL])
                    if own:
                        ACT(dT[:], dT[:], AF.Exp, [kdT], [kdT])

                def p4_():
                    TT('dve', dL[:], dL[:], col4(beta_a, kt, hb), ALU.mult, [kdL, ka], [kdL])
                    TT('dve', Lf_[:], p4(st['bkk']), dL[:], ALU.mult, [st['kkk'], kdL], [kLf])

                def p5():
                    CP('act', L0[:, 0], Lf_[:], [kLf], ['L0' + kh])

                def p6():
                    TT('dve', L0[:, 1], Lf_[:], L0[:, 0], ALU.subtract, [kLf, 'L0' + kh], ['L0' + kh])

                def p7():
                    for a_ in range(2):
                        for i in range(4):
                            TR(pb[hb][:, a_ * 512 + i * 128:a_ * 512 + (i + 1) * 128], L0[:, a_, i, :], ident_b[:], ['L0' + kh, 'ident_b'], [pkb])
                    if own:
                        st['bkq'], st['kkq'] = nb6()
                        for i in range(4):
                            h = hb * 4 + i
                            MM(pf[st['bkq']][:, i * 128:(i + 1) * 128], kin[si][:, h, :], qin[si][:, h, :], True, True, [('kin', si), ('qin', si)], [st['kkq']])

                def p8():
                    CP('act', M0[:].rearrange("p a h t -> p (a h t)"), pb[hb][:, 0:1024], [pkb], ['M0' + kh])
                    TT('dve', YF0[:], bc4(ident_f), pb[hb][:, 0:512].rearrange("p (h t) -> p h t", h=4), ALU.subtract, ['ident_f', pkb], ['YF0' + kh])
                    TT('dve', YF0[:], YF0[:], pb[hb][:, 512:1024].rearrange("p (h t) -> p h t", h=4), ALU.subtract, ['YF0' + kh, pkb], ['YF0' + kh])
                    if own:
                        TT('dve', f4(ATb[sl], hb), p4(st['bkq']), dT[:], ALU.mult, [st['kkq'], kdT], ['ATb' + kh])

                def p9():
                    CP('act', Y0[:, 0], YF0[:], ['YF0' + kh], ['Y0' + kh])

                def p10():
                    TT('pool', Y0[:, 1], YF0[:], Y0[:, 0], ALU.subtract, ['YF0' + kh, 'Y0' + kh], ['Y0' + kh])
                return [p1, p2, p3, p4_, p5, p6, p7, p8, p9, p10]

            def a2_level_hops(kt, lev, hb):
                sl = kt % NSL
                kh = 'h%d' % hb + 's%d' % sl
                cur, nxt = lev % 2, (lev + 1) % 2
                Lc, Mc, Yc, YFc = 'L%d' % cur + kh, 'M%d' % cur + kh, 'Y%d' % cur + kh, 'YF%d' % cur + kh
                Ln_, Mn_, Yn_, YFn = 'L%d' % nxt + kh, 'M%d' % nxt + kh, 'Y%d' % nxt + kh, 'YF%d' % nxt + kh
                Lc_t, Mc_t, Yc_t = LS[sl][hb][cur], MS[sl][hb][cur], YS[sl][hb][cur]
                Ln_t, Mn_t, Yn_t = LS[sl][hb][nxt], MS[sl][hb][nxt], YS[sl][hb][nxt]
                st = {}

                def h1():
                    st['bl'], st['kl'] = nb6()
                    for i in range(4):
                        o_ = pf[st['bl']][:, i * 128:(i + 1) * 128]
                        MM(o_, Mc_t[:, 0, i, :], Lc_t[:, 0, i, :], True, False, [Mc, Lc], [st['kl']])
                        MM(o_, Mc_t[:, 0, i, :], Lc_t[:, 1, i, :], False, False, [Mc, Lc], [st['kl']])
                        MM(o_, Mc_t[:, 1, i, :], Lc_t[:, 0, i, :], False, True, [Mc, Lc], [st['kl']])

                def h2():
                    CP('act', Ln_t[:, 0], p4(st['bl']), [st['kl']], [Ln_])

                def h3():
                    TT('dve', Ln_t[:, 1], p4(st['bl']), Ln_t[:, 0], ALU.subtract, [st['kl'], Ln_], [Ln_])

                def h4():
                    if lev < 5:
                        for a_ in range(2):
                            for i in range(4):
                                TR(pb[hb][:, a_ * 512 + i * 128:a_ * 512 + (i + 1) * 128], Ln_t[:, a_, i, :], ident_b[:], [Ln_, 'ident_b'], [('pb', hb)])
                    st['by'], st['ky'] = nb6()
                    for i in range(4):
                        o_ = pf[st['by']][:, i * 128:(i + 1) * 128]
                        MM(o_, Ln_t[:, 0, i, :], Yc_t[:, 0, i, :], True, False, [Ln_, Yc], [st['ky']])
                        MM(o_, Ln_t[:, 0, i, :], Yc_t[:, 1, i, :], False, False, [Ln_, Yc], [st['ky']])
                        MM(o_, Ln_t[:, 1, i, :], Yc_t[:, 0, i, :], False, True, [Ln_, Yc], [st['ky']])

                def h5():
                    if lev < 5:
                        CP('act', Mn_t[:, 0].rearrange("p h t -> p (h t)"), pb[hb][:, 0:512], [('pb', hb)], [Mn_])
                        CP('dve', Mn_t[:, 1].rearrange("p h t -> p (h t)"), pb[hb][:, 512:1024], [('pb', hb)], [Mn_])
                    TT('dve', YF[sl][hb][nxt][:], p4(st['by']), YF[sl][hb][cur][:], ALU.add, [st['ky'], YFc], [YFn])

                def h6():
                    CP('act', Yn_t[:, 0], YF[sl][hb][nxt][:], [YFn], [Yn_])

                def h7():
                    if lev < 5:
                        TT('pool', Yn_t[:, 1], YF[sl][hb][nxt][:], Yn_t[:, 0], ALU.subtract, [YFn, Yn_], [Yn_])
                return [h1, h2, h3, h4, h5, h6, h7]

            def a2_finish(kt):
                sl = kt % NSL
                for hb in range(2):
                    kh = 'h%d' % hb + 's%d' % sl
                    TTv = YS[sl][hb][0]
                    kT_ = 'Y0' + kh
                    bu, ku = nb6()
                    for i in range(4):
                        h = hb * 4 + i
                        MM(pf[bu][:, i * 128:(i + 1) * 128], TTv[:, 0, i, :], Vb[sl][:, h, :], True, True, [kT_, 'Vb' + kh], [ku])
                    CP('act', f4(Ub[sl], hb), p4(bu), [ku], ['Ub' + kh])
                    bw, kw_ = nb6()
                    for i in range(4):
                        h = hb * 4 + i
                        MM(pf[bw][:, i * 128:(i + 1) * 128], Kbg[sl][:, h, :], TTv[:, 0, i, :], True, True, [kT_, 'Kbg' + kh], [kw_])
                    CP('dve', f4(WTb[sl], hb), p4(bw), [kw_], ['WTb' + kh])

            def a2_recur(kt):
                own = kt >= NTH
                ot = kt - NTH
                sl = kt % NSL
                si = kt % NIN
                ka = ('gates', kt // 4)
                for hb in range(2):
                    kh = 'h%d' % hb + 's%d' % sl
                    bws, kws = nb6()
                    for i in range(4):
                        h = hb * 4 + i
                        MM(pf[bws][:, i * 128:(i + 1) * 128], WTb[sl][:, h, :], Sb[:, h, :], True, True, ['WTb' + kh, 'Sb%d' % hb], [kws])
                    TT('dve', f4(Vn, hb), f4(Ub[sl], hb), p4(bws), ALU.subtract, ['Ub' + kh, kws], ['Vn%d' % hb])
                    if own:
                        bq1, kq1 = nb6()
                        for i in range(4):
                            h = hb * 4 + i
                            MM(pf[bq1][:, i * 128:(i + 1) * 128], qin[si][:, h, :], Sb[:, h, :], True, True, [('qin', si), 'Sb%d' % hb], [kq1])
                        bq2, kq2 = nb6()
                        for i in range(4):
                            h = hb * 4 + i
                            MM(pf[bq2][:, i * 128:(i + 1) * 128], ATb[sl][:, h, :], Vn[:, h, :], True, True, ['ATb' + kh, 'Vn%d' % hb], [kq2])
                        TT('dve', o1[:], p4(bq1), col4(eg_a, kt, hb), ALU.mult, [kq1, ka], ['o1'])
                        TT('dve', f4(otile[sl], hb), o1[:], p4(bq2), ALU.add, ['o1', kq2], ['otile'])
                    bkv, kkv = nb6()
                    for i in range(4):
                        h = hb * 4 + i
                        MM(pf[bkv][:, i * 128:(i + 1) * 128], Ktt[sl][:, h, :], Vn[:, h, :], True, True, ['Ktt' + kh, 'Vn%d' % hb], [kkv])
                    TT('pool', f4(Sf, hb), f4(Sf, hb), col4(egl_a, kt, hb), ALU.mult, ['Sf%d' % hb, ka], ['Sf%d' % hb])
                    TT('dve', f4(Sf, hb), f4(Sf, hb), p4(bkv), ALU.add, ['Sf%d' % hb, kkv], ['Sf%d' % hb])
                    CP('act', f4(Sb, hb), f4(Sf, hb), ['Sf%d' % hb], ['Sb%d' % hb])
                if own:
                    DMA('sp', odn_d[ot * 128:(ot + 1) * 128, :], otile[sl][:].rearrange("p h t -> p (h t)"), ['otile'], [('odn_d', ot)], 'odn_out')

            for kt in range(min(4, NT)):
                a2_loads(kt)
            for p_ in range(NT // 2):
                t0, t1 = 2 * p_, 2 * p_ + 1
                a2_prep(t0)
                a2_prep(t1)
                for lev in range(6):
                    hops = [a2_level_hops(kt, lev, hb) for kt in (t0, t1) for hb in range(2)]
                    for hi in (0, 1, 2):
                        for hp in hops:
                            hp[hi]()
                    for pair in ((0, 1), (2, 3)):
                        for si_ in pair:
                            hops[si_][3]()
                        for si_ in pair:
                            hops[si_][4]()
                    for hi in (5, 6):
                        for hp in hops:
                            hp[hi]()
                a2_finish(t0)
                a2_finish(t1)
                a2_recur(t0)
                a2_recur(t1)
                for kt in (t0 + 4, t1 + 4):
                    if kt < NT:
                        a2_loads(kt)
            barrier([])

        SCALE = 192.0 ** -0.5
        with ExitStack() as eb:
            Wabs = sbt(eb, "Wabs", [128, 3, 8, 256], BF16)
            wuqr = sbt(eb, "wuqr", [128, 3, 512], BF16)
            Wov = sbt(eb, "Wov", [128, 2, 8, 1024], BF16)
            with ExitStack() as e1:
                wuv_s = sbt(e1, "wuv_s", [128, 2, 1024])
                wbm_s = sbt(e1, "wbm_s", [128, 8, 1024])
                VT = sbt(e1, "VTt", [128, 256])
                DMA('sp', wuv_s[:], w_uv.rearrange("(c p) n -> p c n", p=128), [], ['wuv_s'], 'c_wuv')
                DMA('sp', wbm_s[:], w_br_mla.rearrange("(c p) n -> p c n", p=128), [], ['wbm_s'], 'c_wbm')
                for h in range(8):
                    for c in range(2):
                        TR(pf[1][:, 256 + c * 128:256 + (c + 1) * 128], wuv_s[:, c, h * 128:(h + 1) * 128], ident_f[:], ['wuv_s', 'ident_f'], [('pf', 1)])
                    CP('dve', VT[:], pf[1][:, 256:512], [('pf', 1)], ['VT'])
                    for c in range(2):
                        for n in range(2):
                            MM(pf[3][:], VT[:, c * 128:(c + 1) * 128], wbm_s[:, h, n * 512:(n + 1) * 512], True, True, ['VT', 'wbm_s'], [('pf', 3)])
                            CP('dve', Wov[:, c, h, n * 512:(n + 1) * 512], pf[3][:], [('pf', 3)], ['Wov'])
                wuq_s = sbt(e1, "wuq_s", [128, 3, 1536])
                wuk_s = sbt(e1, "wuk_s", [128, 2, 1024])
                AT = sbt(e1, "ATt", [128, 384])
                BT = sbt(e1, "BTt", [128, 256])
                DMA('sp', wuq_s[:], w_uq.rearrange("(c p) n -> p c n", p=128), [], ['wuq_s'], 'c_wuq')
                DMA('sp', wuk_s[:], w_uk.rearrange("(c p) n -> p c n", p=128), [], ['wuk_s'], 'c_wuk')
                for c in range(3):
                    TS('dve', wuq_s[:, c, :], wuq_s[:, c, :], qnw[:, c:c + 1], ALU.mult, ['wuq_s', 'qnw'], ['wuq_s'])
                for c in range(3):
                    CP('dve', wuqr[:, c, :].rearrange("p (h r) -> p h r", h=8), wuq_s[:, c, :].rearrange("p (h d) -> p h d", h=8)[:, :, 128:192], ['wuq_s'], ['wuqr'])
                for h in range(8):
                    for c in range(3):
                        TR(pf[0][:, c * 128:(c + 1) * 128], wuq_s[:, c, h * 192:h * 192 + 128], ident_f[:], ['wuq_s', 'ident_f'], [('pf', 0)])
                    CP('act', AT[:], pf[0][:, 0:384], [('pf', 0)], ['AT'])
                    for c in range(2):
                        TR(pf[1][:, c * 128:(c + 1) * 128], wuk_s[:, c, h * 128:(h + 1) * 128], ident_f[:], ['wuk_s', 'ident_f'], [('pf', 1)])
                    CP('dve', BT[:], pf[1][:, 0:256], [('pf', 1)], ['BT'])
                    for c in range(3):
                        MM(pf[2][:, 0:256], AT[:, c * 128:(c + 1) * 128], BT[:], True, True, ['AT', 'BT'], [('pf', 2)])
                        CP('act', Wabs[:, c, h, :], pf[2][:, 0:256], [('pf', 2)], ['Wabs'])

            barrier([])
            cqt = [sbt(eb, "cqt%d" % i, [128, 3, 128], BF16) for i in range(2)]
            qlT2 = [sbt(eb, "qlT%d" % i, [128, 2, 8, 128], BF16) for i in range(2)]
            qrp = sbt(eb, "qrp", [128, 8, 64])
            qrt = sbt(eb, "qrt", [128, 8, 64])
            qrb = sbt(eb, "qrb", [128, 8, 64], BF16)
            qrT2 = [sbt(eb, "qrT%d" % i, [64, 8, 128], BF16) for i in range(2)]
            PT = [sbt(eb, "PT%d" % i, [128, 512], BF16) for i in range(2)]
            rec = sbt(eb, "rec", [128, 512])
            olT = sbt(eb, "olT", [128, 2, 8, 128], BF16)
            ym = sbt(eb, "ym", [128, 1024])

            def cq_load(qt):
                if qt < NTH:
                    DMA('sp', cqt[qt % 2][:].rearrange("p c t -> p (c t)"), cq_d[qt], [('cq_d', qt)], [('cqt', qt % 2)], 'cqt%d' % (qt % 2))

            def prologue_rounds(qt):
                u = qt % 2
                cq_ = cqt[u]
                kq = ('cqt', u)
                rounds = []
                for cc in range(2):
                    for hq in range(2):
                        def rnd(cc=cc, hq=hq):
                            for i in range(4):
                                h = hq * 4 + i
                                for c in range(3):
                                    MM(pf[5][:, i * 128:(i + 1) * 128], Wabs[:, c, h, cc * 128:(cc + 1) * 128], cq_[:, c, :], c == 0, c == 2, ['Wabs', kq], [('pf', 5)])
                            CP('dve', qlT2[u][:, cc, hq * 4:(hq + 1) * 4, :].rearrange("p h t -> p (h t)"), pf[5][:], [('pf', 5)], [('qlT', u)])
                        rounds.append(rnd)

                def rope():
                    for c in range(3):
                        MM(pf[5][:], cq_[:, c, :], wuqr[:, c, :], c == 0, c == 2, ['wuqr', kq], [('pf', 5)])
                    pv = pf[5][:].rearrange("p (h r) -> p h r", h=8)
                    sinb = cs_all[:, NTH + qt, None, 0:32].broadcast_to([128, 8, 32])
                    cosb = cs_all[:, NTH + qt, None, 32:64].broadcast_to([128, 8, 32])
                    TT('dve', qrp[:, :, 0:32], pv[:, :, 0:32], cosb, ALU.mult, [('pf', 5), 'cs_all'], ['qrp'])
                    TT('dve', qrp[:, :, 32:64], pv[:, :, 32:64], cosb, ALU.mult, [('pf', 5), 'cs_all'], ['qrp'])
                    TT('dve', qrt[:, :, 0:32], pv[:, :, 32:64], sinb, ALU.mult, [('pf', 5), 'cs_all'], ['qrt'])
                    TT('dve', qrt[:, :, 32:64], pv[:, :, 0:32], sinb, ALU.mult, [('pf', 5), 'cs_all'], ['qrt'])
                    TT('dve', qrb[:, :, 0:32], qrp[:, :, 0:32], qrt[:, :, 0:32], ALU.subtract, ['qrp', 'qrt'], ['qrb'])
                    TT('dve', qrb[:, :, 32:64], qrp[:, :, 32:64], qrt[:, :, 32:64], ALU.add, ['qrp', 'qrt'], ['qrb'])
                    for h in range(8):
                        TR(pb[0][0:64, h * 128:(h + 1) * 128], qrb[:, h, :], ident_b[:], ['qrb', 'ident_b'], [('pb', 0)])
                    CP('dve', qrT2[u][:].rearrange("p h t -> p (h t)"), pb[0][0:64, 0:1024], [('pb', 0)], [('qrT', u)])
                rounds.append(rope)
                return rounds

            cq_load(0)
            cq_load(1)
            for r_ in prologue_rounds(0):
                r_()
            for qt in range(NTH):
                qlT = qlT2[qt % 2]
                qrT = qrT2[qt % 2]
                kql, kqr = ('qlT', qt % 2), ('qrT', qt % 2)
                nxt_rounds = prologue_rounds(qt + 1) if qt + 1 < NTH else []
                nkb = NTH + qt + 1
                for hg in range(2):
                    hs = slice(hg * 4, (hg + 1) * 4)
                    def scores(kb):
                        sp_ = kb % 2
                        ksl = slice(kb * 128, (kb + 1) * 128)
                        for cc in range(2):
                            MM(pf[sp_][:], ckvT[:, cc, ksl], qlT[:, cc, hs, :].rearrange("p h t -> p (h t)"), cc == 0, False, [('ckvT', kb), kql], [('pf', sp_)])
                        MM(pf[sp_][:], krT[:, ksl], qrT[:, hs, :].rearrange("p h t -> p (h t)"), False, True, [('krT', kb), kqr], [('pf', sp_)])

                    scores(0)
                    for kb in range(nkb):
                        sp_ = kb % 2
                        if kb + 1 < nkb:
                            scores(kb + 1)
                        ACT(PT[sp_][:], pf[sp_][:], AF.Exp, [('pf', sp_), 'kbias'], [('PT', sp_)], bias=kbias[:, kb:kb + 1], scale=SCALE)
                        if kb == nkb - 1:
                            TT('pool', PT[sp_][:].rearrange("p (h t) -> p h t", h=4), PT[sp_][:].rearrange("p (h t) -> p h t", h=4), caus_b[:, None, :].broadcast_to([128, 4, 128]), ALU.mult, [('PT', sp_), 'caus_b'], [('PT', sp_)])
                        for cc in range(2):
                            MM(pf[2 + cc][:], ckvk[:, kb, cc * 128:(cc + 1) * 128], PT[sp_][:], kb == 0, kb == nkb - 1, [('ckvk', kb), ('PT', sp_)], [('pf', 2 + cc)])
                        MM(pf[4][:], ones_b[:], PT[sp_][:], kb == 0, kb == nkb - 1, ['ones_b', ('PT', sp_)], [('pf', 4)])
                        if hg == 1 and kb % 3 == 1 and nxt_rounds:
                            nxt_rounds.pop(0)()
                    ACT(rec[:], pf[4][:], AF.Ln, [('pf', 4)], ['rec'])
                    ACT(rec[:], rec[:], AF.Exp, ['rec'], ['rec'], scale=-1.0)
                    for cc in range(2):
                        TT('dve', olT[:, cc, hs, :].rearrange("p h t -> p (h t)"), pf[2 + cc][:], rec[:], ALU.mult, [('pf', 2 + cc), 'rec'], ['olT'])
                while nxt_rounds:
                    nxt_rounds.pop(0)()
                for n in range(2):
                    i = 0
                    for h in range(8):
                        for cc in range(2):
                            MM(pf[n][:], olT[:, cc, h, :], Wov[:, cc, h, n * 512:(n + 1) * 512], i == 0, i == 15, ['olT', 'Wov'], [('pf', n)])
                            i += 1
                    CP('act' if n else 'dve', ym[:, n * 512:(n + 1) * 512], pf[n][:], [('pf', n)], ['ym'])
                DMA('sp', ymla_d[qt * 128:(qt + 1) * 128, :], ym[:], ['ym'], [('ymla_d', qt)], 'ym_out')
                cq_load(qt + 2)
            barrier([])
        eAB.close()

        def load_w(es_, name, src, kchunks, ncols, c0=0, colblk=512, groups=None):
            t = sbt(es_, name, [128, kchunks, ncols], BF16)
            nblk = ncols // colblk
            ngrp = len(groups) if groups is not None else nblk
            if not hasattr(P, 'alias'):
                P.alias = {}
            for gi in range(ngrp):
                P.alias[(name, gi)] = [(name, 'ch0'), (name, 'ch1')]
            for kc in range(kchunks):
                DMA('pool', t[:, kc, :], src[kc * 128:(kc + 1) * 128, c0:c0 + ncols], [], [(name, 'ch%d' % (kc % 2))], 'w_%s_%d' % (name, kc % 2))
            return t

        def bcast_vec(es_, name, src, n):
            t = sbt(es_, name, [128, n])
            DMA('sp', t[:], src.partition_broadcast(128), [], [name], 'v_' + name)
            return t

        def layer_norm(x_ap, g_t, b_t, out_ap, stats, mv, rstd, rkey, wkey, gk, bk, sfx):
            xr = x_ap.rearrange("p (c f) -> p c f", c=2)
            for c in range(2):
                P.op('dve', (lambda o_, i_: (lambda e: e.bn_stats(out=o_, in_=i_)))(stats[:, c, :], xr[:, c, :]), reads=[rkey], writes=['ln_stats' + sfx])
            P.op('dve', lambda e: e.bn_aggr(out=mv[:], in_=stats[:]), reads=['ln_stats' + sfx], writes=['ln_mv' + sfx])
            ACT(rstd[:, 0:1], mv[:, 1:2], AF.Ln, ['ln_mv' + sfx, 'epscol'], ['ln_rstd' + sfx], bias=epscol[:, 1:2])
            ACT(rstd[:, 0:1], rstd[:, 0:1], AF.Exp, ['ln_rstd' + sfx], ['ln_rstd' + sfx], scale=-0.5)
            TS('dve', x_ap, x_ap, mv[:, 0:1], ALU.subtract, [rkey, 'ln_mv' + sfx, 'ln_rstd' + sfx], [rkey], s2=rstd[:, 0:1], op1=ALU.mult)
            TT('dve', x_ap, x_ap, g_t[:], ALU.mult, [rkey, gk], [rkey])
            TT('dve', out_ap, x_ap, b_t[:], ALU.add, [rkey, bk], [wkey])

        with ExitStack() as ec:
            wg = load_w(ec, "wg", w_in, 8, 2048, 4816)
            wz = load_w(ec, "wz", w_in, 8, 1024, 3072)
            wbd = load_w(ec, "wbd", w_br_dn, 8, 1024)
            wo = load_w(ec, "wo", w_o, 8, 1024)
            g1 = bcast_vec(ec, "g1", ln1_g, 1024)
            b1 = bcast_vec(ec, "b1", ln1_b, 1024)

            def two(name, shape, dt=F32):
                return [sbt(ec, "%s_%d" % (name, i), shape, dt) for i in range(2)]
            zs = two("zs", [128, 1024])
            orms = two("orms", [128, 8])
            odf = two("odf", [128, 1024])
            junkc = sbt(ec, "junkc", [128, 128])
            xf = two("xf", [128, 1024])
            xbb = two("xbb", [128, 1024], BF16)
            xTc = two("xTc", [128, 8, 128], BF16)
            odt = two("odt", [128, 1024], BF16)
            odT = two("odT", [128, 8, 128], BF16)
            ymt = two("ymt", [128, 1024])
            sg = two("sg", [128, 2048])
            mix = two("mix", [128, 1024])
            mixb = two("mixb", [128, 1024], BF16)
            mixT = two("mixT", [128, 8, 128], BF16)
            r1 = two("r1", [128, 1024])
            hout = two("hout", [128, 1024])
            stats = two("stats", [128, 2, 6])
            mv = two("mv", [128, 2])
            rstd = two("rstd", [128, 1])

            def c1_loads(t):
                u = t % 2
                tsl = slice(t * 128, (t + 1) * 128)
                DMA('sp', xf[u][:], xs[TH + t * 128:TH + (t + 1) * 128, :], [], [('xf', u)], 'c_xf%d' % u)
                DMA('pool', xbb[u][:], xs[TH + t * 128:TH + (t + 1) * 128, :], [], [('xbb', u)], 'c_xbb%d' % u)
                DMA('sp', odf[u][:], odn_d[tsl, :], [('odn_d', t)], [('odf', u)], 'c_odt%d' % u)
                DMA('sp', ymt[u][:], ymla_d[tsl, :], [('ymla_d', t)], [('ymt', u)], 'c_ymt%d' % u)

            def c1_A(t):
                u = t % 2
                for kc in range(8):
                    TR(pb[0][:, kc * 128:(kc + 1) * 128], xbb[u][:, kc * 128:(kc + 1) * 128], ident_b[:], [('xbb', u), 'ident_b'], [('pb', 0)])
                CP('act', xTc[u][:].rearrange("p a b -> p (a b)"), pb[0][:, 0:1024], [('pb', 0)], [('xTc', u)])
                for h in range(8):
                    ACT(junkc[:], odf[u][:, h * 128:(h + 1) * 128], AF.Square, [('odf', u)], ['junkc', ('orms', u)], accum=orms[u][:, h:h + 1])
                ACT(orms[u][:], orms[u][:], AF.Ln, [('orms', u), 'epscol'], [('orms', u)], bias=epscol[:, 0:1], scale=1.0 / 128)
                ACT(orms[u][:], orms[u][:], AF.Exp, [('orms', u)], [('orms', u)], scale=-0.5)
                for h in range(8):
                    STT('dve', odf[u][:, h * 128:(h + 1) * 128], odf[u][:, h * 128:(h + 1) * 128], orms[u][:, h:h + 1], dnw[:], ALU.mult, ALU.mult, [('odf', u), ('orms', u), 'dnw'], [('odf', u)])
                for n in range(4):
                    for kc in range(8):
                        MM(pf[n][:], xTc[u][:, kc, :], wg[:, kc, n * 512:(n + 1) * 512], kc == 0, kc == 7, [('xTc', u), ('wg', n)], [('pf', n)])
                    sgn = sg[u][:, n * 512:(n + 1) * 512]
                    ACT(sgn, pf[n][:], AF.Exp, [('pf', n)], [('sg', u, n)], scale=-1.0)
                    ACT(sgn, sgn, AF.Ln, [('sg', u, n), 'onecol'], [('sg', u, n)], bias=onecol[:, 0:1])
                    ACT(sgn, sgn, AF.Exp, [('sg', u, n)], [('sg', u, n)], scale=-1.0)
                for n in range(2):
                    for kc in range(8):
                        MM(pf[4 + n][:], xTc[u][:, kc, :], wz[:, kc, n * 512:(n + 1) * 512], kc == 0, kc == 7, [('xTc', u), ('wz', n)], [('pf', 4 + n)])
                    zn = zs[u][:, n * 512:(n + 1) * 512]
                    ACT(zn, pf[4 + n][:], AF.Exp, [('pf', 4 + n)], [('zs', u, n)], scale=-1.0)
                    ACT(zn, zn, AF.Ln, [('zs', u, n), 'onecol'], [('zs', u, n)], bias=onecol[:, 0:1])
                    ACT(zn, zn, AF.Exp, [('zs', u, n)], [('zs', u, n)], scale=-1.0)
                    TT('dve', zn, zn, pf[4 + n][:], ALU.mult, [('zs', u, n), ('pf', 4 + n)], [('zs', u, n)])
                    TT('dve', odt[u][:, n * 512:(n + 1) * 512], odf[u][:, n * 512:(n + 1) * 512], zn, ALU.mult, [('odf', u), ('zs', u, n)], [('odt', u)])
                for kc in range(8):
                    TR(pb[1][:, kc * 128:(kc + 1) * 128], odt[u][:, kc * 128:(kc + 1) * 128], ident_b[:], [('odt', u), 'ident_b'], [('pb', 1)])
                CP('act', odT[u][:].rearrange("p a b -> p (a b)"), pb[1][:, 0:1024], [('pb', 1)], [('odT', u)])
                TT('dve', ymt[u][:], ymt[u][:], sg[u][:, 1024:2048], ALU.mult, [('ymt', u), ('sg', u, 2), ('sg', u, 3)], [('ymt', u)])
                for n in range(2):
                    for kc in range(8):
                        MM(pf[4 + n][:], odT[u][:, kc, :], wbd[:, kc, n * 512:(n + 1) * 512], kc == 0, kc == 7, [('odT', u), ('wbd', n)], [('pf', 4 + n)])
                    TT('dve', mix[u][:, n * 512:(n + 1) * 512], pf[4 + n][:], sg[u][:, n * 512:(n + 1) * 512], ALU.mult, [('pf', 4 + n), ('sg', u, n)], [('mix', u)])
                TT('dve', mixb[u][:], mix[u][:], ymt[u][:], ALU.add, [('mix', u), ('ymt', u)], [('mixb', u)])

            def c1_B(t):
                u = t % 2
                tsl = slice(t * 128, (t + 1) * 128)
                for kc in range(8):
                    TR(pb[0][:, kc * 128:(kc + 1) * 128], mixb[u][:, kc * 128:(kc + 1) * 128], ident_b[:], [('mixb', u), 'ident_b'], [('pb', 0)])
                CP('act', mixT[u][:].rearrange("p a b -> p (a b)"), pb[0][:, 0:1024], [('pb', 0)], [('mixT', u)])
                for n in range(2):
                    for kc in range(8):
                        MM(pf[n][:], mixT[u][:, kc, :], wo[:, kc, n * 512:(n + 1) * 512], kc == 0, kc == 7, [('mixT', u), ('wo', n)], [('pf', n)])
                    STT('dve', r1[u][:, n * 512:(n + 1) * 512], xf[u][:, n * 512:(n + 1) * 512], ALPHA, pf[n][:], ALU.mult, ALU.add, [('xf', u), ('pf', n)], [('r1', u)])
                layer_norm(r1[u][:], g1, b1, hout[u][:], stats[u], mv[u], rstd[u], ('r1', u), ('hout', u), 'g1', 'b1', 'c1%d' % u)
                DMA('sp', h1_d[tsl, :], hout[u][:], [('hout', u)], [('h1_d', t)], 'c_hout%d' % u)

            c1_loads(0)
            if NTH > 1:
                c1_loads(1)
            c1_A(0)
            for t in range(NTH):
                if t + 1 < NTH:
                    c1_A(t + 1)
                c1_B(t)
                if t + 2 < NTH:
                    c1_loads(t + 2)
            barrier([])

        with ExitStack() as ed:
            wf1 = load_w(ed, "wf1", w_ffn_in, 8, 5632, colblk=256, groups=[[j, 11 + j] for j in range(11)])
            wpg = load_w(ed, "wpg", w_ple_gate, 8, 1024)
            wpl = load_w(ed, "wpl", w_ple, 2, 1024)
            wf2 = load_w(ed, "wf2", w_ffn_out, 22, 1024)
            g2 = bcast_vec(ed, "g2", ln2_g, 1024)
            b2 = bcast_vec(ed, "b2", ln2_b, 1024)

            def two(name, shape, dt=F32):
                return [sbt(ed, "%s_%d" % (name, i), shape, dt) for i in range(2)]
            hf = two("hf", [128, 1024])
            hb2 = two("hb2", [128, 1024], BF16)
            hT = two("hT", [128, 8, 128], BF16)
            pt_ = [sbt(ed, "pt_", [128, 256], BF16)] * 2
            pT = [sbt(ed, "pT", [128, 2, 128], BF16)] * 2
            sgt = two("sgt", [128, 256])
            act = sbt(ed, "act", [128, 2816], BF16)
            actT = sbt(ed, "actT", [128, 22, 128], BF16)
            plg = sbt(ed, "plg", [128, 1024])
            yout = two("yout", [128, 1024])
            stats2 = sbt(ed, "stats2", [128, 2, 6])
            mv2 = sbt(ed, "mv2", [128, 2])
            rstd2 = sbt(ed, "rstd2", [128, 1])

            def c2_loads(t):
                u = t % 2
                tsl = slice(t * 128, (t + 1) * 128)
                DMA('sp', hf[u][:], h1_d[tsl, :], [('h1_d', t)], [('hf', u)], 'd_hf%d' % u)
                DMA('pool', hb2[u][:], h1_d[tsl, :], [('h1_d', t)], [('hb', u)], 'd_hb%d' % u)

            def c2_A(t):
                u = t % 2
                DMA('pool', pt_[u][:], pp[t * 128:(t + 1) * 128, :], [], ['pt_'], 'd_pt')
                for kc in range(8):
                    TR(pb[0][:, kc * 128:(kc + 1) * 128], hb2[u][:, kc * 128:(kc + 1) * 128], ident_b[:], [('hb', u), 'ident_b'], [('pb', 0)])
                CP('act', hT[u][:].rearrange("p a b -> p (a b)"), pb[0][:, 0:1024], [('pb', 0)], [('hT', u)])
                for kc in range(2):
                    TR(pb[1][:, kc * 128:(kc + 1) * 128], pt_[u][:, kc * 128:(kc + 1) * 128], ident_b[:], ['pt_', 'ident_b'], [('pb', 1)])
                CP('dve', pT[u][:].rearrange("p a b -> p (a b)"), pb[1][:, 0:256], [('pb', 1)], ['pT'])
                for j in range(11):
                    bk = j % 2
                    for kc in range(8):
                        MM(pf[bk][:, 0:256], hT[u][:, kc, :], wf1[:, kc, j * 256:(j + 1) * 256], kc == 0, kc == 7, [('hT', u), ('wf1', j)], [('pf', bk)])
                    for kc in range(8):
                        MM(pf[bk][:, 256:512], hT[u][:, kc, :], wf1[:, kc, 2816 + j * 256:2816 + (j + 1) * 256], kc == 0, kc == 7, [('hT', u), ('wf1', j)], [('pf', bk)])
                    s_ = sgt[bk]
                    ACT(s_[:], pf[bk][:, 0:256], AF.Exp, [('pf', bk)], [('sgt', bk)], scale=-1.0)
                    ACT(s_[:], s_[:], AF.Ln, [('sgt', bk), 'onecol'], [('sgt', bk)], bias=onecol[:, 0:1])
                    ACT(s_[:], s_[:], AF.Exp, [('sgt', bk)], [('sgt', bk)], scale=-1.0)
                    TT('dve', s_[:], s_[:], pf[bk][:, 0:256], ALU.mult, [('sgt', bk), ('pf', bk)], [('sgt', bk)])
                    TT('dve', act[:, j * 256:(j + 1) * 256], s_[:], pf[bk][:, 256:512], ALU.mult, [('sgt', bk), ('pf', bk)], [('act', j)])

            def c2_B1(t):
                u = t % 2
                for n in range(2):
                    for kc in range(8):
                        MM(pf[4][:], hT[u][:, kc, :], wpg[:, kc, n * 512:(n + 1) * 512], kc == 0, kc == 7, [('hT', u), ('wpg', n)], [('pf', 4)])
                    pn = plg[:, n * 512:(n + 1) * 512]
                    ACT(pn, pf[4][:], AF.Exp, [('pf', 4)], [('plg', n)], scale=-1.0)
                    ACT(pn, pn, AF.Ln, [('plg', n), 'onecol'], [('plg', n)], bias=onecol[:, 0:1])
                    ACT(pn, pn, AF.Exp, [('plg', n)], [('plg', n)], scale=-1.0)
                    for kc in range(2):
                        MM(pf[5][:], pT[u][:, kc, :], wpl[:, kc, n * 512:(n + 1) * 512], kc == 0, kc == 1, ['pT', ('wpl', n)], [('pf', 5)])
                    TT('dve', pn, pn, pf[5][:], ALU.mult, [('plg', n), ('pf', 5)], [('plg', n)])
                for j in range(22):
                    TR(pb[(j // 8) % 2][:, (j % 8) * 128:(j % 8 + 1) * 128], act[:, j * 128:(j + 1) * 128], ident_b[:], [('act', j // 2), 'ident_b'], [('pb', (j // 8) % 2)])
                    if j % 8 == 7 or j == 21:
                        j0 = (j // 8) * 8
                        n_ = j - j0 + 1
                        CP('act' if (j // 8) % 2 else 'dve', actT[:, j0:j0 + n_, :].rearrange("p a b -> p (a b)"), pb[(j // 8) % 2][:, 0:n_ * 128], [('pb', (j // 8) % 2)], [('actT', j // 8)])
                for n in range(2):
                    for j in range(22):
                        MM(pf[2 + n][:], actT[:, j, :], wf2[:, j, n * 512:(n + 1) * 512], j == 0, j == 21, [('actT', j // 8), ('wf2', n)], [('pf', 2 + n)])

            def c2_B2(t):
                u = t % 2
                tsl = slice(t * 128, (t + 1) * 128)
                yk = ('yout', u)
                for n in range(2):
                    STT('dve', yout[u][:, n * 512:(n + 1) * 512], hf[u][:, n * 512:(n + 1) * 512], ALPHA, pf[2 + n][:], ALU.mult, ALU.add, [('hf', u), ('pf', 2 + n)], [yk])
                TT('dve', yout[u][:], yout[u][:], plg[:], ALU.add, [yk, ('plg', 0), ('plg', 1)], [yk])
                layer_norm(yout[u][:], g2, b2, yout[u][:], stats2, mv2, rstd2, yk, yk, 'g2', 'b2', 'c2')
                DMA('sp', out[tsl, :], yout[u][:], [yk], [('out', t)], 'd_out%d' % u)

            c2_loads(0)
            if NTH > 1:
                c2_loads(1)
            c2_A(0)
            for t in range(NTH):
                c2_B1(t)
                if t + 1 < NTH:
                    c2_A(t + 1)
                c2_B2(t)
                if t + 2 < NTH:
                    c2_loads(t + 2)
        P.emit(es)
    return nc


_CACHE = {}


def _host_inputs(inputs, TH, b, half):
    x = np.asarray(inputs['x'], np.float32)
    S = 2 * TH
    own = x[b, half * TH:(half + 1) * TH]
    pre = x[b, 0:TH] if half == 1 else np.zeros((TH, 1024), np.float32)
    posa = np.asarray(inputs['positions'], np.int32)[b]
    pos_own = posa[half * TH:(half + 1) * TH]
    pos_pre = posa[0:TH] if half == 1 else np.zeros((TH,), np.int32)
    kb = np.zeros((128, 2 * (TH // 128)), np.float32)
    if half == 0:
        kb[:, :TH // 128] = NEG
    invf = (10000.0 ** (-np.arange(0, 64, 2, dtype=np.float32) / 64.0)).astype(np.float32) / np.float32(2 * np.pi)
    invf2 = np.broadcast_to(np.concatenate([invf, invf])[None, :], (128, 64)).astype(np.float32)
    m = {
        'xs': np.ascontiguousarray(np.concatenate([pre, own], 0)),
        'pp': np.ascontiguousarray(np.asarray(inputs['p'], np.float32)[0, b, half * TH:(half + 1) * TH]),
        'pos': np.ascontiguousarray(np.concatenate([pos_pre, pos_own]).reshape(-1, 128).T),
        'kbias': kb, 'invf': np.ascontiguousarray(invf2),
    }
    sq = lambda k: np.ascontiguousarray(np.asarray(inputs[k], np.float32)[0])
    m['w_in'] = sq('w_in')
    m['conv_w'] = sq('conv_w')
    m['dn_a_log'] = sq('dn_a_log')
    m['dn_dt_bias'] = sq('dn_dt_bias')
    m['dn_norm_w'] = sq('dn_norm_w')
    m['q_norm_w'] = sq('q_norm_w').reshape(3, 128)
    m['w_uq'] = sq('w_uq').reshape(384, 1536)
    m['kv_norm_w'] = sq('kv_norm_w')
    m['w_uk'] = sq('w_uk').reshape(256, 1024)
    m['w_uv'] = sq('w_uv').reshape(256, 1024)
    for k in ('w_br_dn', 'w_br_mla', 'w_o', 'ln1_g', 'ln1_b', 'w_ffn_in', 'w_ffn_out', 'w_ple', 'w_ple_gate', 'ln2_g', 'ln2_b'):
        m[k] = sq(k)
    return m


def kernel(**inputs):
    x = np.asarray(inputs['x'])
    B, S, _ = x.shape
    TH = S // 2
    if TH not in _CACHE:
        _CACHE[TH] = build(TH)
    nc = _CACHE[TH]
    in_maps = []
    cores = []
    for b in range(B):
        for half in range(2):
            in_maps.append(_host_inputs(inputs, TH, b, half))
            cores.append((b, half))
    res = run_bass_kernel_spmd(nc, in_maps, core_ids=list(range(len(in_maps))))
    out = np.zeros((B, S, 1024), np.float32)
    for i, (b, half) in enumerate(cores):
        out[b, half * TH:(half + 1) * TH] = res.results[i]['out']
    return out
```

```python
import numpy as np
from contextlib import ExitStack
import concourse.bass as bass
import concourse.mybir as mybir
from concourse.bass_utils import run_bass_kernel_spmd

F32 = mybir.dt.float32
BF16 = mybir.dt.bfloat16
I32 = mybir.dt.int32
AF = mybir.ActivationFunctionType
ALU = mybir.AluOpType
NEG = -30000.0
ALPHA = 2.0 ** 0.25


class Prog:
    ENG = ('pe', 'act', 'dve', 'pool', 'sp')

    def __init__(self, nc):
        self.nc = nc
        self.ops = []
        self.last_w = {}
        self.readers = {}
        self.dma_cnt = {}

    def op(self, eng, fn, reads=(), writes=(), dma_slot=None):
        if getattr(self, 'defer', None) is not None:
            self.defer.append((eng, fn, list(reads), list(writes), dma_slot))
            return None
        o = dict(eng=eng, fn=fn, deps=set(), idx=len(self.ops), need_inc=False, dma_slot=dma_slot)
        al = getattr(self, 'alias', {})
        if al:
            reads = [x for k in reads for x in al.get(k, [k])]
        isps = lambda k: isinstance(k, tuple) and k[0] in ('pf', 'pb')
        writes = [k[:2] if isps(k) else k for k in writes] + [k[:2] for k in reads if isps(k)]
        reads = [k for k in reads if not isps(k)]
        writes = list(dict.fromkeys(writes))
        for r in reads:
            w = self.last_w.get(r)
            if w is not None:
                o['deps'].add(w)
        for w_ in writes:
            w = self.last_w.get(w_)
            if w is not None:
                o['deps'].add(w)
            for rd in self.readers.get(w_, ()):
                o['deps'].add(rd)
        for r in reads:
            self.readers.setdefault(r, []).append(o['idx'])
        for w_ in writes:
            self.last_w[w_] = o['idx']
            self.readers[w_] = []
        o['deps'].discard(o['idx'])
        if dma_slot is not None:
            self.dma_cnt[dma_slot] = self.dma_cnt.get(dma_slot, 0) + 1
            o['dma_val'] = 16 * self.dma_cnt[dma_slot]
        self.ops.append(o)
        return o

    def deferred(self, thunk):
        self.defer = []
        thunk()
        rec, self.defer = self.defer, None
        hops, cur = [], None
        for r in rec:
            if r[0] != cur:
                hops.append([])
                cur = r[0]
            hops[-1].append(r)

        def mk(hop):
            def run():
                for (eng, fn, rd, wr, slot) in hop:
                    self.op(eng, fn, rd, wr, slot)
            return run
        return [mk(h) for h in hops]

    def emit(self, es):
        nc = self.nc
        ops = self.ops
        for o in ops:
            for d in o['deps']:
                a = ops[d]
                if a['dma_slot'] is None:
                    if a['eng'] == 'pe' and o['eng'] == 'pe':
                        continue
                    a['need_inc'] = True
        SEG = 30000
        cnt = {e: 0 for e in self.ENG}
        for o in ops:
            if o['dma_slot'] is None and o['need_inc']:
                c = cnt[o['eng']]
                o['sem'] = (o['eng'], c // SEG)
                o['val'] = c % SEG + 1
                cnt[o['eng']] = c + 1
        sems = {}

        def getsem(key):
            if key not in sems:
                sems[key] = es.enter_context(nc.semaphore("s_" + "_".join(str(k) for k in key)))
            return sems[key]
        waited = {e: {} for e in self.ENG}
        for o in ops:
            need = {}
            for d in o['deps']:
                a = ops[d]
                if a['dma_slot'] is not None:
                    key = ('dma', a['dma_slot'])
                    v = a['dma_val']
                else:
                    if a['eng'] == 'pe' and o['eng'] == 'pe':
                        continue
                    key = a['sem']
                    v = a['val']
                if need.get(key, 0) < v:
                    need[key] = v
            w = []
            for key, v in need.items():
                if waited[o['eng']].get(key, 0) >= v:
                    continue
                waited[o['eng']][key] = v
                w.append((getsem(key), v))
            o['waits'] = w
            if o['dma_slot'] is not None:
                o['incsem'] = getsem(('dma', o['dma_slot']))
            elif o['need_inc']:
                o['incsem'] = getsem(o['sem'])
        final = [(getsem(('dma', k)), 16 * v) for k, v in self.dma_cnt.items()]
        per = {e: [o for o in ops if o['eng'] == e] for e in self.ENG}

        def run(eng_obj, lst, fin=False):
            for o in lst:
                for (s, v) in o['waits']:
                    eng_obj.wait_ge(s, v)
                ins = o['fn'](eng_obj)
                if o['dma_slot'] is not None:
                    ins.then_inc(o['incsem'], 16)
                elif o['need_inc']:
                    ins.then_inc(o['incsem'], 1)
            if fin:
                for (s, v) in final:
                    eng_obj.wait_ge(s, v)
        with nc.Block() as block:
            @block.tensor
            def _(e):
                run(e, per['pe'])

            @block.scalar
            def _(e):
                run(e, per['act'])

            @block.vector
            def _(e):
                run(e, per['dve'])

            @block.gpsimd
            def _(e):
                run(e, per['pool'])

            @block.sync
            def _(e):
                run(e, per['sp'], fin=True)


def build(TH, dbg=False):
    NTH = TH // 128
    NT = 2 * NTH
    nc = bass.Bass("TRN2", target_bir_lowering=False)

    def di(n, s, dt=F32):
        return nc.dram_tensor(n, s, dt, kind="ExternalInput").ap()
    xs = di("xs", [2 * TH, 1024])
    pp = di("pp", [TH, 256])
    pos = di("pos", [128, NT], I32)
    kbias_d = di("kbias", [128, NT])
    invf_d = di("invf", [128, 64])
    w_in = di("w_in", [1024, 6864])
    conv_w = di("conv_w", [4, 3072])
    a_log = di("dn_a_log", [8])
    dt_bias = di("dn_dt_bias", [8])
    dn_norm_w = di("dn_norm_w", [128])
    q_norm_w = di("q_norm_w", [3, 128])
    w_uq = di("w_uq", [384, 1536])
    kv_norm_w = di("kv_norm_w", [256])
    w_uk = di("w_uk", [256, 1024])
    w_uv = di("w_uv", [256, 1024])
    w_br_dn = di("w_br_dn", [1024, 1024])
    w_br_mla = di("w_br_mla", [1024, 1024])
    w_o = di("w_o", [1024, 1024])
    ln1_g = di("ln1_g", [1024])
    ln1_b = di("ln1_b", [1024])
    w_ffn_in = di("w_ffn_in", [1024, 5632])
    w_ffn_out = di("w_ffn_out", [2816, 1024])
    w_ple = di("w_ple", [256, 1024])
    w_ple_gate = di("w_ple_gate", [1024, 1024])
    ln2_g = di("ln2_g", [1024])
    ln2_b = di("ln2_b", [1024])
    out = nc.dram_tensor("out", [TH, 1024], F32, kind="ExternalOutput").ap()
    odn_d = nc.dram_tensor("odn_s", [TH, 1024], F32).ap()
    ymla_d = nc.dram_tensor("ymla_s", [TH, 1024], F32).ap()
    h1_d = nc.dram_tensor("h1_s", [TH, 1024], F32).ap()
    cq_d = nc.dram_tensor("cq_s", [TH // 128, 128, 384], BF16).ap()
    qn_d = nc.dram_tensor("qn_s", [8, 128, TH], BF16).ap()
    kn_d = nc.dram_tensor("kn_s", [8, 128, 2 * TH], BF16).ap()
    vn_d = nc.dram_tensor("vn_s", [8, 128, 2 * TH], BF16).ap()

    es = ExitStack()
    with es:
        P = Prog(nc)
        uid = [0]

        def sbt(es_, name, shape, dt=F32):
            return es_.enter_context(nc.sbuf_tensor(name, shape, dt))

        pf = [es.enter_context(nc.psum_tensor("pf%d" % i, [128, 512], F32)) for i in range(6)]
        pb = [es.enter_context(nc.psum_tensor("pb%d" % i, [128, 1024], BF16)) for i in range(2)]

        def MM(o, lhsT, rhs, start, stop, r, w):
            P.op('pe', lambda e: e.matmul(o, lhsT=lhsT, rhs=rhs, start=start, stop=stop), reads=r, writes=w)

        def TR(o, i, ident, r, w):
            P.op('pe', lambda e: e.transpose(out=o, in_=i, identity=ident), reads=r, writes=w)

        def ACT(o, i, func, r, w, bias=None, scale=None, accum=None):
            kw = {}
            if bias is not None:
                kw['bias'] = bias
            if scale is not None:
                kw['scale'] = scale
            if accum is not None:
                kw['accum_out'] = accum
            P.op('act', lambda e: e.activation(out=o, in_=i, func=func, **kw), reads=r, writes=w)

        def TT(eng, o, a, b, op, r, w):
            P.op(eng, lambda e: e.tensor_tensor(out=o, in0=a, in1=b, op=op), reads=r, writes=w)

        def TS(eng, o, a, s1, op0, r, w, s2=None, op1=None):
            if op1 is None:
                P.op(eng, lambda e: e.tensor_scalar(out=o, in0=a, scalar1=s1, scalar2=None, op0=op0), reads=r, writes=w)
            else:
                P.op(eng, lambda e: e.tensor_scalar(out=o, in0=a, scalar1=s1, scalar2=s2, op0=op0, op1=op1), reads=r, writes=w)

        def STT(eng, o, a, s, b, op0, op1, r, w):
            P.op(eng, lambda e: e.scalar_tensor_tensor(out=o, in0=a, scalar=s, in1=b, op0=op0, op1=op1), reads=r, writes=w)

        def CP(eng, o, i, r, w):
            if eng == 'act':
                P.op('act', lambda e: e.copy(out=o, in_=i), reads=r, writes=w)
            else:
                P.op(eng, lambda e: e.tensor_copy(out=o, in_=i), reads=r, writes=w)

        def DMA(eng, o, i, r, w, slot):
            P.op(eng, lambda e: e.dma_start(out=o, in_=i), reads=r, writes=w, dma_slot=slot)

        def MEMSET(eng, o, v, w):
            P.op(eng, lambda e: e.memset(o, v), writes=w)

        def AFSEL(o, pattern, cmp, fill, cm, key):
            P.op('pool', lambda e: e.affine_select(out=o, in_=o, pattern=pattern, compare_op=cmp, fill=fill, base=0, channel_multiplier=cm), reads=[key], writes=[key])

        ident_f = sbt(es, "ident_f", [128, 128])
        ident_b = sbt(es, "ident_b", [128, 128], BF16)
        ones_f = sbt(es, "ones_f", [128, 128])
        ones_b = sbt(es, "ones_b", [128, 128], BF16)
        triU_f = sbt(es, "triU_f", [128, 128])
        caus_b = sbt(es, "caus_b", [128, 128], BF16)
        maskLs = sbt(es, "maskLs", [128, 128])
        maskU = sbt(es, "maskU", [128, 128])
        onecol = sbt(es, "onecol", [128, 1])
        epscol = sbt(es, "epscol", [128, 4])
        MEMSET('pool', ident_f[:], 0.0, ['ident_f'])
        AFSEL(ident_f[:], [[-1, 128]], ALU.not_equal, 1.0, 1, 'ident_f')
        CP('pool', ident_b[:], ident_f[:], ['ident_f'], ['ident_b'])
        MEMSET('pool', ones_f[:], 1.0, ['ones_f'])
        MEMSET('pool', ones_b[:], 1.0, ['ones_b'])
        MEMSET('pool', onecol[:], 1.0, ['onecol'])
        MEMSET('pool', epscol[:, 0:1], 1e-6, ['epscol'])
        MEMSET('pool', epscol[:, 1:2], 1e-5, ['epscol'])
        MEMSET('pool', triU_f[:], 1.0, ['triU_f'])
        AFSEL(triU_f[:], [[1, 128]], ALU.is_ge, 0.0, -1, 'triU_f')
        CP('pool', caus_b[:], triU_f[:], ['triU_f'], ['caus_b'])
        MEMSET('pool', maskLs[:], 0.0, ['maskLs'])
        AFSEL(maskLs[:], [[-1, 128]], ALU.is_gt, NEG, 1, 'maskLs')
        MEMSET('pool', maskU[:], 0.0, ['maskU'])
        AFSEL(maskU[:], [[1, 128]], ALU.is_ge, NEG, -1, 'maskU')

        negA = sbt(es, "negA", [128, 8])
        dtb = sbt(es, "dtb", [128, 8])
        dnw = sbt(es, "dnw", [128, 128])
        kvw = sbt(es, "kvw", [128, 256])
        invf = sbt(es, "invf_s", [128, 64])
        kbias = sbt(es, "kbias_s", [128, NT])
        DMA('sp', negA[:], a_log.partition_broadcast(128), [], ['negA'], 'c_negA')
        DMA('sp', dtb[:], dt_bias.partition_broadcast(128), [], ['dtb'], 'c_dtb')
        DMA('sp', dnw[:], dn_norm_w.partition_broadcast(128), [], ['dnw'], 'c_dnw')
        DMA('sp', kvw[:], kv_norm_w.partition_broadcast(128), [], ['kvw'], 'c_kvw')
        DMA('sp', invf[:], invf_d, [], ['invf'], 'c_invf')
        DMA('sp', kbias[:], kbias_d, [], ['kbias'], 'c_kbias')
        ACT(negA[:], negA[:], AF.Exp, ['negA'], ['negA'])
        TS('dve', negA[:], negA[:], -1.0, ALU.mult, ['negA'], ['negA'])

        cw = sbt(es, "cw", [128, 24, 4])
        qnw = sbt(es, "qnw", [128, 3])
        bar_t = sbt(es, "bar_t", [128, 8])
        eAB = ExitStack()
        ckvT = sbt(eAB, "ckvT", [128, 2, NT * 128], BF16)
        krT = sbt(eAB, "krT", [64, NT * 128], BF16)
        ckvk = sbt(eAB, "ckvk", [128, NT, 256], BF16)
        cs_all = sbt(eAB, "cs_all", [128, NT, 64])
        with ExitStack() as e0:
            cwr = sbt(e0, "cwr", [4, 3072])
            qnr = sbt(e0, "qnr", [3, 128])
            DMA('sp', cwr[:], conv_w, [], ['cwr'], 'c_cwr')
            DMA('sp', qnr[:], q_norm_w, [], ['qnr'], 'c_qnr')
            for blk in range(24):
                TR(pf[0][:, blk * 4:(blk + 1) * 4], cwr[0:4, blk * 128:(blk + 1) * 128], ident_f[0:4, 0:4], ['cwr', 'ident_f'], [('pf', 0)])
            CP('dve', cw[:].rearrange("p b i -> p (b i)"), pf[0][:, 0:96], [('pf', 0)], ['cw'])
            TR(pf[1][:, 0:3], qnr[0:3, :], ident_f[0:3, 0:3], ['qnr', 'ident_f'], [('pf', 1)])
            CP('dve', qnw[:], pf[1][:, 0:3], [('pf', 1)], ['qnw'])

        keysA = ['cwr', 'qnr', 'wuq_s', 'wuk_s', 'wuv_s', 'wbm_s', 'AT', 'BT', 'VT']

        def barrier(extra_reads):
            allk = list(set(list(P.last_w.keys()) + list(P.readers.keys())))
            P.op('pool', lambda e: e.memset(bar_t[:, 0:1], 1.0), reads=allk, writes=allk + ['BAR'])
            P.op('dve', lambda e: e.memset(bar_t[:, 1:2], 1.0), reads=['BAR'], writes=['BAR_dve'])
            P.op('act', lambda e: e.copy(out=bar_t[:, 2:3], in_=bar_t[:, 0:1]), reads=['BAR'], writes=['BAR_act'])
            P.op('pe', lambda e: e.transpose(out=pf[0][0:4, 0:4], in_=ident_f[0:4, 0:4], identity=ident_f[0:4, 0:4]), reads=['BAR'], writes=[('pf', 0)])
            P.op('sp', lambda e: e.dma_start(out=bar_t[:, 4:8], in_=kbias_d[:, 0:4]), reads=['BAR'], writes=['BAR_sp'], dma_slot='bar_sp')

        barrier(keysA)

        NGRP = NT // 4
        GP = NTH // 4
        beta_a = sbt(eAB, "beta_a", [128, NT, 8])
        gcs_a = sbt(eAB, "gcs_a", [128, NT, 8])
        eg_a = sbt(eAB, "eg_a", [128, NT, 8])
        ekt_a = sbt(eAB, "ekt_a", [128, NT, 8])
        egl_a = sbt(eAB, "egl_a", [128, NT, 8])
        bkg_a = sbt(eAB, "bkg_a", [128, NT, 8])
        g8_a = sbt(eAB, "g8_a", [128, NT, 8])
        with ExitStack() as er:
            posi = sbt(er, "posi", [128, NT], I32)
            posf = sbt(er, "posf", [128, NT])
            ang = sbt(er, "ang", [128, NT, 64])
            angi = sbt(er, "angi", [128, NT, 64], I32)
            angf = sbt(er, "angf", [128, NT, 64])
            DMA('sp', posi[:], pos, [], ['posi'], 'posi')
            CP('dve', posf[:], posi[:], ['posi'], ['posf'])
            TT('dve', ang[:], invf[:, None, :].broadcast_to([128, NT, 64]), posf[:, :, None].broadcast_to([128, NT, 64]), ALU.mult, ['invf', 'posf'], ['ang'])
            TS('dve', ang[:, :, 32:64], ang[:, :, 32:64], 0.25, ALU.add, ['ang'], ['ang'])
            CP('dve', angi[:], ang[:], ['ang'], ['angi'])
            CP('dve', angf[:], angi[:], ['angi'], ['angf'])
            TT('dve', ang[:], ang[:], angf[:], ALU.subtract, ['ang', 'angf'], ['ang'])
            TS('dve', angf[:], ang[:], 0.5, ALU.is_gt, ['ang'], ['angf'])
            TT('dve', ang[:], ang[:], angf[:], ALU.subtract, ['ang', 'angf'], ['ang'])
            TS('dve', angf[:], ang[:], -0.5, ALU.is_lt, ['ang'], ['angf'])
            TT('dve', ang[:], ang[:], angf[:], ALU.add, ['ang', 'angf'], ['ang'])
            ACT(cs_all[:], ang[:], AF.Sin, ['ang'], ['cs_all'], scale=2 * np.pi)
        barrier([])
        with ExitStack() as ea:
            wA = sbt(ea, "wA", [128, 8, 3792], BF16)
            wA_grp = {c2: 0 for c2 in range(12)}
            for kc in range(8):
                DMA('pool', wA[:, kc, 3072:3792], w_in[kc * 128:(kc + 1) * 128, 4096:4816], [], [('wA', 'small')], 'wA_small')
            for kc in range(8):
                DMA('pool', wA[:, kc, 0:3072], w_in[kc * 128:(kc + 1) * 128, 0:3072], [], [('wA', 0)], 'wA_0')
            wAk = [('wA', 'small')]
            xb = [sbt(ea, "xb%d" % i, [128, 1024], BF16) for i in range(2)]
            xTg = [sbt(ea, "xTg%d" % i, [128, 8, 512], BF16) for i in range(2)]
            tmp8 = sbt(ea, "tmp8", [128, 4, 8])
            smallf = sbt(ea, "smallf", [128, 4, 8])
            ssq8 = [sbt(ea, "ssq8_%d" % i, [128, 8]) for i in range(2)]
            junk = sbt(ea, "junk", [128, 384])
            junk2 = [sbt(ea, "junk2_%d" % i, [128, 64]) for i in range(2)]
            cqb = [sbt(ea, "cqb%d" % i, [128, 384], BF16) for i in range(2)]
            krr = [sbt(ea, "krr%d" % i, [128, 64]) for i in range(2)]
            krb = [sbt(ea, "krb%d" % i, [128, 64], BF16) for i in range(2)]
            cqs = [sbt(ea, "cqs%d" % i, [128, 384], BF16) for i in range(2)]
            hist = sbt(ea, "hist", [128, 24, 3])
            D_CB, D_CT, D_SG, D_SL, D_SQ, D_LN, D_OB = 3, 4, 3, 6, 3, 3, 3
            cb = [sbt(ea, "cb%d" % i, [128, 515]) for i in range(D_CB)]
            ct = [sbt(ea, "ct%d" % i, [128, 512]) for i in range(D_CT)]
            sg = [sbt(ea, "sg%d" % i, [128, 512]) for i in range(D_SG)]
            sl = [sbt(ea, "sl%d" % i, [128, 512]) for i in range(D_SL)]
            sq = [sbt(ea, "sq%d" % i, [128, 512], BF16) for i in range(D_SQ)]
            lnr = [sbt(ea, "lnr%d" % i, [128, 512]) for i in range(D_LN)]
            ob = [sbt(ea, "ob%d" % i, [128, 512], BF16) for i in range(D_OB)]
            MEMSET('pool', hist[:], 0.0, ['hist'])
            bankc = [0]

            def nb():
                b = 2 + bankc[0] % 4
                bankc[0] += 1
                return b

            def x_prologue(g):
                Xg = xTg[g % 2]
                for j in range(4):
                    kt = g * 4 + j
                    s2 = kt % 2
                    DMA('pool', xb[s2][:], xs[kt * 128:(kt + 1) * 128, :], [], [('xb', s2)], 'xb%d' % s2)
                    for kc in range(8):
                        TR(pb[s2][:, kc * 128:(kc + 1) * 128], xb[s2][:, kc * 128:(kc + 1) * 128], ident_b[:], [('xb', s2), 'ident_b'], [('pb', s2)])
                    CP('act', Xg[:, :, j * 128:(j + 1) * 128], pb[s2][:, 0:1024].rearrange("p (a b) -> p a b", a=8), [('pb', s2)], [('xTg', g % 2)])

            def gates(g):
                Xg = xTg[g % 2]
                kXg = ('xTg', g % 2)
                t4 = slice(g * 4, g * 4 + 4)
                ka = ('gates', g)
                for j in range(4):
                    for kc in range(8):
                        MM(pf[0][:, j * 16:(j + 1) * 16], Xg[:, kc, j * 128:(j + 1) * 128], wA[:, kc, 3072:3088], kc == 0, kc == 7, [kXg] + wAk, [('pf', 0)])
                pba = pf[0][:, 0:64].rearrange("p (j c) -> p j c", j=4)
                ACT(beta_a[:, t4, :], pba[:, :, 0:8], AF.Sigmoid, [('pf', 0)], [ka])
                TT('dve', tmp8[:], pba[:, :, 8:16], dtb[:, None, :].broadcast_to([128, 4, 8]), ALU.add, [('pf', 0), 'dtb'], ['tmp8'])
                ACT(tmp8[:], tmp8[:], AF.Exp, ['tmp8'], ['tmp8'])
                ACT(tmp8[:], tmp8[:], AF.Ln, ['tmp8', 'onecol'], ['tmp8'], bias=onecol[:, 0:1])
                TT('dve', g8_a[:, t4, :], tmp8[:], negA[:, None, :].broadcast_to([128, 4, 8]), ALU.mult, ['tmp8', 'negA'], [ka])
                g32 = g8_a[:, t4, :].rearrange("p j c -> p (j c)")
                MM(pf[0][:, 64:96], triU_f[:], g32, True, True, ['triU_f', ka], [('pf', 0)])
                MM(pf[0][:, 96:128], ones_f[:], g32, True, True, ['ones_f', ka], [('pf', 0)])
                pgc = pf[0][:, 64:96].rearrange("p (j c) -> p j c", j=4)
                pgl = pf[0][:, 96:128].rearrange("p (j c) -> p j c", j=4)
                CP('dve', gcs_a[:, t4, :], pgc, [('pf', 0)], [ka])
                ACT(eg_a[:, t4, :], pgc, AF.Exp, [('pf', 0)], [ka])
                ACT(egl_a[:, t4, :], pgl, AF.Exp, [('pf', 0)], [ka])
                TT('dve', smallf[:], pgl, gcs_a[:, t4, :], ALU.subtract, [('pf', 0), ka], ['smallf'])
                ACT(ekt_a[:, t4, :], smallf[:], AF.Exp, ['smallf'], [ka])
                TT('dve', bkg_a[:, t4, :], beta_a[:, t4, :], eg_a[:, t4, :], ALU.mult, [ka], [ka])

            def keys_tile(kt):
                g, j = kt // 4, kt % 4
                Xg = xTg[g % 2]
                kXg = ('xTg', g % 2)
                own = kt >= NTH
                ot = kt - NTH
                s2 = kt % 2
                tsl_ = slice(j * 128, (j + 1) * 128)
                cs = cs_all[:, kt, :]
                for kc in range(8):
                    MM(pf[1][:, 0:320], Xg[:, kc, tsl_], wA[:, kc, 3472:3792], kc == 0, kc == 7, [kXg] + wAk, [('pf', 1)])
                sq_ = ssq8[s2]
                ks = ('ssq8', s2)
                ACT(junk[:, 0:256], pf[1][:, 0:256], AF.Square, [('pf', 1)], ['junk', ks], accum=sq_[:, 0:1])
                ACT(sq_[:, 1:2], sq_[:, 0:1], AF.Ln, [ks, 'epscol'], [ks], bias=epscol[:, 0:1], scale=1.0 / 256)
                ACT(sq_[:, 2:3], sq_[:, 1:2], AF.Exp, [ks], [ks], scale=-0.5)
                STT('dve', ckvk[:, kt, :], pf[1][:, 0:256], sq_[:, 2:3], kvw[:], ALU.mult, ALU.mult, [('pf', 1), ks, 'kvw'], [('ckvk', kt)])
                kr_, j2_ = krr[s2], junk2[s2]
                TT('dve', kr_[:, 0:32], pf[1][:, 256:288], cs[:, 32:64], ALU.mult, [('pf', 1), 'cs_all'], [('krr', s2)])
                TT('dve', kr_[:, 32:64], pf[1][:, 288:320], cs[:, 32:64], ALU.mult, [('pf', 1), 'cs_all'], [('krr', s2)])
                TT('dve', j2_[:, 0:32], pf[1][:, 288:320], cs[:, 0:32], ALU.mult, [('pf', 1), 'cs_all'], [('junk2', s2)])
                TT('dve', j2_[:, 32:64], pf[1][:, 256:288], cs[:, 0:32], ALU.mult, [('pf', 1), 'cs_all'], [('junk2', s2)])
                TT('pool', kr_[:, 0:32], kr_[:, 0:32], j2_[:, 0:32], ALU.subtract, [('krr', s2), ('junk2', s2)], [('krr', s2)])
                TT('pool', kr_[:, 32:64], kr_[:, 32:64], j2_[:, 32:64], ALU.add, [('krr', s2), ('junk2', s2)], [('krr', s2)])
                CP('pool', krb[s2][:], kr_[:], [('krr', s2)], [('krb', s2)])
                for c in range(2):
                    TR(pb[s2][:, c * 128:(c + 1) * 128], ckvk[:, kt, c * 128:(c + 1) * 128], ident_b[:], [('ckvk', kt), 'ident_b'], [('pb', s2)])
                TR(pb[s2][0:64, 256:384], krb[s2][:], ident_b[:], [('krb', s2), 'ident_b'], [('pb', s2)])
                if own:
                    for kc in range(8):
                        MM(pf[0][:, 128:512], Xg[:, kc, tsl_], wA[:, kc, 3088:3472], kc == 0, kc == 7, [kXg] + wAk, [('pf', 0)])
                    ACT(junk[:, 0:384], pf[0][:, 128:512], AF.Square, [('pf', 0)], ['junk', ks], accum=sq_[:, 3:4])
                    ACT(sq_[:, 4:5], sq_[:, 3:4], AF.Ln, [ks, 'epscol'], [ks], bias=epscol[:, 0:1], scale=1.0 / 384)
                    ACT(sq_[:, 5:6], sq_[:, 4:5], AF.Exp, [ks], [ks], scale=-0.5)
                    TS('dve', cqb[s2][:], pf[0][:, 128:512], sq_[:, 5:6], ALU.mult, [('pf', 0), ks], [('cqb', s2)])
                    for c in range(3):
                        TR(pb[s2][:, 384 + c * 128:384 + (c + 1) * 128], cqb[s2][:, c * 128:(c + 1) * 128], ident_b[:], [('cqb', s2), 'ident_b'], [('pb', s2)])
                CP('act', ckvT[:, :, kt * 128:(kt + 1) * 128], pb[s2][:, 0:256].rearrange("p (c t) -> p c t", c=2), [('pb', s2)], [('ckvT', kt)])
                CP('act', krT[:, kt * 128:(kt + 1) * 128], pb[s2][0:64, 256:384], [('pb', s2)], [('krT', kt)])
                if own:
                    CP('act', cqs[s2][:], pb[s2][:, 384:768], [('pb', s2)], [('cqs', s2)])
                    DMA('sp', cq_d[ot], cqs[s2][:], [('cqs', s2)], [('cq_d', ot)], 'cqs%d' % s2)

            blocks = []
            for g in range(NGRP):
                own_g = g >= GP
                k_in_g = 0
                for h in range(8):
                    for kind in ('q', 'k', 'v'):
                        hist_only = False
                        if kind == 'q' and not own_g:
                            if g != GP - 1:
                                continue
                            hist_only = True
                        blocks.append(dict(g=g, h=h, kind=kind, hist_only=hist_only, blk={'q': 0, 'k': 8, 'v': 16}[kind] + h, idx=len(blocks), k_in_g=k_in_g))
                        k_in_g += 1

            def stage(B, s):
                g, h, kind, blk = B['g'], B['h'], B['kind'], B['blk']
                i_ = B['idx']
                rcb, rct, rsg, rsl, rsq, rln, rob = i_ % D_CB, i_ % D_CT, i_ % D_SG, i_ % D_SL, i_ % D_SQ, i_ % D_LN, i_ % D_OB
                Xg = xTg[g % 2]
                kXg = ('xTg', g % 2)
                gsl = slice(g * 512, (g + 1) * 512)
                bP = 2 + i_ % 2
                bO = 4 + i_ % 2
                if s == 0:
                    if B['k_in_g'] == 0:
                        hs_ = P.deferred(lambda: gates(g))
                        pending.extend(hs_)
                        pending_grp.extend([g] * len(hs_))
                    if B['k_in_g'] in (2, 5, 8, 11):
                        kt_ = g * 4 + (B['k_in_g'] - 2) // 3
                        hs_ = P.deferred(lambda: keys_tile(kt_))
                        pending.extend(hs_)
                        pending_grp.extend([g] * len(hs_))
                    if B['k_in_g'] == 13 and g + 1 < NGRP:
                        while pending:
                            pending.pop(0)()
                            pending_grp.pop(0)
                        x_prologue(g + 1)
                    for _ in range(3):
                        if pending:
                            pending.pop(0)()
                            pending_grp.pop(0)
                    for kc in range(8):
                        MM(pf[bP][:], wA[:, kc, blk * 128:(blk + 1) * 128], Xg[:, kc, :], kc == 0, kc == 7, [kXg, ('wA', wA_grp[blk // 2])], [('pf', bP)])
                    return
                if s == 1:
                    CP('act', cb[rcb][:, 3:515], pf[bP][:], [('pf', bP)], [('cb', rcb)])
                    CP('pool', cb[rcb][:, 0:3], hist[:, blk, :], [('hist', blk)], [('cb', rcb)])
                    CP('pool', hist[:, blk, :], cb[rcb][:, 512:515], [('cb', rcb)], [('hist', blk)])
                    return
                if B['hist_only']:
                    return
                if s == 2:
                    TS('dve', ct[rct][:], cb[rcb][:, 0:512], cw[:, blk, 0:1], ALU.mult, [('cb', rcb), 'cw'], [('ct', rct)])
                    for i in range(1, 4):
                        STT('dve', ct[rct][:], cb[rcb][:, i:i + 512], cw[:, blk, i:i + 1], ct[rct][:], ALU.mult, ALU.add, [('cb', rcb), 'cw', ('ct', rct)], [('ct', rct)])
                elif s == 3:
                    ACT(sg[rsg][:], ct[rct][:], AF.Exp, [('ct', rct)], [('sg', rsg)], scale=-1.0)
                    ACT(sg[rsg][:], sg[rsg][:], AF.Ln, [('sg', rsg), 'onecol'], [('sg', rsg)], bias=onecol[:, 0:1])
                    ACT(sg[rsg][:], sg[rsg][:], AF.Exp, [('sg', rsg)], [('sg', rsg)], scale=-1.0)
                elif s == 4:
                    if kind == 'v':
                        TT('dve', ob[rob][:], ct[rct][:], sg[rsg][:], ALU.mult, [('ct', rct), ('sg', rsg)], [('ob', rob)])
                        DMA('sp', vn_d[h, :, gsl], ob[rob][:], [('ob', rob)], [('vn_d', g)], 'ob%d' % rob)
                        return
                    TT('dve', sl[rsl][:], ct[rct][:], sg[rsg][:], ALU.mult, [('ct', rct), ('sg', rsg)], [('sl', rsl)])
                elif kind == 'v':
                    return
                elif s == 5:
                    TT('pool', sq[rsq][:], sl[rsl][:], sl[rsl][:], ALU.mult, [('sl', rsl)], [('sq', rsq)])
                elif s == 6:
                    MM(pf[bO][:], ones_b[:], sq[rsq][:], True, True, ['ones_b', ('sq', rsq)], [('pf', bO)])
                elif s == 7:
                    ACT(lnr[rln][:], pf[bO][:], AF.Ln, [('pf', bO), 'epscol'], [('lnr', rln)], bias=epscol[:, 0:1])
                    ACT(lnr[rln][:], lnr[rln][:], AF.Exp, [('lnr', rln)], [('lnr', rln)], scale=-0.5)
                elif s == 8:
                    if kind == 'q':
                        STT('dve', ob[rob][:], sl[rsl][:], 128.0 ** -0.5, lnr[rln][:], ALU.mult, ALU.mult, [('sl', rsl), ('lnr', rln)], [('ob', rob)])
                        DMA('sp', qn_d[h, :, (g - GP) * 512:(g - GP + 1) * 512], ob[rob][:], [('ob', rob)], [('qn_d', g)], 'ob%d' % rob)
                    else:
                        TT('dve', ob[rob][:], sl[rsl][:], lnr[rln][:], ALU.mult, [('sl', rsl), ('lnr', rln)], [('ob', rob)])
                        DMA('sp', kn_d[h, :, gsl], ob[rob][:], [('ob', rob)], [('kn_d', g)], 'ob%d' % rob)

            pending = []
            pending_grp = []
            x_prologue(0)
            NS = 9
            for step in range(len(blocks) + NS - 1):
                for s in reversed(range(NS)):
                    i = step - s
                    if 0 <= i < len(blocks):
                        stage(blocks[i], s)
            while pending:
                pending.pop(0)()
            barrier([])

        with ExitStack() as ea:
            NSL = 2

            def per_slot(name, shape, dt=F32):
                return [sbt(ea, "%s_s%d" % (name, i), shape, dt) for i in range(NSL)]
            NIN = 4
            kin = [sbt(ea, "kin_s%d" % i, [128, 8, 128], BF16) for i in range(NIN)]
            vin = [sbt(ea, "vin_s%d" % i, [128, 8, 128], BF16) for i in range(NIN)]
            qin = [sbt(ea, "qin_s%d" % i, [128, 8, 128], BF16) for i in range(NIN)]
            rhsg = sbt(ea, "rhsg", [128, 8, 128])
            negones = sbt(ea, "negones", [128, 128])
            MEMSET('pool', negones[:], -1.0, ['negones'])
            Kbg = per_slot("Kbg", [128, 8, 128], BF16)
            Ktt = per_slot("Ktt", [128, 8, 128], BF16)
            Vb = per_slot("Vb", [128, 8, 128], BF16)
            Lf = [sbt(ea, "Lf%d" % i, [128, 4, 128]) for i in range(2)]
            LS = [[[sbt(ea, "LS%d_%d_%d" % (s_, hb_, c_), [128, 2, 4, 128], BF16) for c_ in range(2)] for hb_ in range(2)] for s_ in range(NSL)]
            MS = [[[sbt(ea, "MS%d_%d_%d" % (s_, hb_, c_), [128, 2, 4, 128], BF16) for c_ in range(2)] for hb_ in range(2)] for s_ in range(NSL)]
            YS = [[[sbt(ea, "YS%d_%d_%d" % (s_, hb_, c_), [128, 2, 4, 128], BF16) for c_ in range(2)] for hb_ in range(2)] for s_ in range(NSL)]
            YF = [[[sbt(ea, "YF%d_%d_%d" % (s_, hb_, c_), [128, 4, 128]) for c_ in range(2)] for hb_ in range(2)] for s_ in range(NSL)]
            decL = [sbt(ea, "decL%d" % i, [128, 4, 128]) for i in range(2)]
            decT = [sbt(ea, "decT%d" % i, [128, 4, 128]) for i in range(2)]
            ATb = per_slot("ATb", [128, 8, 128], BF16)
            Ub = per_slot("Ub", [128, 8, 128])
            WTb = per_slot("WTb", [128, 8, 128], BF16)
            Vn = sbt(ea, "Vn", [128, 8, 128], BF16)
            Sf = sbt(ea, "Sf", [128, 8, 128])
            Sb = sbt(ea, "Sb", [128, 8, 128], BF16)
            o1 = sbt(ea, "o1", [128, 4, 128])
            otile = [sbt(ea, "otile", [128, 8, 128])] * NSL
            MEMSET('pool', Sf[:], 0.0, ['Sf0', 'Sf1'])
            MEMSET('pool', Sb[:], 0.0, ['Sb0', 'Sb1'])
            bankc = [0]

            def nb6():
                b = bankc[0] % 6
                bankc[0] += 1
                return b, ('pf', b)

            def f4(t, hb):
                return t[:, hb * 4:(hb + 1) * 4, :]

            def p4(b):
                return pf[b][:].rearrange("p (h t) -> p h t", h=4)

            def bc4(ap2):
                return ap2[:, None, :].broadcast_to([128, 4, 128])

            def col4(arr, kt, hb):
                return arr[:, kt, hb * 4:(hb + 1) * 4, None].broadcast_to([128, 4, 128])

            def a2_loads(kt):
                own = kt >= NTH
                ot = kt - NTH
                sl = kt % NIN
                tok = slice(kt * 128, (kt + 1) * 128)
                DMA('sp', kin[sl][:], kn_d[:, :, tok].rearrange("h d t -> d h t"), [('kn_d', kt // 4)], [('kin', sl)], 'kin%d' % sl)
                DMA('sp', vin[sl][:], vn_d[:, :, tok].rearrange("h d t -> d h t"), [('vn_d', kt // 4)], [('vin', sl)], 'vin%d' % sl)
                if own:
                    DMA('sp', qin[sl][:], qn_d[:, :, ot * 128:(ot + 1) * 128].rearrange("h d t -> d h t"), [('qn_d', kt // 4)], [('qin', sl)], 'qin%d' % sl)

            def a2_prep(kt):
                own = kt >= NTH
                sl = kt % NSL
                ss = 's%d' % sl
                ka = ('gates', kt // 4)
                TT('pool', rhsg[:], triU_f[:, None, :].broadcast_to([128, 8, 128]), g8_a[:, kt, :, None].broadcast_to([128, 8, 128]), ALU.mult, ['triU_f', ka], ['rhsg'])
                hops = []
                for hb in range(2):
                    hops.append(a2_prep_hops(kt, hb))
                for hi in range(len(hops[0])):
                    for hp in hops:
                        hp[hi]()

            def a2_prep_hops(kt, hb):
                own = kt >= NTH
                sl = kt % NSL
                si = kt % NIN
                ss = 's%d' % sl
                ka = ('gates', kt // 4)
                kh = 'h%d' % hb + ss
                dL, dT, Lf_ = decL[hb], decT[hb], Lf[hb]
                kdL, kdT, kLf = ('decL', hb), ('decT', hb), ('Lf', hb)
                L0, M0, Y0, YF0 = LS[sl][hb][0], MS[sl][hb][0], YS[sl][hb][0], YF[sl][hb][0]
                st = {}
                pkb = ('pb', hb)

                def p1():
                    for i in range(4):
                        TR(pb[hb][:, i * 128:(i + 1) * 128], kin[si][:, hb * 4 + i, :], ident_b[:], [('kin', si), 'ident_b'], [pkb])
                    for i in range(4):
                        TR(pb[hb][:, 512 + i * 128:512 + (i + 1) * 128], vin[si][:, hb * 4 + i, :], ident_b[:], [('vin', si), 'ident_b'], [pkb])
                    st['bkk'], st['kkk'] = nb6()
                    for i in range(4):
                        h = hb * 4 + i
                        MM(pf[st['bkk']][:, i * 128:(i + 1) * 128], kin[si][:, h, :], kin[si][:, h, :], True, True, [('kin', si)], [st['kkk']])
                    st['bdf'], st['kdf'] = nb6()
                    for i in range(4):
                        h = hb * 4 + i
                        MM(pf[st['bdf']][:, i * 128:(i + 1) * 128], negones[:], rhsg[:, h, :], True, False, ['negones', 'rhsg'], [st['kdf']])
                        MM(pf[st['bdf']][:, i * 128:(i + 1) * 128], rhsg[:, h, :], ones_f[:], False, True, ['ones_f', 'rhsg'], [st['kdf']])

                def p2():
                    pk = pb[hb][:, 0:512].rearrange("p (h t) -> p h t", h=4)
                    pv_ = pb[hb][:, 512:1024].rearrange("p (h t) -> p h t", h=4)
                    TT('dve', dL[:], p4(st['bdf']), bc4(maskLs), ALU.add, [st['kdf'], 'maskLs'], [kdL])
                    if own:
                        TT('dve', dT[:], bc4(maskU), p4(st['bdf']), ALU.subtract, [st['kdf'], 'maskU'], [kdT])
                    TT('dve', f4(Kbg[sl], hb), pk, col4(bkg_a, kt, hb), ALU.mult, [pkb, ka], ['Kbg' + kh])
                    TT('pool' if False else 'dve', f4(Ktt[sl], hb), pk, col4(ekt_a, kt, hb), ALU.mult, [pkb, ka], ['Ktt' + kh])
                    TT('dve', f4(Vb[sl], hb), pv_, col4(beta_a, kt, hb), ALU.mult, [pkb, ka], ['Vb' + kh])

                def p3():
                    ACT(dL[:], dL[:], AF.Exp, [kdL], [kdL])
                    if own:
                        ACT(dT[:], dT[:], AF.Exp, [kdT], [kdT])

                def p4_():
                    TT('dve', dL[:], dL[:], col4(beta_a, kt, hb), ALU.mult, [kdL, ka], [kdL])
                    TT('dve', Lf_[:], p4(st['bkk']), dL[:], ALU.mult, [st['kkk'], kdL], [kLf])

                def p5():
                    CP('act', L0[:, 0], Lf_[:], [kLf], ['L0' + kh])

                def p6():
                    TT('dve', L0[:, 1], Lf_[:], L0[:, 0], ALU.subtract, [kLf, 'L0' + kh], ['L0' + kh])

                def p7():
                    for a_ in range(2):
                        for i in range(4):
                            TR(pb[hb][:, a_ * 512 + i * 128:a_ * 512 + (i + 1) * 128], L0[:, a_, i, :], ident_b[:], ['L0' + kh, 'ident_b'], [pkb])
                    if own:
                        st['bkq'], st['kkq'] = nb6()
                        for i in range(4):
                            h = hb * 4 + i
                            MM(pf[st['bkq']][:, i * 128:(i + 1) * 128], kin[si][:, h, :], qin[si][:, h, :], True, True, [('kin', si), ('qin', si)], [st['kkq']])

                def p8():
                    CP('act', M0[:].rearrange("p a h t -> p (a h t)"), pb[hb][:, 0:1024], [pkb], ['M0' + kh])
                    TT('dve', YF0[:], bc4(ident_f), pb[hb][:, 0:512].rearrange("p (h t) -> p h t", h=4), ALU.subtract, ['ident_f', pkb], ['YF0' + kh])
                    TT('dve', YF0[:], YF0[:], pb[hb][:, 512:1024].rearrange("p (h t) -> p h t", h=4), ALU.subtract, ['YF0' + kh, pkb], ['YF0' + kh])
                    if own:
                        TT('dve', f4(ATb[sl], hb), p4(st['bkq']), dT[:], ALU.mult, [st['kkq'], kdT], ['ATb' + kh])

                def p9():
                    CP('act', Y0[:, 0], YF0[:], ['YF0' + kh], ['Y0' + kh])

                def p10():
                    TT('pool', Y0[:, 1], YF0[:], Y0[:, 0], ALU.subtract, ['YF0' + kh, 'Y0' + kh], ['Y0' + kh])
                return [p1, p2, p3, p4_, p5, p6, p7, p8, p9, p10]

            def a2_level_hops(kt, lev, hb):
                sl = kt % NSL
                kh = 'h%d' % hb + 's%d' % sl
                cur, nxt = lev % 2, (lev + 1) % 2
                Lc, Mc, Yc, YFc = 'L%d' % cur + kh, 'M%d' % cur + kh, 'Y%d' % cur + kh, 'YF%d' % cur + kh
                Ln_, Mn_, Yn_, YFn = 'L%d' % nxt + kh, 'M%d' % nxt + kh, 'Y%d' % nxt + kh, 'YF%d' % nxt + kh
                Lc_t, Mc_t, Yc_t = LS[sl][hb][cur], MS[sl][hb][cur], YS[sl][hb][cur]
                Ln_t, Mn_t, Yn_t = LS[sl][hb][nxt], MS[sl][hb][nxt], YS[sl][hb][nxt]
                st = {}

                def h1():
                    st['bl'], st['kl'] = nb6()
                    for i in range(4):
                        o_ = pf[st['bl']][:, i * 128:(i + 1) * 128]
                        MM(o_, Mc_t[:, 0, i, :], Lc_t[:, 0, i, :], True, False, [Mc, Lc], [st['kl']])
                        MM(o_, Mc_t[:, 0, i, :], Lc_t[:, 1, i, :], False, False, [Mc, Lc], [st['kl']])
                        MM(o_, Mc_t[:, 1, i, :], Lc_t[:, 0, i, :], False, True, [Mc, Lc], [st['kl']])

                def h2():
                    CP('act', Ln_t[:, 0], p4(st['bl']), [st['kl']], [Ln_])

                def h3():
                    TT('dve', Ln_t[:, 1], p4(st['bl']), Ln_t[:, 0], ALU.subtract, [st['kl'], Ln_], [Ln_])

                def h4():
                    if lev < 5:
                        for a_ in range(2):
                            for i in range(4):
                                TR(pb[hb][:, a_ * 512 + i * 128:a_ * 512 + (i + 1) * 128], Ln_t[:, a_, i, :], ident_b[:], [Ln_, 'ident_b'], [('pb', hb)])
                    st['by'], st['ky'] = nb6()
                    for i in range(4):
                        o_ = pf[st['by']][:, i * 128:(i + 1) * 128]
                        MM(o_, Ln_t[:, 0, i, :], Yc_t[:, 0, i, :], True, False, [Ln_, Yc], [st['ky']])
                        MM(o_, Ln_t[:, 0, i, :], Yc_t[:, 1, i, :], False, False, [Ln_, Yc], [st['ky']])
                        MM(o_, Ln_t[:, 1, i, :], Yc_t[:, 0, i, :], False, True, [Ln_, Yc], [st['ky']])

                def h5():
                    if lev < 5:
                        CP('act', Mn_t[:, 0].rearrange("p h t -> p (h t)"), pb[hb][:, 0:512], [('pb', hb)], [Mn_])
                        CP('dve', Mn_t[:, 1].rearrange("p h t -> p (h t)"), pb[hb][:, 512:1024], [('pb', hb)], [Mn_])
                    TT('dve', YF[sl][hb][nxt][:], p4(st['by']), YF[sl][hb][cur][:], ALU.add, [st['ky'], YFc], [YFn])

                def h6():
                    CP('act', Yn_t[:, 0], YF[sl][hb][nxt][:], [YFn], [Yn_])

                def h7():
                    if lev < 5:
                        TT('pool', Yn_t[:, 1], YF[sl][hb][nxt][:], Yn_t[:, 0], ALU.subtract, [YFn, Yn_], [Yn_])
                return [h1, h2, h3, h4, h5, h6, h7]

            def a2_finish(kt):
                sl = kt % NSL
                for hb in range(2):
                    kh = 'h%d' % hb + 's%d' % sl
                    TTv = YS[sl][hb][0]
                    kT_ = 'Y0' + kh
                    bu, ku = nb6()
                    for i in range(4):
                        h = hb * 4 + i
                        MM(pf[bu][:, i * 128:(i + 1) * 128], TTv[:, 0, i, :], Vb[sl][:, h, :], True, True, [kT_, 'Vb' + kh], [ku])
                    CP('act', f4(Ub[sl], hb), p4(bu), [ku], ['Ub' + kh])
                    bw, kw_ = nb6()
                    for i in range(4):
                        h = hb * 4 + i
                        MM(pf[bw][:, i * 128:(i + 1) * 128], Kbg[sl][:, h, :], TTv[:, 0, i, :], True, True, [kT_, 'Kbg' + kh], [kw_])
                    CP('dve', f4(WTb[sl], hb), p4(bw), [kw_], ['WTb' + kh])

            def a2_recur(kt):
                own = kt >= NTH
                ot = kt - NTH
                sl = kt % NSL
                si = kt % NIN
                ka = ('gates', kt // 4)
                for hb in range(2):
                    kh = 'h%d' % hb + 's%d' % sl
                    bws, kws = nb6()
                    for i in range(4):
                        h = hb * 4 + i
                        MM(pf[bws][:, i * 128:(i + 1) * 128], WTb[sl][:, h, :], Sb[:, h, :], True, True, ['WTb' + kh, 'Sb%d' % hb], [kws])
                    TT('dve', f4(Vn, hb), f4(Ub[sl], hb), p4(bws), ALU.subtract, ['Ub' + kh, kws], ['Vn%d' % hb])
                    if own:
                        bq1, kq1 = nb6()
                        for i in range(4):
                            h = hb * 4 + i
                            MM(pf[bq1][:, i * 128:(i + 1) * 128], qin[si][:, h, :], Sb[:, h, :], True, True, [('qin', si), 'Sb%d' % hb], [kq1])
                        bq2, kq2 = nb6()
                        for i in range(4):
                            h = hb * 4 + i
                            MM(pf[bq2][:, i * 128:(i + 1) * 128], ATb[sl][:, h, :], Vn[:, h, :], True, True, ['ATb' + kh, 'Vn%d' % hb], [kq2])
                        TT('dve', o1[:], p4(bq1), col4(eg_a, kt, hb), ALU.mult, [kq1, ka], ['o1'])
                        TT('dve', f4(otile[sl], hb), o1[:], p4(bq2), ALU.add, ['o1', kq2], ['otile'])
                    bkv, kkv = nb6()
                    for i in range(4):
                        h = hb * 4 + i
                        MM(pf[bkv][:, i * 128:(i + 1) * 128], Ktt[sl][:, h, :], Vn[:, h, :], True, True, ['Ktt' + kh, 'Vn%d' % hb], [kkv])
                    TT('pool', f4(Sf, hb), f4(Sf, hb), col4(egl_a, kt, hb), ALU.mult, ['Sf%d' % hb, ka], ['Sf%d' % hb])
                    TT('dve', f4(Sf, hb), f4(Sf, hb), p4(bkv), ALU.add, ['Sf%d' % hb, kkv], ['Sf%d' % hb])
                    CP('act', f4(Sb, hb), f4(Sf, hb), ['Sf%d' % hb], ['Sb%d' % hb])
                if own:
                    DMA('sp', odn_d[ot * 128:(ot + 1) * 128, :], otile[sl][:].rearrange("p h t -> p (h t)"), ['otile'], [('odn_d', ot)], 'odn_out')

            for kt in range(min(4, NT)):
                a2_loads(kt)
            for p_ in range(NT // 2):
                t0, t1 = 2 * p_, 2 * p_ + 1
                a2_prep(t0)
                a2_prep(t1)
                for lev in range(6):
                    hops = [a2_level_hops(kt, lev, hb) for kt in (t0, t1) for hb in range(2)]
                    for hi in (0, 1, 2):
                        for hp in hops:
                            hp[hi]()
                    for pair in ((0, 1), (2, 3)):
                        for si_ in pair:
                            hops[si_][3]()
                        for si_ in pair:
                            hops[si_][4]()
                    for hi in (5, 6):
                        for hp in hops:
                            hp[hi]()
                a2_finish(t0)
                a2_finish(t1)
                a2_recur(t0)
                a2_recur(t1)
                for kt in (t0 + 4, t1 + 4):
                    if kt < NT:
                        a2_loads(kt)
            barrier([])

        SCALE = 192.0 ** -0.5
        with ExitStack() as eb:
            Wabs = sbt(eb, "Wabs", [128, 3, 8, 256], BF16)
            wuqr = sbt(eb, "wuqr", [128, 3, 512], BF16)
            Wov = sbt(eb, "Wov", [128, 2, 8, 1024], BF16)
            with ExitStack() as e1:
                wuv_s = sbt(e1, "wuv_s", [128, 2, 1024])
                wbm_s = sbt(e1, "wbm_s", [128, 8, 1024])
                VT = sbt(e1, "VTt", [128, 256])
                DMA('sp', wuv_s[:], w_uv.rearrange("(c p) n -> p c n", p=128), [], ['wuv_s'], 'c_wuv')
                DMA('sp', wbm_s[:], w_br_mla.rearrange("(c p) n -> p c n", p=128), [], ['wbm_s'], 'c_wbm')
                for h in range(8):
                    for c in range(2):
                        TR(pf[1][:, 256 + c * 128:256 + (c + 1) * 128], wuv_s[:, c, h * 128:(h + 1) * 128], ident_f[:], ['wuv_s', 'ident_f'], [('pf', 1)])
                    CP('dve', VT[:], pf[1][:, 256:512], [('pf', 1)], ['VT'])
                    for c in range(2):
                        for n in range(2):
                            MM(pf[3][:], VT[:, c * 128:(c + 1) * 128], wbm_s[:, h, n * 512:(n + 1) * 512], True, True, ['VT', 'wbm_s'], [('pf', 3)])
                            CP('dve', Wov[:, c, h, n * 512:(n + 1) * 512], pf[3][:], [('pf', 3)], ['Wov'])
                wuq_s = sbt(e1, "wuq_s", [128, 3, 1536])
                wuk_s = sbt(e1, "wuk_s", [128, 2, 1024])
                AT = sbt(e1, "ATt", [128, 384])
                BT = sbt(e1, "BTt", [128, 256])
                DMA('sp', wuq_s[:], w_uq.rearrange("(c p) n -> p c n", p=128), [], ['wuq_s'], 'c_wuq')
                DMA('sp', wuk_s[:], w_uk.rearrange("(c p) n -> p c n", p=128), [], ['wuk_s'], 'c_wuk')
                for c in range(3):
                    TS('dve', wuq_s[:, c, :], wuq_s[:, c, :], qnw[:, c:c + 1], ALU.mult, ['wuq_s', 'qnw'], ['wuq_s'])
                for c in range(3):
                    CP('dve', wuqr[:, c, :].rearrange("p (h r) -> p h r", h=8), wuq_s[:, c, :].rearrange("p (h d) -> p h d", h=8)[:, :, 128:192], ['wuq_s'], ['wuqr'])
                for h in range(8):
                    for c in range(3):
                        TR(pf[0][:, c * 128:(c + 1) * 128], wuq_s[:, c, h * 192:h * 192 + 128], ident_f[:], ['wuq_s', 'ident_f'], [('pf', 0)])
                    CP('act', AT[:], pf[0][:, 0:384], [('pf', 0)], ['AT'])
                    for c in range(2):
                        TR(pf[1][:, c * 128:(c + 1) * 128], wuk_s[:, c, h * 128:(h + 1) * 128], ident_f[:], ['wuk_s', 'ident_f'], [('pf', 1)])
                    CP('dve', BT[:], pf[1][:, 0:256], [('pf', 1)], ['BT'])
                    for c in range(3):
                        MM(pf[2][:, 0:256], AT[:, c * 128:(c + 1) * 128], BT[:], True, True, ['AT', 'BT'], [('pf', 2)])
                        CP('act', Wabs[:, c, h, :], pf[2][:, 0:256], [('pf', 2)], ['Wabs'])

            barrier([])
            cqt = [sbt(eb, "cqt%d" % i, [128, 3, 128], BF16) for i in range(2)]
            qlT2 = [sbt(eb, "qlT%d" % i, [128, 2, 8, 128], BF16) for i in range(2)]
            qrp = sbt(eb, "qrp", [128, 8, 64])
            qrt = sbt(eb, "qrt", [128, 8, 64])
            qrb = sbt(eb, "qrb", [128, 8, 64], BF16)
            qrT2 = [sbt(eb, "qrT%d" % i, [64, 8, 128], BF16) for i in range(2)]
            PT = [sbt(eb, "PT%d" % i, [128, 512], BF16) for i in range(2)]
            rec = sbt(eb, "rec", [128, 512])
            olT = sbt(eb, "olT", [128, 2, 8, 128], BF16)
            ym = sbt(eb, "ym", [128, 1024])

            def cq_load(qt):
                if qt < NTH:
                    DMA('sp', cqt[qt % 2][:].rearrange("p c t -> p (c t)"), cq_d[qt], [('cq_d', qt)], [('cqt', qt % 2)], 'cqt%d' % (qt % 2))

            def prologue_rounds(qt):
                u = qt % 2
                cq_ = cqt[u]
                kq = ('cqt', u)
                rounds = []
                for cc in range(2):
                    for hq in range(2):
                        def rnd(cc=cc, hq=hq):
                            for i in range(4):
                                h = hq * 4 + i
                                for c in range(3):
                                    MM(pf[5][:, i * 128:(i + 1) * 128], Wabs[:, c, h, cc * 128:(cc + 1) * 128], cq_[:, c, :], c == 0, c == 2, ['Wabs', kq], [('pf', 5)])
                            CP('dve', qlT2[u][:, cc, hq * 4:(hq + 1) * 4, :].rearrange("p h t -> p (h t)"), pf[5][:], [('pf', 5)], [('qlT', u)])
                        rounds.append(rnd)

                def rope():
                    for c in range(3):
                        MM(pf[5][:], cq_[:, c, :], wuqr[:, c, :], c == 0, c == 2, ['wuqr', kq], [('pf', 5)])
                    pv = pf[5][:].rearrange("p (h r) -> p h r", h=8)
                    sinb = cs_all[:, NTH + qt, None, 0:32].broadcast_to([128, 8, 32])
                    cosb = cs_all[:, NTH + qt, None, 32:64].broadcast_to([128, 8, 32])
                    TT('dve', qrp[:, :, 0:32], pv[:, :, 0:32], cosb, ALU.mult, [('pf', 5), 'cs_all'], ['qrp'])
                    TT('dve', qrp[:, :, 32:64], pv[:, :, 32:64], cosb, ALU.mult, [('pf', 5), 'cs_all'], ['qrp'])
                    TT('dve', qrt[:, :, 0:32], pv[:, :, 32:64], sinb, ALU.mult, [('pf', 5), 'cs_all'], ['qrt'])
                    TT('dve', qrt[:, :, 32:64], pv[:, :, 0:32], sinb, ALU.mult, [('pf', 5), 'cs_all'], ['qrt'])
                    TT('dve', qrb[:, :, 0:32], qrp[:, :, 0:32], qrt[:, :, 0:32], ALU.subtract, ['qrp', 'qrt'], ['qrb'])
                    TT('dve', qrb[:, :, 32:64], qrp[:, :, 32:64], qrt[:, :, 32:64], ALU.add, ['qrp', 'qrt'], ['qrb'])
                    for h in range(8):
                        TR(pb[0][0:64, h * 128:(h + 1) * 128], qrb[:, h, :], ident_b[:], ['qrb', 'ident_b'], [('pb', 0)])
                    CP('dve', qrT2[u][:].rearrange("p h t -> p (h t)"), pb[0][0:64, 0:1024], [('pb', 0)], [('qrT', u)])
                rounds.append(rope)
                return rounds

            cq_load(0)
            cq_load(1)
            for r_ in prologue_rounds(0):
                r_()
            for qt in range(NTH):
                qlT = qlT2[qt % 2]
                qrT = qrT2[qt % 2]
                kql, kqr = ('qlT', qt % 2), ('qrT', qt % 2)
                nxt_rounds = prologue_rounds(qt + 1) if qt + 1 < NTH else []
                nkb = NTH + qt + 1
                for hg in range(2):
                    hs = slice(hg * 4, (hg + 1) * 4)
                    def scores(kb):
                        sp_ = kb % 2
                        ksl = slice(kb * 128, (kb + 1) * 128)
                        for cc in range(2):
                            MM(pf[sp_][:], ckvT[:, cc, ksl], qlT[:, cc, hs, :].rearrange("p h t -> p (h t)"), cc == 0, False, [('ckvT', kb), kql], [('pf', sp_)])
                        MM(pf[sp_][:], krT[:, ksl], qrT[:, hs, :].rearrange("p h t -> p (h t)"), False, True, [('krT', kb), kqr], [('pf', sp_)])

                    scores(0)
                    for kb in range(nkb):
                        sp_ = kb % 2
                        if kb + 1 < nkb:
                            scores(kb + 1)
                        ACT(PT[sp_][:], pf[sp_][:], AF.Exp, [('pf', sp_), 'kbias'], [('PT', sp_)], bias=kbias[:, kb:kb + 1], scale=SCALE)
                        if kb == nkb - 1:
                            TT('dve', PT[sp_][:].rearrange("p (h t) -> p h t", h=4), PT[sp_][:].rearrange("p (h t) -> p h t", h=4), caus_b[:, None, :].broadcast_to([128, 4, 128]), ALU.mult, [('PT', sp_), 'caus_b'], [('PT', sp_)])
                        for cc in range(2):
                            MM(pf[2 + cc][:], ckvk[:, kb, cc * 128:(cc + 1) * 128], PT[sp_][:], kb == 0, kb == nkb - 1, [('ckvk', kb), ('PT', sp_)], [('pf', 2 + cc)])
                        MM(pf[4][:], ones_b[:], PT[sp_][:], kb == 0, kb == nkb - 1, ['ones_b', ('PT', sp_)], [('pf', 4)])
                        if hg == 1 and kb % 3 == 1 and nxt_rounds:
                            nxt_rounds.pop(0)()
                    ACT(rec[:], pf[4][:], AF.Ln, [('pf', 4)], ['rec'])
                    ACT(rec[:], rec[:], AF.Exp, ['rec'], ['rec'], scale=-1.0)
                    for cc in range(2):
                        TT('dve', olT[:, cc, hs, :].rearrange("p h t -> p (h t)"), pf[2 + cc][:], rec[:], ALU.mult, [('pf', 2 + cc), 'rec'], ['olT'])
                while nxt_rounds:
                    nxt_rounds.pop(0)()
                for n in range(2):
                    i = 0
                    for h in range(8):
                        for cc in range(2):
                            MM(pf[n][:], olT[:, cc, h, :], Wov[:, cc, h, n * 512:(n + 1) * 512], i == 0, i == 15, ['olT', 'Wov'], [('pf', n)])
                            i += 1
                    CP('act' if n else 'dve', ym[:, n * 512:(n + 1) * 512], pf[n][:], [('pf', n)], ['ym'])
                DMA('sp', ymla_d[qt * 128:(qt + 1) * 128, :], ym[:], ['ym'], [('ymla_d', qt)], 'ym_out')
                cq_load(qt + 2)
            barrier([])
        eAB.close()

        def load_w(es_, name, src, kchunks, ncols, c0=0, colblk=512, groups=None):
            t = sbt(es_, name, [128, kchunks, ncols], BF16)
            nblk = ncols // colblk
            ngrp = len(groups) if groups is not None else nblk
            if not hasattr(P, 'alias'):
                P.alias = {}
            for gi in range(ngrp):
                P.alias[(name, gi)] = [(name, 'ch0'), (name, 'ch1')]
            for kc in range(kchunks):
                DMA('pool', t[:, kc, :], src[kc * 128:(kc + 1) * 128, c0:c0 + ncols], [], [(name, 'ch%d' % (kc % 2))], 'w_%s_%d' % (name, kc % 2))
            return t

        def bcast_vec(es_, name, src, n):
            t = sbt(es_, name, [128, n])
            DMA('sp', t[:], src.partition_broadcast(128), [], [name], 'v_' + name)
            return t

        def layer_norm(x_ap, g_t, b_t, out_ap, stats, mv, rstd, rkey, wkey, gk, bk, sfx):
            xr = x_ap.rearrange("p (c f) -> p c f", c=2)
            for c in range(2):
                P.op('dve', (lambda o_, i_: (lambda e: e.bn_stats(out=o_, in_=i_)))(stats[:, c, :], xr[:, c, :]), reads=[rkey], writes=['ln_stats' + sfx])
            P.op('dve', lambda e: e.bn_aggr(out=mv[:], in_=stats[:]), reads=['ln_stats' + sfx], writes=['ln_mv' + sfx])
            ACT(rstd[:, 0:1], mv[:, 1:2], AF.Ln, ['ln_mv' + sfx, 'epscol'], ['ln_rstd' + sfx], bias=epscol[:, 1:2])
            ACT(rstd[:, 0:1], rstd[:, 0:1], AF.Exp, ['ln_rstd' + sfx], ['ln_rstd' + sfx], scale=-0.5)
            TS('dve', x_ap, x_ap, mv[:, 0:1], ALU.subtract, [rkey, 'ln_mv' + sfx, 'ln_rstd' + sfx], [rkey], s2=rstd[:, 0:1], op1=ALU.mult)
            TT('dve', x_ap, x_ap, g_t[:], ALU.mult, [rkey, gk], [rkey])
            TT('dve', out_ap, x_ap, b_t[:], ALU.add, [rkey, bk], [wkey])

        with ExitStack() as ec:
            wg = load_w(ec, "wg", w_in, 8, 2048, 4816)
            wz = load_w(ec, "wz", w_in, 8, 1024, 3072)
            wbd = load_w(ec, "wbd", w_br_dn, 8, 1024)
            wo = load_w(ec, "wo", w_o, 8, 1024)
            g1 = bcast_vec(ec, "g1", ln1_g, 1024)
            b1 = bcast_vec(ec, "b1", ln1_b, 1024)

            def two(name, shape, dt=F32):
                return [sbt(ec, "%s_%d" % (name, i), shape, dt) for i in range(2)]
            zs = two("zs", [128, 1024])
            orms = two("orms", [128, 8])
            odf = two("odf", [128, 1024])
            junkc = sbt(ec, "junkc", [128, 128])
            xf = two("xf", [128, 1024])
            xbb = two("xbb", [128, 1024], BF16)
            xTc = two("xTc", [128, 8, 128], BF16)
            odt = two("odt", [128, 1024], BF16)
            odT = two("odT", [128, 8, 128], BF16)
            ymt = two("ymt", [128, 1024])
            sg = two("sg", [128, 2048])
            mix = two("mix", [128, 1024])
            mixb = two("mixb", [128, 1024], BF16)
            mixT = two("mixT", [128, 8, 128], BF16)
            r1 = two("r1", [128, 1024])
            hout = two("hout", [128, 1024])
            stats = two("stats", [128, 2, 6])
            mv = two("mv", [128, 2])
            rstd = two("rstd", [128, 1])

            def c1_loads(t):
                u = t % 2
                tsl = slice(t * 128, (t + 1) * 128)
                DMA('sp', xf[u][:], xs[TH + t * 128:TH + (t + 1) * 128, :], [], [('xf', u)], 'c_xf%d' % u)
                DMA('pool', xbb[u][:], xs[TH + t * 128:TH + (t + 1) * 128, :], [], [('xbb', u)], 'c_xbb%d' % u)
                DMA('sp', odf[u][:], odn_d[tsl, :], [('odn_d', t)], [('odf', u)], 'c_odt%d' % u)
                DMA('sp', ymt[u][:], ymla_d[tsl, :], [('ymla_d', t)], [('ymt', u)], 'c_ymt%d' % u)

            def c1_A(t):
                u = t % 2
                for kc in range(8):
                    TR(pb[0][:, kc * 128:(kc + 1) * 128], xbb[u][:, kc * 128:(kc + 1) * 128], ident_b[:], [('xbb', u), 'ident_b'], [('pb', 0)])
                CP('act', xTc[u][:].rearrange("p a b -> p (a b)"), pb[0][:, 0:1024], [('pb', 0)], [('xTc', u)])
                for h in range(8):
                    ACT(junkc[:], odf[u][:, h * 128:(h + 1) * 128], AF.Square, [('odf', u)], ['junkc', ('orms', u)], accum=orms[u][:, h:h + 1])
                ACT(orms[u][:], orms[u][:], AF.Ln, [('orms', u), 'epscol'], [('orms', u)], bias=epscol[:, 0:1], scale=1.0 / 128)
                ACT(orms[u][:], orms[u][:], AF.Exp, [('orms', u)], [('orms', u)], scale=-0.5)
                for h in range(8):
                    STT('dve', odf[u][:, h * 128:(h + 1) * 128], odf[u][:, h * 128:(h + 1) * 128], orms[u][:, h:h + 1], dnw[:], ALU.mult, ALU.mult, [('odf', u), ('orms', u), 'dnw'], [('odf', u)])
                for n in range(4):
                    for kc in range(8):
                        MM(pf[n][:], xTc[u][:, kc, :], wg[:, kc, n * 512:(n + 1) * 512], kc == 0, kc == 7, [('xTc', u), ('wg', n)], [('pf', n)])
                    sgn = sg[u][:, n * 512:(n + 1) * 512]
                    ACT(sgn, pf[n][:], AF.Exp, [('pf', n)], [('sg', u, n)], scale=-1.0)
                    ACT(sgn, sgn, AF.Ln, [('sg', u, n), 'onecol'], [('sg', u, n)], bias=onecol[:, 0:1])
                    ACT(sgn, sgn, AF.Exp, [('sg', u, n)], [('sg', u, n)], scale=-1.0)
                for n in range(2):
                    for kc in range(8):
                        MM(pf[4 + n][:], xTc[u][:, kc, :], wz[:, kc, n * 512:(n + 1) * 512], kc == 0, kc == 7, [('xTc', u), ('wz', n)], [('pf', 4 + n)])
                    zn = zs[u][:, n * 512:(n + 1) * 512]
                    ACT(zn, pf[4 + n][:], AF.Exp, [('pf', 4 + n)], [('zs', u, n)], scale=-1.0)
                    ACT(zn, zn, AF.Ln, [('zs', u, n), 'onecol'], [('zs', u, n)], bias=onecol[:, 0:1])
                    ACT(zn, zn, AF.Exp, [('zs', u, n)], [('zs', u, n)], scale=-1.0)
                    TT('dve', zn, zn, pf[4 + n][:], ALU.mult, [('zs', u, n), ('pf', 4 + n)], [('zs', u, n)])
                    TT('dve', odt[u][:, n * 512:(n + 1) * 512], odf[u][:, n * 512:(n + 1) * 512], zn, ALU.mult, [('odf', u), ('zs', u, n)], [('odt', u)])
                for kc in range(8):
                    TR(pb[1][:, kc * 128:(kc + 1) * 128], odt[u][:, kc * 128:(kc + 1) * 128], ident_b[:], [('odt', u), 'ident_b'], [('pb', 1)])
                CP('act', odT[u][:].rearrange("p a b -> p (a b)"), pb[1][:, 0:1024], [('pb', 1)], [('odT', u)])
                TT('dve', ymt[u][:], ymt[u][:], sg[u][:, 1024:2048], ALU.mult, [('ymt', u), ('sg', u, 2), ('sg', u, 3)], [('ymt', u)])
                for n in range(2):
                    for kc in range(8):
                        MM(pf[4 + n][:], odT[u][:, kc, :], wbd[:, kc, n * 512:(n + 1) * 512], kc == 0, kc == 7, [('odT', u), ('wbd', n)], [('pf', 4 + n)])
                    TT('dve', mix[u][:, n * 512:(n + 1) * 512], pf[4 + n][:], sg[u][:, n * 512:(n + 1) * 512], ALU.mult, [('pf', 4 + n), ('sg', u, n)], [('mix', u)])
                TT('dve', mixb[u][:], mix[u][:], ymt[u][:], ALU.add, [('mix', u), ('ymt', u)], [('mixb', u)])

            def c1_B(t):
                u = t % 2
                tsl = slice(t * 128, (t + 1) * 128)
                for kc in range(8):
                    TR(pb[0][:, kc * 128:(kc + 1) * 128], mixb[u][:, kc * 128:(kc + 1) * 128], ident_b[:], [('mixb', u), 'ident_b'], [('pb', 0)])
                CP('act', mixT[u][:].rearrange("p a b -> p (a b)"), pb[0][:, 0:1024], [('pb', 0)], [('mixT', u)])
                for n in range(2):
                    for kc in range(8):
                        MM(pf[n][:], mixT[u][:, kc, :], wo[:, kc, n * 512:(n + 1) * 512], kc == 0, kc == 7, [('mixT', u), ('wo', n)], [('pf', n)])
                    STT('dve', r1[u][:, n * 512:(n + 1) * 512], xf[u][:, n * 512:(n + 1) * 512], ALPHA, pf[n][:], ALU.mult, ALU.add, [('xf', u), ('pf', n)], [('r1', u)])
                layer_norm(r1[u][:], g1, b1, hout[u][:], stats[u], mv[u], rstd[u], ('r1', u), ('hout', u), 'g1', 'b1', 'c1%d' % u)
                DMA('sp', h1_d[tsl, :], hout[u][:], [('hout', u)], [('h1_d', t)], 'c_hout%d' % u)

            c1_loads(0)
            if NTH > 1:
                c1_loads(1)
            c1_A(0)
            for t in range(NTH):
                if t + 1 < NTH:
                    c1_A(t + 1)
                c1_B(t)
                if t + 2 < NTH:
                    c1_loads(t + 2)
            barrier([])

        with ExitStack() as ed:
            wf1 = load_w(ed, "wf1", w_ffn_in, 8, 5632, colblk=256, groups=[[j, 11 + j] for j in range(11)])
            wpg = load_w(ed, "wpg", w_ple_gate, 8, 1024)
            wpl = load_w(ed, "wpl", w_ple, 2, 1024)
            wf2 = load_w(ed, "wf2", w_ffn_out, 22, 1024)
            g2 = bcast_vec(ed, "g2", ln2_g, 1024)
            b2 = bcast_vec(ed, "b2", ln2_b, 1024)

            def two(name, shape, dt=F32):
                return [sbt(ed, "%s_%d" % (name, i), shape, dt) for i in range(2)]
            hf = two("hf", [128, 1024])
            hb2 = two("hb2", [128, 1024], BF16)
            hT = two("hT", [128, 8, 128], BF16)
            pt_ = [sbt(ed, "pt_", [128, 256], BF16)] * 2
            pT = [sbt(ed, "pT", [128, 2, 128], BF16)] * 2
            sgt = two("sgt", [128, 256])
            act = sbt(ed, "act", [128, 2816], BF16)
            actT = sbt(ed, "actT", [128, 22, 128], BF16)
            plg = sbt(ed, "plg", [128, 1024])
            yout = two("yout", [128, 1024])
            stats2 = sbt(ed, "stats2", [128, 2, 6])
            mv2 = sbt(ed, "mv2", [128, 2])
            rstd2 = sbt(ed, "rstd2", [128, 1])

            def c2_loads(t):
                u = t % 2
                tsl = slice(t * 128, (t + 1) * 128)
                DMA('sp', hf[u][:], h1_d[tsl, :], [('h1_d', t)], [('hf', u)], 'd_hf%d' % u)
                DMA('pool', hb2[u][:], h1_d[tsl, :], [('h1_d', t)], [('hb', u)], 'd_hb%d' % u)

            def c2_A(t):
                u = t % 2
                DMA('pool', pt_[u][:], pp[t * 128:(t + 1) * 128, :], [], ['pt_'], 'd_pt')
                for kc in range(8):
                    TR(pb[0][:, kc * 128:(kc + 1) * 128], hb2[u][:, kc * 128:(kc + 1) * 128], ident_b[:], [('hb', u), 'ident_b'], [('pb', 0)])
                CP('act', hT[u][:].rearrange("p a b -> p (a b)"), pb[0][:, 0:1024], [('pb', 0)], [('hT', u)])
                for kc in range(2):
                    TR(pb[1][:, kc * 128:(kc + 1) * 128], pt_[u][:, kc * 128:(kc + 1) * 128], ident_b[:], ['pt_', 'ident_b'], [('pb', 1)])
                CP('dve', pT[u][:].rearrange("p a b -> p (a b)"), pb[1][:, 0:256], [('pb', 1)], ['pT'])
                for j in range(11):
                    bk = j % 2
                    for kc in range(8):
                        MM(pf[bk][:, 0:256], hT[u][:, kc, :], wf1[:, kc, j * 256:(j + 1) * 256], kc == 0, kc == 7, [('hT', u), ('wf1', j)], [('pf', bk)])
                    for kc in range(8):
                        MM(pf[bk][:, 256:512], hT[u][:, kc, :], wf1[:, kc, 2816 + j * 256:2816 + (j + 1) * 256], kc == 0, kc == 7, [('hT', u), ('wf1', j)], [('pf', bk)])
                    s_ = sgt[bk]
                    ACT(s_[:], pf[bk][:, 0:256], AF.Exp, [('pf', bk)], [('sgt', bk)], scale=-1.0)
                    ACT(s_[:], s_[:], AF.Ln, [('sgt', bk), 'onecol'], [('sgt', bk)], bias=onecol[:, 0:1])
                    ACT(s_[:], s_[:], AF.Exp, [('sgt', bk)], [('sgt', bk)], scale=-1.0)
                    TT('dve', s_[:], s_[:], pf[bk][:, 0:256], ALU.mult, [('sgt', bk), ('pf', bk)], [('sgt', bk)])
                    TT('dve', act[:, j * 256:(j + 1) * 256], s_[:], pf[bk][:, 256:512], ALU.mult, [('sgt', bk), ('pf', bk)], [('act', j)])

            def c2_B1(t):
                u = t % 2
                for n in range(2):
                    for kc in range(8):
                        MM(pf[4][:], hT[u][:, kc, :], wpg[:, kc, n * 512:(n + 1) * 512], kc == 0, kc == 7, [('hT', u), ('wpg', n)], [('pf', 4)])
                    pn = plg[:, n * 512:(n + 1) * 512]
                    ACT(pn, pf[4][:], AF.Exp, [('pf', 4)], [('plg', n)], scale=-1.0)
                    ACT(pn, pn, AF.Ln, [('plg', n), 'onecol'], [('plg', n)], bias=onecol[:, 0:1])
                    ACT(pn, pn, AF.Exp, [('plg', n)], [('plg', n)], scale=-1.0)
                    for kc in range(2):
                        MM(pf[5][:], pT[u][:, kc, :], wpl[:, kc, n * 512:(n + 1) * 512], kc == 0, kc == 1, ['pT', ('wpl', n)], [('pf', 5)])
                    TT('dve', pn, pn, pf[5][:], ALU.mult, [('plg', n), ('pf', 5)], [('plg', n)])
                for j in range(22):
                    TR(pb[(j // 8) % 2][:, (j % 8) * 128:(j % 8 + 1) * 128], act[:, j * 128:(j + 1) * 128], ident_b[:], [('act', j // 2), 'ident_b'], [('pb', (j // 8) % 2)])
                    if j % 8 == 7 or j == 21:
                        j0 = (j // 8) * 8
                        n_ = j - j0 + 1
                        CP('act' if (j // 8) % 2 else 'dve', actT[:, j0:j0 + n_, :].rearrange("p a b -> p (a b)"), pb[(j // 8) % 2][:, 0:n_ * 128], [('pb', (j // 8) % 2)], [('actT', j // 8)])
                for n in range(2):
                    for j in range(22):
                        MM(pf[2 + n][:], actT[:, j, :], wf2[:, j, n * 512:(n + 1) * 512], j == 0, j == 21, [('actT', j // 8), ('wf2', n)], [('pf', 2 + n)])

            def c2_B2(t):
                u = t % 2
                tsl = slice(t * 128, (t + 1) * 128)
                yk = ('yout', u)
                for n in range(2):
                    STT('dve', yout[u][:, n * 512:(n + 1) * 512], hf[u][:, n * 512:(n + 1) * 512], ALPHA, pf[2 + n][:], ALU.mult, ALU.add, [('hf', u), ('pf', 2 + n)], [yk])
                TT('dve', yout[u][:], yout[u][:], plg[:], ALU.add, [yk, ('plg', 0), ('plg', 1)], [yk])
                layer_norm(yout[u][:], g2, b2, yout[u][:], stats2, mv2, rstd2, yk, yk, 'g2', 'b2', 'c2')
                DMA('sp', out[tsl, :], yout[u][:], [yk], [('out', t)], 'd_out%d' % u)

            c2_loads(0)
            if NTH > 1:
                c2_loads(1)
            c2_A(0)
            for t in range(NTH):
                c2_B1(t)
                if t + 1 < NTH:
                    c2_A(t + 1)
                c2_B2(t)
                if t + 2 < NTH:
                    c2_loads(t + 2)
        P.emit(es)
    return nc


_CACHE = {}


def _host_inputs(inputs, TH, b, half):
    x = np.asarray(inputs['x'], np.float32)
    S = 2 * TH
    own = x[b, half * TH:(half + 1) * TH]
    pre = x[b, 0:TH] if half == 1 else np.zeros((TH, 1024), np.float32)
    posa = np.asarray(inputs['positions'], np.int32)[b]
    pos_own = posa[half * TH:(half + 1) * TH]
    pos_pre = posa[0:TH] if half == 1 else np.zeros((TH,), np.int32)
    kb = np.zeros((128, 2 * (TH // 128)), np.float32)
    if half == 0:
        kb[:, :TH // 128] = NEG
    invf = (10000.0 ** (-np.arange(0, 64, 2, dtype=np.float32) / 64.0)).astype(np.float32) / np.float32(2 * np.pi)
    invf2 = np.broadcast_to(np.concatenate([invf, invf])[None, :], (128, 64)).astype(np.float32)
    m = {
        'xs': np.ascontiguousarray(np.concatenate([pre, own], 0)),
        'pp': np.ascontiguousarray(np.asarray(inputs['p'], np.float32)[0, b, half * TH:(half + 1) * TH]),
        'pos': np.ascontiguousarray(np.concatenate([pos_pre, pos_own]).reshape(-1, 128).T),
        'kbias': kb, 'invf': np.ascontiguousarray(invf2),
    }
    sq = lambda k: np.ascontiguousarray(np.asarray(inputs[k], np.float32)[0])
    m['w_in'] = sq('w_in')
    m['conv_w'] = sq('conv_w')
    m['dn_a_log'] = sq('dn_a_log')
    m['dn_dt_bias'] = sq('dn_dt_bias')
    m['dn_norm_w'] = sq('dn_norm_w')
    m['q_norm_w'] = sq('q_norm_w').reshape(3, 128)
    m['w_uq'] = sq('w_uq').reshape(384, 1536)
    m['kv_norm_w'] = sq('kv_norm_w')
    m['w_uk'] = sq('w_uk').reshape(256, 1024)
    m['w_uv'] = sq('w_uv').reshape(256, 1024)
    for k in ('w_br_dn', 'w_br_mla', 'w_o', 'ln1_g', 'ln1_b', 'w_ffn_in', 'w_ffn_out', 'w_ple', 'w_ple_gate', 'ln2_g', 'ln2_b'):
        m[k] = sq(k)
    return m


def kernel(**inputs):
    x = np.asarray(inputs['x'])
    B, S, _ = x.shape
    TH = S // 2
    if TH not in _CACHE:
        _CACHE[TH] = build(TH)
    nc = _CACHE[TH]
    in_maps = []
    cores = []
    for b in range(B):
        for half in range(2):
            in_maps.append(_host_inputs(inputs, TH, b, half))
            cores.append((b, half))
    res = run_bass_kernel_spmd(nc, in_maps, core_ids=list(range(len(in_maps))))
    out = np.zeros((B, S, 1024), np.float32)
    for i, (b, half) in enumerate(cores):
        out[b, half * TH:(half + 1) * TH] = res.results[i]['out']
    return out
```
